# Optimizing a Trainium2 kernel written in Bass

```python
import jax, jax.numpy as jnp
from jax import lax
import numpy as np

D_MODEL = 1024
BATCH = 2
SEQ = 8192
DEPTH = 2

N_META = 16
D_MIX = D_MODEL
D_POOL = D_MIX // 4
POOL_WINDOWS = (2, 4, 8, 16)
POOL_GROUPS = len(POOL_WINDOWS)
POOL_GW = D_POOL // POOL_GROUPS
D_CONV = D_MIX // 4
CONV_K = 31
D_RNN = D_MIX - D_POOL - D_CONV
RG_HEADS = 8
RG_HD = D_RNN // RG_HEADS
RG_CONV_K = 4
RG_C = 8.0
D_IN = D_POOL + 2 * D_CONV + 2 * D_RNN
D_FF = 4 * D_MODEL
EPS = 1e-6

kernel_name = "hymba_pool_conformer_rglru_hybrid"


def _rmsnorm(x, g):
    xf = x.astype(jnp.float32)
    y = xf * lax.rsqrt(jnp.mean(xf * xf, axis=-1, keepdims=True) + EPS)
    return y.astype(x.dtype) * g


def _layernorm(x, g, b):
    xf = x.astype(jnp.float32)
    mu = jnp.mean(xf, axis=-1, keepdims=True)
    var = jnp.mean(jnp.square(xf - mu), axis=-1, keepdims=True)
    return ((xf - mu) * lax.rsqrt(var + EPS)).astype(x.dtype) * g + b


def _causal_depthwise_conv(x, w, b):
    K, C = w.shape
    y = lax.conv_general_dilated(
        x, w[:, None, :], window_strides=(1,), padding=[(K - 1, 0)],
        dimension_numbers=('NWC', 'WIO', 'NWC'), feature_group_count=C)
    return y + b


def _pool_mixer(u, w_grp, scale):
    B_, T, _ = u.shape
    uf = u.astype(jnp.float32)
    cs = jnp.cumsum(uf, axis=1)
    pos = jnp.arange(1, T + 1, dtype=jnp.int32)
    outs = []
    for g, w in enumerate(POOL_WINDOWS):
        sl = slice(g * POOL_GW, (g + 1) * POOL_GW)
        c = cs[..., sl]
        prev = jnp.pad(c, ((0, 0), (w, 0), (0, 0)))[:, :T]
        cnt = jnp.minimum(pos, w).astype(jnp.float32)[None, :, None]
        outs.append((c - prev) / cnt - uf[..., sl])
    pooled = jnp.stack(outs, axis=2).astype(u.dtype)
    mixed = jnp.einsum('btgc,gcd->btgd', pooled, w_grp).reshape(B_, T, D_POOL)
    return mixed * scale


def _conformer_conv(v, gt, w_dw, b_dw, ln_g, ln_b, w_pw):
    u = v * jax.nn.sigmoid(gt)
    u = _causal_depthwise_conv(u, w_dw, b_dw)
    u = jax.nn.silu(_layernorm(u, ln_g, ln_b))
    return u @ w_pw


def _linear_scan(a, b):
    def comb(l, r):
        return (l[0] * r[0], r[0] * l[1] + r[1])
    _, h = lax.associative_scan(comb, (a, b), axis=1)
    return h


def _rglru_branch(gate_in, x_in, conv_w, conv_b, w_a, b_a, w_x, b_x, lam):
    B_, T, _ = x_in.shape
    f32 = jnp.float32
    xc = _causal_depthwise_conv(x_in, conv_w, conv_b)
    xh = xc.reshape(B_, T, RG_HEADS, RG_HD)
    r = jax.nn.sigmoid((jnp.einsum('bthc,hcd->bthd', xh, w_a).reshape(B_, T, D_RNN) + b_a).astype(f32))
    i = jax.nn.sigmoid((jnp.einsum('bthc,hcd->bthd', xh, w_x).reshape(B_, T, D_RNN) + b_x).astype(f32))
    log_a = -RG_C * r * jax.nn.softplus(-lam.astype(f32))
    a = jnp.exp(log_a)
    mult = jnp.sqrt(-jnp.expm1(2.0 * log_a))
    h = _linear_scan(a, mult * (i * xc.astype(f32)))
    return (jax.nn.gelu(gate_in.astype(f32)) * h).astype(x_in.dtype)


def setup_inputs(seed: int = 0) -> dict:
    key = jax.random.key(seed)
    ks = jax.random.split(key, 24)
    f32 = jnp.float32
    L = DEPTH

    def nrm(k, shape, scale):
        return jax.random.normal(k, shape, f32) * scale

    u = jax.random.uniform(ks[15], (L, D_RNN), f32, 0.9, 0.999)
    a0 = u ** (1.0 / RG_C)
    rg_lambda = jnp.log(a0) - jnp.log1p(-a0)
    return {
        "x": nrm(ks[0], (BATCH, SEQ, D_MODEL), 1.0),
        "meta_tokens": nrm(ks[1], (N_META, D_MODEL), 1.0),
        "mix_norm_g": 1.0 + nrm(ks[2], (L, D_MODEL), 0.05),
        "w_in": nrm(ks[3], (L, D_MODEL, D_IN), D_MODEL ** -0.5),
        "pool_w": nrm(ks[4], (L, POOL_GROUPS, POOL_GW, POOL_GW), POOL_GW ** -0.5),
        "pool_scale": 1.0 + nrm(ks[5], (L, D_POOL), 0.1),
        "convb_dw_w": nrm(ks[6], (L, CONV_K, D_CONV), CONV_K ** -0.5),
        "convb_dw_b": nrm(ks[7], (L, D_CONV), 0.02),
        "convb_ln_g": 1.0 + nrm(ks[8], (L, D_CONV), 0.05),
        "convb_ln_b": nrm(ks[9], (L, D_CONV), 0.02),
        "convb_pw_w": nrm(ks[10], (L, D_CONV, D_CONV), D_CONV ** -0.5),
        "rg_conv_w": nrm(ks[11], (L, RG_CONV_K, D_RNN), RG_CONV_K ** -0.5),
        "rg_conv_b": nrm(ks[12], (L, D_RNN), 0.02),
        "rg_w_a": nrm(ks[13], (L, RG_HEADS, RG_HD, RG_HD), RG_HD ** -0.5),
        "rg_b_a": nrm(ks[14], (L, D_RNN), 0.02),
        "rg_w_x": nrm(ks[16], (L, RG_HEADS, RG_HD, RG_HD), RG_HD ** -0.5),
        "rg_b_x": nrm(ks[17], (L, D_RNN), 0.02),
        "rg_lambda": rg_lambda,
        "w_out": nrm(ks[18], (L, D_MIX, D_MODEL), D_MIX ** -0.5),
        "mlp_norm_g": 1.0 + nrm(ks[19], (L, D_MODEL), 0.05),
        "w_up": nrm(ks[20], (L, D_MODEL, D_FF), D_MODEL ** -0.5),
        "w_down": nrm(ks[21], (L, D_FF, D_MODEL), D_FF ** -0.5),
        "final_norm_g": 1.0 + nrm(ks[22], (D_MODEL,), 0.05),
    }


def reference(x, meta_tokens, mix_norm_g, w_in, pool_w, pool_scale, convb_dw_w, convb_dw_b,
              convb_ln_g, convb_ln_b, convb_pw_w, rg_conv_w, rg_conv_b, rg_w_a, rg_b_a,
              rg_w_x, rg_b_x, rg_lambda, w_out, mlp_norm_g, w_up, w_down, final_norm_g):
    B_ = x.shape[0]
    meta = jnp.broadcast_to(meta_tokens[None].astype(x.dtype), (B_, N_META, D_MODEL))
    h = jnp.concatenate([meta, x], axis=1)
    splits = [D_POOL, D_POOL + D_CONV, D_POOL + 2 * D_CONV, D_POOL + 2 * D_CONV + D_RNN]
    for l in range(DEPTH):
        u = _rmsnorm(h, mix_norm_g[l])
        p = u @ w_in[l]
        p_pool, p_bval, p_bgate, p_cgate, p_cx = jnp.split(p, splits, axis=-1)
        y_a = _pool_mixer(p_pool, pool_w[l], pool_scale[l])
        y_b = _conformer_conv(p_bval, p_bgate, convb_dw_w[l], convb_dw_b[l],
                              convb_ln_g[l], convb_ln_b[l], convb_pw_w[l])
        y_c = _rglru_branch(p_cgate, p_cx, rg_conv_w[l], rg_conv_b[l], rg_w_a[l], rg_b_a[l],
                            rg_w_x[l], rg_b_x[l], rg_lambda[l])
        y = jnp.concatenate([y_a, y_b, y_c], axis=-1)
        h = h + y @ w_out[l]
        u = _rmsnorm(h, mlp_norm_g[l])
        h = h + jnp.square(jax.nn.relu(u @ w_up[l])) @ w_down[l]
    h = _rmsnorm(h, final_norm_g)
    return h[:, N_META:]
```

```python
import contextlib
import numpy as np
import concourse.bass as bass
import concourse.mybir as mybir
from concourse.bass_utils import run_bass_kernel_spmd

F32 = mybir.dt.float32
BF16 = mybir.dt.bfloat16
AF = mybir.ActivationFunctionType
ALU = mybir.AluOpType

NCORES = 8
D = 1024
KT = 8
PRE = 32
MAIN = 2048
NT = PRE + MAIN
NB = 5
BW = NT // NB
PADL = 32
EPS = 1e-6
NRING = 6
GELU_K = 1.5957691216057308

_CST = {}
_off = 0
for _n, _w in [("g_mix", 8), ("g_mlp", 8), ("g_fin", 8), ("pool_scale", 2), ("invw", 2), ("invwm1", 2),
               ("himask", 1), ("dw_w", 62), ("dw_b", 2), ("ln_g", 2), ("ln_b", 2), ("rgc_w", 16),
               ("rgc_b", 4), ("b_a", 4), ("b_x", 4), ("lam", 4), ("scanmask", 32), ("ratio", 64),
               ("pm", 8), ("G", 64), ("Wg", 256), ("Cmat", 512), ("Jm", 128)]:
    _CST[_n] = _off
    _off += _w
NCF = _off
_CBF = {"ident": 0, "pw": 128, "Wa": 640, "Wx": 1152}
NCB = 1664

ENGS = ("pe", "act", "dve", "pool", "sp")


class Op:
    __slots__ = ("idx", "eng", "fn", "deps", "dma_sem", "count", "signal", "waits", "inc")

    def __init__(self, idx, eng, fn, dma_sem, inc):
        self.idx = idx
        self.eng = eng
        self.fn = fn
        self.deps = {}
        self.dma_sem = dma_sem
        self.count = None
        self.signal = False
        self.waits = []
        self.inc = inc


class Sched:
    def __init__(self):
        self.ops = []
        self.last_writer = {}
        self.readers = {}

    def op(self, eng, fn, reads=(), writes=(), dma_sem=None, inc=16):
        o = Op(len(self.ops), eng, fn, dma_sem, inc)
        for k in reads:
            w = self.last_writer.get(k)
            if w is not None:
                o.deps[w] = "raw"
            if isinstance(k, tuple) and k[0] == "ps":
                for r in self.readers.get(k, ()):
                    if self.ops[r].eng != eng and r not in o.deps:
                        o.deps[r] = "rar"
        for k in writes:
            w = self.last_writer.get(k)
            if w is not None and w not in o.deps:
                o.deps[w] = "waw"
            for r in self.readers.get(k, ()):
                if r not in o.deps:
                    o.deps[r] = "war"
        for k in reads:
            self.readers.setdefault(k, []).append(o.idx)
        for k in writes:
            self.last_writer[k] = o.idx
            self.readers[k] = []
        self.ops.append(o)
        return o

    def emit(self, nc, st):
        ops = self.ops
        for o in ops:
            for d, kind in o.deps.items():
                p = ops[d]
                if p.dma_sem is not None or o.dma_sem is not None or p.eng != o.eng:
                    need = True
                else:
                    need = (o.eng != "pe")
                if need:
                    o.waits.append(d)
                    p.signal = True
        cnt = {e: 0 for e in ENGS}
        dcnt = {}
        for o in ops:
            if o.dma_sem is not None:
                dcnt[o.dma_sem] = dcnt.get(o.dma_sem, 0) + o.inc
                o.count = dcnt[o.dma_sem]
            elif o.signal:
                cnt[o.eng] += 1
                o.count = cnt[o.eng]
        sems = {e: st.enter_context(nc.semaphore("s_" + e)) for e in ENGS}
        dsems = {k: st.enter_context(nc.semaphore("d_" + str(k))) for k in sorted(dcnt)}
        block = st.enter_context(nc.Block())
        queues = {e: [o for o in ops if o.eng == e] for e in ENGS}

        def run_queue(eng_name, engine):
            waited = {}
            for o in queues[eng_name]:
                need = {}
                for d in o.waits:
                    p = ops[d]
                    key = ("d", p.dma_sem) if p.dma_sem is not None else ("e", p.eng)
                    if p.count > need.get(key, 0):
                        need[key] = p.count
                for key, val in need.items():
                    if waited.get(key, 0) >= val:
                        continue
                    waited[key] = val
                    engine.wait_ge(dsems[key[1]] if key[0] == "d" else sems[key[1]], val)
                ins = o.fn(engine)
                if o.dma_sem is not None:
                    ins.then_inc(dsems[o.dma_sem], o.inc)
                elif o.signal:
                    ins.then_inc(sems[o.eng], 1)

        @block.tensor
        def _(e):
            run_queue("pe", e)

        @block.scalar
        def _(e):
            run_queue("act", e)

        @block.vector
        def _(e):
            run_queue("dve", e)

        @block.gpsimd
        def _(e):
            run_queue("pool", e)

        @block.sync
        def _(e):
            run_queue("sp", e)


class Pool:
    def __init__(self, nc, name, shape, dtype, n):
        self.t = [nc.alloc_sbuf_tensor(f"{name}{i}", shape, dtype) for i in range(n)]
        self.name = name
        self.i = 0

    def get(self):
        i = self.i % len(self.t)
        self.i += 1
        return self.t[i], (self.name, i)


def build_program(mode, nblocks):
    nc = bass.Bass("TRN2", target_bir_lowering=False)
    hT = nc.dram_tensor("hT", [KT, 128, NT], F32, kind="ExternalInput").ap()
    wstream = nc.dram_tensor("wstream", [nblocks, 128, 1024], F32, kind="ExternalInput").ap()
    cst_d = nc.dram_tensor("cst", [128, NCF], F32, kind="ExternalInput").ap()
    cbf_d = nc.dram_tensor("cbf", [128, NCB], F32, kind="ExternalInput").ap()
    if mode == "pre":
        out_d = nc.dram_tensor("endp", [128, 8], F32, kind="ExternalOutput").ap()
        gate_out = nc.dram_tensor("gate_out", [4, 128, NT], F32, kind="ExternalOutput").ap()
        ya_out = nc.dram_tensor("ya_out", [4, 128, NT], BF16, kind="ExternalOutput").ap()
        arg_out = nc.dram_tensor("arg_out", [4, 128, NT], F32, kind="ExternalOutput").ap()
        brg_out = nc.dram_tensor("brg_out", [4, 128, NT], F32, kind="ExternalOutput").ap()
    else:
        gate_in = nc.dram_tensor("gate_in", [4, 128, NT], F32, kind="ExternalInput").ap()
        ya_in = nc.dram_tensor("ya_in", [4, 128, NT], BF16, kind="ExternalInput").ap()
        arg_in = nc.dram_tensor("arg_in", [4, 128, NT], F32, kind="ExternalInput").ap()
        brg_in = nc.dram_tensor("brg_in", [4, 128, NT], F32, kind="ExternalInput").ap()
    if mode == "pre":
        pass
    elif mode == "layer":
        out_d = nc.dram_tensor("hout", [KT, 128, NT], F32, kind="ExternalOutput").ap()
    else:
        out_d = nc.dram_tensor("yout", [KT, 128, MAIN], F32, kind="ExternalOutput").ap()

    S = Sched()
    h = nc.alloc_sbuf_tensor("h", [128, KT, NT], F32)
    A = nc.alloc_sbuf_tensor("A", [128, KT, NT], BF16)
    Y = nc.alloc_sbuf_tensor("Y", [128, KT, NT], BF16) if mode != "pre" else None
    NSTG = 6 if mode == "pre" else 4
    STG = nc.alloc_sbuf_tensor("STG", [128, NSTG, PADL + NT], BF16)
    ring = [nc.alloc_sbuf_tensor(f"wr{i}", [128, 1024], BF16) for i in range(NRING)]
    cst = nc.alloc_sbuf_tensor("cst_s", [128, NCF], F32)
    cbf = nc.alloc_sbuf_tensor("cbf_s", [128, NCB], BF16)
    pmat = nc.alloc_sbuf_tensor("pmat", [128, 8, 128], BF16)
    rgd = nc.alloc_sbuf_tensor("rgd", [128, 16, 128], BF16)
    ones = nc.alloc_sbuf_tensor("ones", [128, 128], BF16)
    dring = [nc.alloc_sbuf_tensor(f"dg{i}", [128, 128], BF16) for i in range(8)]
    small = nc.alloc_sbuf_tensor("small", [128, 64], F32)
    tf = Pool(nc, "tf", [128, BW], F32, 22 if mode == "pre" else 12)
    ths = Pool(nc, "ths", [128, BW], F32, 5)
    tb = Pool(nc, "tb", [128, BW], BF16, 8 if mode == "pre" else 6)
    ps_t = [nc.alloc_psum_tensor(f"ps{i}", [128, 512], F32) for i in range(8)]
    ps_i = [0]

    def PS():
        i = ps_i[0] % 8
        ps_i[0] += 1
        return ps_t[i], ("ps", i)

    def C(name, j=0, w=1):
        o = _CST[name] + j
        return cst[:, o:o + w]

    def bcols(b):
        return slice(b * BW, (b + 1) * BW)

    S.op("sp", lambda e: e.dma_start(out=cst[:], in_=cst_d), writes=["cst"], dma_sem="ldc")
    S.op("pool", lambda e: e.dma_start(out=cbf[:], in_=cbf_d), writes=["cbf"], dma_sem="ldb")
    hT_p = hT.rearrange("k p t -> p k t")
    for b in range(NB):
        S.op("sp", lambda e, b=b: e.dma_start(out=h[:, :, bcols(b)], in_=hT_p[:, :, bcols(b)]),
             writes=[("h", k, b) for k in range(KT)], dma_sem=f"ldh{b}")

    wstate = {"next_dma": 0, "next_use": 0}

    def w_issue():
        i = wstate["next_dma"]
        if i >= nblocks:
            return
        wstate["next_dma"] += 1
        s = i % NRING
        S.op("pool", lambda e, i=i, s=s: e.dma_start(out=ring[s][:], in_=wstream[i]),
             writes=[("w", s)], dma_sem=f"w{s}")

    for _ in range(NRING):
        w_issue()

    def w_acquire():
        i = wstate["next_use"]
        wstate["next_use"] += 1
        s = i % NRING
        return ring[s], ("w", s)

    S.op("dve", lambda e: e.memset(ones[:], 1.0 / 1024.0), writes=["ones"])
    for s4 in range(NSTG):
        S.op("dve", lambda e, s4=s4: e.memset(STG[:, s4, 0:PADL], 0.0), writes=[("stgpad", s4)])
    S.op("act", lambda e: e.activation(out=small[:, 12:16], in_=C("lam", 0, 4), func=AF.Exp, scale=-1.0),
         reads=["cst"], writes=["sm_t"])
    S.op("act", lambda e: e.activation(out=small[:, 16:20], in_=small[:, 12:16], func=AF.Ln, bias=1.0),
         reads=["sm_t"], writes=["sm_t2"])
    S.op("dve", lambda e: e.tensor_scalar(out=small[:, 0:4], in0=small[:, 16:20], scalar1=-8.0, scalar2=None,
                                          op0=ALU.mult), reads=["sm_t2"], writes=["c1"])
    S.op("dve", lambda e: e.tensor_scalar(out=small[:, 4:8], in0=small[:, 16:20], scalar1=-16.0, scalar2=None,
                                          op0=ALU.mult), reads=["sm_t2"], writes=["c2"])
    S.op("dve", lambda e: e.memset(small[:, 8:12], 0.0), writes=["carry"])
    if mode != "pre":
        for r in range(8):
            gE = C("G", r * 8, 4)
            gP = C("G", r * 8 + 4, 4)
            S.op("dve", lambda e, gP=gP: e.tensor_tensor(out=small[:, 20:24], in0=gP, in1=small[:, 8:12], op=ALU.mult),
                 reads=["cst", "carry"], writes=["cc_t"])
            S.op("dve", lambda e, gE=gE: e.tensor_tensor(out=small[:, 24:28], in0=small[:, 20:24], in1=gE, op=ALU.add),
                 reads=["cc_t", "cst"], writes=["cc_u"])
            S.op("dve", lambda e: e.tensor_tensor(out=small[:, 28:32], in0=small[:, 24:28], in1=small[:, 8:12],
                                                  op=ALU.subtract), reads=["cc_u", "carry"], writes=["cc_v"])
            S.op("dve", lambda e, r=r: e.scalar_tensor_tensor(out=small[:, 32:36], in0=small[:, 28:32],
                                                              scalar=C("pm", r), in1=small[:, 8:12],
                                                              op0=ALU.mult, op1=ALU.add),
                 reads=["cc_v", "carry", "cst"], writes=["cc_w"])
            S.op("dve", lambda e: e.tensor_copy(out=small[:, 8:12], in_=small[:, 32:36]), reads=["cc_w"], writes=["carry"])
    for t in range(4):
        for k in range(4):
            S.op("pool", lambda e, t=t, k=k: e.tensor_scalar(out=rgd[:, t * 4 + k, :], in0=cbf[:, 0:128],
                                                             scalar1=C("rgc_w", t * 4 + k), scalar2=1.0,
                                                             op0=ALU.mult, op1=ALU.mult),
                 reads=["cst", "cbf"], writes=[("rgd", t)])
    if mode == "pre":
        for t in range(2):
            wg = cst[:, _CST["Wg"] + t * 128:_CST["Wg"] + (t + 1) * 128]
            S.op("dve", lambda e, t=t, wg=wg: e.tensor_copy(out=pmat[:, t, :], in_=wg), reads=["cst"], writes=[("pm_", t)])
            S.op("dve", lambda e, t=t, wg=wg: e.tensor_scalar(out=pmat[:, 2 + t, :], in0=wg, scalar1=C("invwm1", t),
                                                              scalar2=None, op0=ALU.mult), reads=["cst"], writes=[("pm_", t)])
            S.op("dve", lambda e, t=t, wg=wg: e.tensor_scalar(out=pmat[:, 4 + t, :], in0=wg, scalar1=C("invw", t),
                                                              scalar2=None, op0=ALU.mult), reads=["cst"], writes=[("pm_", t)])
            S.op("dve", lambda e, t=t, wg=wg: e.tensor_scalar(out=pmat[:, 6 + t, :], in0=wg, scalar1=C("invw", t),
                                                              scalar2=C("himask"), op0=ALU.mult, op1=ALU.mult),
                 reads=["cst"], writes=[("pm_", t)])

    def rms_to_A(gname):
        for b in range(NB):
            cs = bcols(b)
            ps, pk = PS()
            for k in range(KT):
                sq, sk = tb.get()
                S.op("act", lambda e, k=k, cs=cs, sq=sq: e.activation(out=sq[:], in_=h[:, k, cs], func=AF.Square),
                     reads=[("h", k, b)], writes=[sk])
                S.op("pe", lambda e, k=k, sq=sq, ps=ps: e.matmul(ps[:, 0:BW], lhsT=ones[:], rhs=sq[:],
                                                                 start=(k == 0), stop=(k == KT - 1)),
                     reads=["ones", sk], writes=[pk])
            l, lk = tf.get()
            S.op("act", lambda e, ps=ps, l=l: e.activation(out=l[:], in_=ps[:, 0:BW], func=AF.Ln, bias=EPS),
                 reads=[pk], writes=[lk])
            S.op("act", lambda e, ps=ps, l=l: e.activation(out=ps[:, 0:BW], in_=l[:], func=AF.Exp, scale=-0.5),
                 reads=[lk], writes=[pk])
            for k in range(KT):
                S.op("dve", lambda e, k=k, cs=cs, ps=ps: e.scalar_tensor_tensor(
                    out=A[:, k, cs], in0=h[:, k, cs], scalar=C(gname, k), in1=ps[:, 0:BW],
                    op0=ALU.mult, op1=ALU.mult),
                    reads=[("h", k, b), pk, "cst"], writes=[("A", k, b)])

    def proj(wt, wk, wsel, rhs_fn, nk, b):
        ps, pk = PS()
        for k in range(nk):
            rap, rkey = rhs_fn(k)
            S.op("pe", lambda e, k=k, rap=rap, ps=ps: e.matmul(ps[:, 0:BW], lhsT=wsel(wt, k), rhs=rap,
                                                               start=(k == 0), stop=(k == nk - 1)),
                 reads=[wk, rkey], writes=[pk])
        return ps, pk

    def w8(wt, k):
        return wt[:, k * 128:(k + 1) * 128]

    def a_rhs(b):
        return lambda k: (A[:, k, bcols(b)], ("A", k, b))

    def stg_cols(slot, b, shift):
        c0 = PADL + b * BW - shift
        return STG[:, slot, c0:c0 + BW]

    def stg_keys(slot, b):
        ks = [("stg", slot, b)]
        ks.append(("stg", slot, b - 1) if b > 0 else ("stgpad", slot))
        return ks

    def inproj_to_stg(slot):
        wt, wk = w_acquire()
        for b in range(NB):
            ps, pk = proj(wt, wk, w8, a_rhs(b), KT, b)
            S.op("act", lambda e, ps=ps, b=b: e.activation(out=STG[:, slot, PADL + b * BW:PADL + (b + 1) * BW],
                                                           in_=ps[:, 0:BW], func=AF.Copy),
                 reads=[pk], writes=[("stg", slot, b)])
        w_issue()

    extra_outs = []
    if mode == "pre":
        rms_to_A("g_mix")
    else:
        for t in range(4):
            S.op("sp", lambda e, t=t: e.dma_start(out=Y[:, t, :], in_=ya_in[t]),
                 writes=[("Y", t, b) for b in range(NB)], dma_sem=f"ldy{t}")

    if mode == "pre":
        inproj_to_stg(0)
        inproj_to_stg(1)
        def pool_tile(t):
            wlo, whi = (2, 4) if t == 0 else (8, 16)
            for b in range(NB):
                ps, pk = PS()
                for k in range(whi):
                    mat = pmat[:, 2 + t, :] if k == 0 else (pmat[:, 4 + t, :] if k < wlo else pmat[:, 6 + t, :])
                    S.op("pe", lambda e, k=k, mat=mat, ps=ps, b=b: e.matmul(ps[:, 0:BW], lhsT=mat, rhs=stg_cols(t, b, k),
                                                                           start=(k == 0), stop=(k == whi - 1)),
                         reads=[("pm_", t)] + stg_keys(t, b), writes=[pk])
                yt, kyt = tb.get()
                S.op("act", lambda e, ps=ps, yt=yt: e.activation(out=yt[:], in_=ps[:, 0:BW],
                                                                 func=AF.Identity, scale=C("pool_scale", t)),
                     reads=[pk, "cst"], writes=[kyt])
                if b == 0:
                    y0 = (yt, kyt)
                else:
                    S.op("sp", lambda e, yt=yt, b=b: e.dma_start(out=ya_out[t][:, bcols(b)], in_=yt[:]), reads=[kyt],
                         writes=[("oy", t, b)], dma_sem=f"sy{kyt[1]}")
                    extra_outs.append(("oy", t, b))
            psS, pkS = PS()
            for k in range(whi):
                mat = pmat[:, 4 + t, :] if k < wlo else pmat[:, 6 + t, :]
                S.op("pe", lambda e, k=k, mat=mat, psS=psS: e.matmul(psS[:, 0:PRE], lhsT=mat,
                                                                    rhs=STG[:, t, PADL - k:PADL - k + PRE],
                                                                    start=(k == 0), stop=(k == whi - 1)),
                     reads=[("pm_", t)] + stg_keys(t, 0), writes=[pkS])
            psX, pkX = PS()
            S.op("pe", lambda e, psX=psX: e.matmul(psX[:, 0:PRE], lhsT=pmat[:, t, :], rhs=STG[:, t, PADL:PADL + PRE],
                                                   start=True, stop=True),
                 reads=[("pm_", t)] + stg_keys(t, 0), writes=[pkX])
            t1, k1 = tf.get()
            S.op("dve", lambda e, psS=psS, t1=t1: e.tensor_tensor(out=t1[:, 0:PRE], in0=psS[:, 0:PRE],
                                                                  in1=C("ratio", t * 32, 32), op=ALU.mult),
                 reads=[pkS, "cst"], writes=[k1])
            t2, k2 = tf.get()
            S.op("dve", lambda e, psX=psX, t1=t1, t2=t2: e.tensor_tensor(out=t2[:, 0:PRE], in0=t1[:, 0:PRE],
                                                                         in1=psX[:, 0:PRE], op=ALU.subtract),
                 reads=[pkX, k1], writes=[k2])
            yt, kyt = y0
            S.op("act", lambda e, t2=t2, yt=yt: e.activation(out=yt[:, 0:PRE], in_=t2[:, 0:PRE], func=AF.Identity,
                                                             scale=C("pool_scale", t)),
                 reads=[k2, "cst"], writes=[kyt])
            S.op("sp", lambda e, yt=yt: e.dma_start(out=ya_out[t][:, bcols(0)], in_=yt[:]), reads=[kyt],
                 writes=[("oy", t, 0)], dma_sem=f"sy{kyt[1]}")
            extra_outs.append(("oy", t, 0))

        pool_tile(0)
        pool_tile(1)

    if mode == "pre":
        S.op("dve", lambda e: e.memset(small[:, 40:44], 0.0), writes=[("rsum", t) for t in range(4)])
    rg_slots = [2, 3, 2, 3] if mode != "pre" else [0, 1, 2, 3]
    if mode == "pre":
        for t in range(4):
            inproj_to_stg(rg_slots[t])
    else:
        pass

    rg_prev = {}

    def rg_group(tiles, b, cgws):
        U = {t: {} for t in tiles}
        if mode != "pre":
            for t in tiles:
                a, ka = tf.get()
                bb, kb = tf.get()
                S.op("sp", lambda e, a=a, t=t: e.dma_start(out=a[:], in_=arg_in[t][:, bcols(b)]), writes=[ka],
                     dma_sem=f"so{ka[1]}")
                S.op("sp", lambda e, bb=bb, t=t: e.dma_start(out=bb[:], in_=brg_in[t][:, bcols(b)]), writes=[kb],
                     dma_sem=f"so{kb[1]}")
                U[t].update(a=a, ka=ka, xc=bb, kxc=kb)
        else:
            for t in tiles:
                slot = rg_slots[t]
                ps_c, pk_c = PS()
                for k in range(4):
                    S.op("pe", lambda e, k=k, ps_c=ps_c, t=t, slot=slot: e.matmul(
                        ps_c[:, 0:BW], lhsT=rgd[:, t * 4 + k, :], rhs=stg_cols(slot, b, 3 - k),
                        start=(k == 0), stop=(k == 3)),
                        reads=[("rgd", t)] + stg_keys(slot, b), writes=[pk_c])
                U[t].update(ps_c=ps_c, pk_c=pk_c)
            for t in tiles:
                u = U[t]
                xc, kxc = tf.get()
                xcb, kxcb = tb.get()
                S.op("dve", lambda e, ps_c=u["ps_c"], xc=xc, t=t: e.tensor_scalar(out=xc[:], in0=ps_c[:, 0:BW], scalar1=C("rgc_b", t),
                                                                                 scalar2=None, op0=ALU.add),
                     reads=[u["pk_c"], "cst"], writes=[kxc])
                S.op("dve", lambda e, xc=xc, xcb=xcb: e.tensor_copy(out=xcb[:], in_=xc[:]), reads=[kxc], writes=[kxcb])
                u.update(xc=xc, kxc=kxc, xcb=xcb, kxcb=kxcb)
            for t in tiles:
                u = U[t]
                ps_a, pk_a = PS()
                S.op("pe", lambda e, ps_a=ps_a, xcb=u["xcb"], t=t: e.matmul(
                    ps_a[:, 0:BW], lhsT=cbf[:, _CBF["Wa"] + t * 128:_CBF["Wa"] + (t + 1) * 128], rhs=xcb[:], start=True, stop=True),
                    reads=["cbf", u["kxcb"]], writes=[pk_a])
                ps_x, pk_x = PS()
                S.op("pe", lambda e, ps_x=ps_x, xcb=u["xcb"], t=t: e.matmul(
                    ps_x[:, 0:BW], lhsT=cbf[:, _CBF["Wx"] + t * 128:_CBF["Wx"] + (t + 1) * 128], rhs=xcb[:], start=True, stop=True),
                    reads=["cbf", u["kxcb"]], writes=[pk_x])
                u.update(ps_a=ps_a, pk_a=pk_a, ps_x=ps_x, pk_x=pk_x)
            for t in tiles:
                u = U[t]
                r, kr = tf.get()
                if mode == "pre":
                    lo = PRE if b == 0 else 0
                    if b == 0:
                        S.op("act", lambda e, ps_a=u["ps_a"], r=r, t=t: e.activation(out=r[:, 0:PRE], in_=ps_a[:, 0:PRE],
                                                                                    func=AF.Sigmoid, bias=C("b_a", t)),
                             reads=[u["pk_a"], "cst"], writes=[kr])
                    S.op("act", lambda e, ps_a=u["ps_a"], r=r, lo=lo, t=t: e.activation(
                        out=r[:, lo:BW], in_=ps_a[:, lo:BW], func=AF.Sigmoid, bias=C("b_a", t),
                        accum_out=small[:, 44 + t:45 + t]),
                        reads=[u["pk_a"], "cst"], writes=[kr, ("racc", t)])
                    S.op("dve", lambda e, t=t: e.tensor_tensor(out=small[:, 40 + t:41 + t], in0=small[:, 40 + t:41 + t],
                                                               in1=small[:, 44 + t:45 + t], op=ALU.add),
                         reads=[("racc", t), ("rsum", t)], writes=[("rsum", t)])
                else:
                    S.op("act", lambda e, ps_a=u["ps_a"], r=r, t=t: e.activation(out=r[:], in_=ps_a[:, 0:BW], func=AF.Sigmoid,
                                                                                bias=C("b_a", t)),
                         reads=[u["pk_a"], "cst"], writes=[kr])
                u.update(r=r, kr=kr)
            for t in tiles:
                u = U[t]
                a, ka = tf.get()
                S.op("act", lambda e, r=u["r"], a=a, t=t: e.activation(out=a[:], in_=r[:], func=AF.Exp, scale=small[:, t:t + 1]),
                     reads=[u["kr"], "c1"], writes=[ka])
                u.update(a=a, ka=ka)
            for t in tiles:
                u = U[t]
                S.op("dve", lambda e, r=u["r"], a=u["a"]: e.tensor_tensor(out=r[:], in0=a[:], in1=a[:], op=ALU.mult),
                     reads=[u["ka"], u["kr"]], writes=[u["kr"]])
            for t in tiles:
                u = U[t]
                S.op("act", lambda e, ps_a=u["ps_a"], ps_x=u["ps_x"], t=t: e.activation(
                    out=ps_a[:, 0:BW], in_=ps_x[:, 0:BW], func=AF.Sigmoid, bias=C("b_x", t)),
                    reads=[u["pk_x"], "cst"], writes=[u["pk_a"]])
            for t in tiles:
                u = U[t]
                S.op("act", lambda e, r=u["r"], ps_x=u["ps_x"]: e.activation(out=ps_x[:, 0:BW], in_=r[:], func=AF.Sqrt,
                                                                             scale=-1.0, bias=1.0),
                     reads=[u["kr"]], writes=[u["pk_x"]])
            for t in tiles:
                u = U[t]
                S.op("dve", lambda e, ps_a=u["ps_a"], xc=u["xc"]: e.tensor_tensor(out=xc[:], in0=ps_a[:, 0:BW], in1=xc[:], op=ALU.mult),
                     reads=[u["pk_a"], u["kxc"]], writes=[u["kxc"]])
            for t in tiles:
                u = U[t]
                S.op("dve", lambda e, ps_x=u["ps_x"], xc=u["xc"]: e.tensor_tensor(out=xc[:], in0=ps_x[:, 0:BW], in1=xc[:], op=ALU.mult),
                     reads=[u["pk_x"], u["kxc"]], writes=[u["kxc"]])
        if mode == "pre":
            if b == 0:
                for t in tiles:
                    u = U[t]
                    S.op("dve", lambda e, bb=u["xc"]: e.tensor_tensor(out=bb[:, 0:PRE], in0=bb[:, 0:PRE], in1=C("scanmask", 0, 32),
                                                                      op=ALU.mult), reads=[u["kxc"], "cst"], writes=[u["kxc"]])
            for t in tiles:
                u = U[t]
                S.op("sp", lambda e, a=u["a"], t=t: e.dma_start(out=arg_out[t][:, bcols(b)], in_=a[:]), reads=[u["ka"]],
                     writes=[("oar", t, b)], dma_sem=f"so{u['ka'][1]}")
                S.op("sp", lambda e, bb=u["xc"], t=t: e.dma_start(out=brg_out[t][:, bcols(b)], in_=bb[:]), reads=[u["kxc"]],
                     writes=[("obr", t, b)], dma_sem=f"so{u['kxc'][1]}")
                extra_outs.append(("oar", t, b))
                extra_outs.append(("obr", t, b))
        if b == 0:
            if mode != "pre":
                for t in tiles:
                    u = U[t]
                    S.op("dve", lambda e, bb=u["xc"], a=u["a"], t=t: e.scalar_tensor_tensor(
                        out=bb[:, PRE:PRE + 1], in0=a[:, PRE:PRE + 1], scalar=small[:, 8 + t:9 + t],
                        in1=bb[:, PRE:PRE + 1], op0=ALU.mult, op1=ALU.add),
                        reads=[u["kxc"], u["ka"], "carry"], writes=[u["kxc"]])
        for t in tiles:
            u = U[t]
            hs, khs = ths.get()
            if t not in rg_prev:
                S.op("dve", lambda e, hs=hs, a=u["a"], bb=u["xc"]: e.tensor_tensor_scan(
                    out=hs[:], data0=a[:], data1=bb[:], initial=0.0, op0=ALU.mult, op1=ALU.add),
                    reads=[u["ka"], u["kxc"]], writes=[khs])
            else:
                ph, pkh = rg_prev[t]
                S.op("dve", lambda e, hs=hs, a=u["a"], bb=u["xc"], ph=ph: e.tensor_tensor_scan(
                    out=hs[:], data0=a[:], data1=bb[:], initial=ph[:, BW - 1:BW], op0=ALU.mult, op1=ALU.add),
                    reads=[u["ka"], u["kxc"], pkh], writes=[khs])
            rg_prev[t] = (hs, khs)
            u.update(hs=hs, khs=khs)
        if mode == "pre":
            if b == NB - 1:
                for t in tiles:
                    S.op("dve", lambda e, hs=U[t]["hs"], t=t: e.tensor_copy(out=small[:, 56 + t:57 + t], in_=hs[:, BW - 1:BW]),
                         reads=[U[t]["khs"]], writes=[("endst", t)])
            return
        for t in tiles:
            u = U[t]
            g, kg = tf.get()
            S.op("sp", lambda e, g=g, t=t: e.dma_start(out=g[:], in_=gate_in[t][:, bcols(b)]), writes=[kg], dma_sem=f"so{kg[1]}")
            S.op("dve", lambda e, g=g, hs=u["hs"], t=t: e.tensor_tensor(out=Y[:, 4 + t, bcols(b)], in0=g[:], in1=hs[:], op=ALU.mult),
                 reads=[kg, u["khs"]], writes=[("Y", 4 + t, b)])

    def gate_block(b, cgws):
        U = {t: {} for t in range(4)}
        for t in range(4):
            wt, wk = cgws[t]
            ps_g, pk_g = proj(wt, wk, w8, a_rhs(b), KT, b)
            U[t].update(ps_g=ps_g, pk_g=pk_g)
        for t in range(4):
            u = U[t]
            x2, kx2 = tf.get()
            S.op("act", lambda e, ps_g=u["ps_g"], x2=x2: e.activation(out=x2[:], in_=ps_g[:, 0:BW], func=AF.Square),
                 reads=[u["pk_g"]], writes=[kx2])
            u.update(x2=x2, kx2=kx2)
        for t in range(4):
            u = U[t]
            S.op("dve", lambda e, x2=u["x2"]: e.tensor_scalar(out=x2[:], in0=x2[:], scalar1=0.044715 * GELU_K, scalar2=GELU_K,
                                                              op0=ALU.mult, op1=ALU.add), reads=[u["kx2"]], writes=[u["kx2"]])
        for t in range(4):
            u = U[t]
            S.op("dve", lambda e, ps_g=u["ps_g"], x2=u["x2"]: e.tensor_tensor(out=x2[:], in0=ps_g[:, 0:BW], in1=x2[:], op=ALU.mult),
                 reads=[u["pk_g"], u["kx2"]], writes=[u["kx2"]])
        for t in range(4):
            u = U[t]
            S.op("act", lambda e, q=u["x2"]: e.activation(out=q[:], in_=q[:], func=AF.Sigmoid),
                 reads=[u["kx2"]], writes=[u["kx2"]])
        for t in range(4):
            u = U[t]
            S.op("dve", lambda e, ps_g=u["ps_g"], q=u["x2"]: e.tensor_tensor(out=q[:], in0=ps_g[:, 0:BW], in1=q[:], op=ALU.mult),
                 reads=[u["pk_g"], u["kx2"]], writes=[u["kx2"]])
            S.op("sp", lambda e, q=u["x2"], t=t: e.dma_start(out=gate_out[t][:, bcols(b)], in_=q[:]), reads=[u["kx2"]],
                 writes=[("og", t, b)], dma_sem=f"so{u['kx2'][1]}")
            extra_outs.append(("og", t, b))

    if mode == "pre":
        for m in range(2):
            wv, kv = w_acquire()
            wg_, kg_ = w_acquire()
            for b in range(NB):
                psv, pkv = proj(wv, kv, w8, a_rhs(b), KT, b)
                psg, pkg = proj(wg_, kg_, w8, a_rhs(b), KT, b)
                sg, ksg = tf.get()
                S.op("act", lambda e, psg=psg, sg=sg: e.activation(out=sg[:], in_=psg[:, 0:BW], func=AF.Sigmoid),
                     reads=[pkg], writes=[ksg])
                S.op("dve", lambda e, psv=psv, sg=sg, b=b, m=m: e.tensor_tensor(
                    out=STG[:, 4 + m, PADL + b * BW:PADL + (b + 1) * BW], in0=psv[:, 0:BW], in1=sg[:], op=ALU.mult),
                    reads=[pkv, ksg], writes=[("stg", 4 + m, b)])
            w_issue()
            w_issue()
        cg_w = [w_acquire() for t in range(4)]

        dg_i = [0]

        def conformer_block(b):
            cen_in = []
            for m in range(2):
                ps, pk = PS()
                for k in range(31):
                    i = dg_i[0] % 8
                    dg_i[0] += 1
                    S.op("pool", lambda e, i=i, m=m, k=k: e.tensor_scalar(out=dring[i][:], in0=cbf[:, 0:128],
                                                                          scalar1=C("dw_w", m * 31 + k), scalar2=1.0,
                                                                          op0=ALU.mult, op1=ALU.mult),
                         reads=["cst", "cbf"], writes=[("dg", i)])
                    S.op("pe", lambda e, i=i, m=m, k=k, ps=ps: e.matmul(ps[:, 0:BW], lhsT=dring[i][:],
                                                                       rhs=stg_cols(4 + m, b, 30 - k),
                                                                       start=(k == 0), stop=(k == 30)),
                         reads=[("dg", i)] + stg_keys(4 + m, b), writes=[pk])
                c, kc = tf.get()
                S.op("act", lambda e, ps=ps, c=c, m=m: e.activation(out=c[:], in_=ps[:, 0:BW], func=AF.Identity,
                                                                    bias=C("dw_b", m)),
                     reads=[pk, "cst"], writes=[kc])
                cen_in.append((c, kc))
            cens = []
            for m in range(2):
                ps, pk = PS()
                for k in range(2):
                    c, kc = cen_in[k]
                    o = _CST["Cmat"] + k * 256 + m * 128
                    S.op("pe", lambda e, ps=ps, c=c, o=o, k=k: e.matmul(ps[:, 0:BW], lhsT=cst[:, o:o + 128], rhs=c[:],
                                                                       start=(k == 0), stop=(k == 1)),
                         reads=["cst", kc], writes=[pk])
                cens.append((ps, pk))
            psv, pkv = PS()
            for m in range(2):
                ps, pk = cens[m]
                sq, ksq = tf.get()
                S.op("act", lambda e, ps=ps, sq=sq: e.activation(out=sq[:], in_=ps[:, 0:BW], func=AF.Square),
                     reads=[pk], writes=[ksq])
                S.op("pe", lambda e, psv=psv, sq=sq, m=m: e.matmul(psv[:, 0:BW], lhsT=cst[:, _CST["Jm"]:_CST["Jm"] + 128],
                                                                  rhs=sq[:], start=(m == 0), stop=(m == 1)),
                     reads=["cst", ksq], writes=[pkv])
            l, kl = tf.get()
            S.op("act", lambda e, psv=psv, l=l: e.activation(out=l[:], in_=psv[:, 0:BW], func=AF.Ln, bias=EPS),
                 reads=[pkv], writes=[kl])
            rs, krs = tf.get()
            S.op("act", lambda e, l=l, rs=rs: e.activation(out=rs[:], in_=l[:], func=AF.Exp, scale=-0.5),
                 reads=[kl], writes=[krs])
            sls = []
            for m in range(2):
                ps, pk = cens[m]
                xn, kxn = tf.get()
                S.op("dve", lambda e, ps=ps, rs=rs, xn=xn: e.tensor_tensor(out=xn[:], in0=ps[:, 0:BW], in1=rs[:], op=ALU.mult),
                     reads=[pk, krs], writes=[kxn])
                sl, ksl = tb.get()
                S.op("act", lambda e, xn=xn, sl=sl, m=m: e.activation(out=sl[:], in_=xn[:], func=AF.Silu,
                                                                      scale=C("ln_g", m), bias=C("ln_b", m)),
                     reads=[kxn, "cst"], writes=[ksl])
                sls.append((sl, ksl))
            for m in range(2):
                ps, pk = PS()
                for k in range(2):
                    sl, ksl = sls[k]
                    o = _CBF["pw"] + k * 256 + m * 128
                    S.op("pe", lambda e, ps=ps, sl=sl, o=o, k=k: e.matmul(ps[:, 0:BW], lhsT=cbf[:, o:o + 128], rhs=sl[:],
                                                                         start=(k == 0), stop=(k == 1)),
                         reads=["cbf", ksl], writes=[pk])
                yt, kyt = tb.get()
                S.op("act", lambda e, ps=ps, yt=yt: e.activation(out=yt[:], in_=ps[:, 0:BW], func=AF.Copy),
                     reads=[pk], writes=[kyt])
                S.op("sp", lambda e, yt=yt, m=m: e.dma_start(out=ya_out[2 + m][:, bcols(b)], in_=yt[:]), reads=[kyt],
                     writes=[("oy", 2 + m, b)], dma_sem=f"sy{kyt[1]}")
                extra_outs.append(("oy", 2 + m, b))

        for b in range(NB):
            rg_group([0, 1, 2, 3], b, None)
            conformer_block(b)
            gate_block(b, cg_w)
        for t in range(4):
            w_issue()
        S.op("dve", lambda e: e.tensor_tensor(out=small[:, 52:56], in0=small[:, 40:44], in1=small[:, 0:4], op=ALU.mult),
             reads=[("rsum", t) for t in range(4)] + ["c1"], writes=["plog"])
        S.op("act", lambda e: e.activation(out=small[:, 52:56], in_=small[:, 52:56], func=AF.Exp),
             reads=["plog"], writes=["pval"])
        S.op("sp", lambda e: e.dma_start(out=out_d[:, 0:4], in_=small[:, 56:60]),
             reads=[("endst", t) for t in range(4)], writes=["o0"], dma_sem="st0")
        S.op("sp", lambda e: e.dma_start(out=out_d[:, 4:8], in_=small[:, 52:56]), reads=["pval"], writes=["o1"], dma_sem="st1")
        S.op("sp", lambda e: e.nop(), reads=["o0", "o1"] + extra_outs)
    else:
        for b in range(NB):
            rg_group([0, 1], b, None)
            rg_group([2, 3], b, None)

        for mt in range(KT):
            wt, wk = w_acquire()
            for b in range(NB):
                ps, pk = proj(wt, wk, w8, lambda k, b=b: (Y[:, k, bcols(b)], ("Y", k, b)), KT, b)
                S.op("dve", lambda e, ps=ps, mt=mt, b=b: e.tensor_tensor(out=h[:, mt, bcols(b)], in0=ps[:, 0:BW],
                                                                        in1=h[:, mt, bcols(b)], op=ALU.add),
                     reads=[pk, ("h", mt, b)], writes=[("h", mt, b)])
            w_issue()

        rms_to_A("g_mlp")
        for G in range(8):
            mo = (G % 2) * 4
            for j in range(4):
                wt, wk = w_acquire()
                for b in range(NB):
                    ps, pk = proj(wt, wk, w8, a_rhs(b), KT, b)
                    r, kr = tf.get()
                    S.op("act", lambda e, ps=ps, r=r: e.activation(out=r[:], in_=ps[:, 0:BW], func=AF.Relu),
                         reads=[pk], writes=[kr])
                    S.op("dve", lambda e, ps=ps, r=r, j=j, b=b, mo=mo: e.tensor_tensor(
                        out=Y[:, mo + j, bcols(b)], in0=ps[:, 0:BW], in1=r[:], op=ALU.mult),
                        reads=[pk, kr], writes=[("Y", mo + j, b)])
                w_issue()
            def down(wt, wk, q, b):
                for mm in range(2):
                    mt = 2 * q + mm
                    ps, pk = proj(wt, wk, lambda wt_, k, mm=mm: wt_[:, (k * 2 + mm) * 128:(k * 2 + mm + 1) * 128],
                                  lambda k, b=b, mo=mo: (Y[:, mo + k, bcols(b)], ("Y", mo + k, b)), 4, b)
                    S.op("dve", lambda e, ps=ps, mt=mt, b=b: e.tensor_tensor(out=h[:, mt, bcols(b)], in0=ps[:, 0:BW],
                                                                            in1=h[:, mt, bcols(b)], op=ALU.add),
                         reads=[pk, ("h", mt, b)], writes=[("h", mt, b)])

            if G < 7:
                for q in range(4):
                    wt, wk = w_acquire()
                    for b in range(NB):
                        down(wt, wk, q, b)
                    w_issue()
            else:
                wq = [w_acquire() for q in range(4)]
                for b in range(NB):
                    for q in range(4):
                        down(wq[q][0], wq[q][1], q, b)
                for q in range(4):
                    w_issue()

        finals = []
        if mode == "layer":
            out_p = out_d.rearrange("k p t -> p k t")
            for b in range(NB):
                S.op("sp", lambda e, b=b: e.dma_start(out=out_p[:, :, bcols(b)], in_=h[:, :, bcols(b)]),
                     reads=[("h", k, b) for k in range(KT)], writes=[("o", b)], dma_sem=f"st{b}")
                finals.append(("o", b))
        else:
            opool = tf
            for b in range(NB):
                cs = bcols(b)
                ps, pk = PS()
                for k in range(KT):
                    sq, sk = tb.get()
                    S.op("act", lambda e, k=k, cs=cs, sq=sq: e.activation(out=sq[:], in_=h[:, k, cs], func=AF.Square),
                         reads=[("h", k, b)], writes=[sk])
                    S.op("pe", lambda e, k=k, sq=sq, ps=ps: e.matmul(ps[:, 0:BW], lhsT=ones[:], rhs=sq[:],
                                                                     start=(k == 0), stop=(k == KT - 1)),
                         reads=["ones", sk], writes=[pk])
                l, lk = tf.get()
                S.op("act", lambda e, ps=ps, l=l: e.activation(out=l[:], in_=ps[:, 0:BW], func=AF.Ln, bias=EPS),
                     reads=[pk], writes=[lk])
                S.op("act", lambda e, ps=ps, l=l: e.activation(out=ps[:, 0:BW], in_=l[:], func=AF.Exp, scale=-0.5),
                     reads=[lk], writes=[pk])
                lo = PRE if b == 0 else 0
                for k in range(KT):
                    o, ok = opool.get()
                    S.op("dve", lambda e, k=k, cs=cs, ps=ps, o=o: e.scalar_tensor_tensor(
                        out=o[:], in0=h[:, k, cs], scalar=C("g_fin", k), in1=ps[:, 0:BW], op0=ALU.mult, op1=ALU.mult),
                        reads=[("h", k, b), pk, "cst"], writes=[ok])
                    d0 = b * BW + lo - PRE
                    S.op("sp", lambda e, k=k, o=o, lo=lo, d0=d0: e.dma_start(out=out_d[k][:, d0:d0 + BW - lo], in_=o[:, lo:BW]),
                         reads=[ok], writes=[("o", k, b)], dma_sem=f"sto{ok[1]}")
                    finals.append(("o", k, b))
        S.op("sp", lambda e: e.nop(), reads=finals)

    assert wstate["next_use"] == nblocks, (wstate, nblocks)
    with contextlib.ExitStack() as st:
        S.emit(nc, st)
    return nc


def _blk(w):
    return np.ascontiguousarray(w.reshape(8, 128, 128).transpose(1, 0, 2).reshape(128, 1024))


def _cx_blocks(w_in_l):
    return [_blk(w_in_l[:, 1280 + 128 * t:1280 + 128 * (t + 1)]) for t in range(4)]


def _pre_stream(w_in_l):
    col = lambda c0: _blk(w_in_l[:, c0:c0 + 128])
    blocks = [col(0), col(128)]
    blocks += _cx_blocks(w_in_l)
    blocks += [col(256), col(512), col(384), col(640)]
    blocks += [col(768 + 128 * t) for t in range(4)]
    return np.stack(blocks).astype(np.float32)


def _layer_stream(w_in_l, w_out_l, w_up_l, w_down_l):
    blocks = [_blk(w_out_l[:, 128 * m:128 * (m + 1)]) for m in range(8)]
    for G in range(8):
        for j in range(4):
            c0 = G * 512 + j * 128
            blocks.append(_blk(w_up_l[:, c0:c0 + 128]))
        for q in range(4):
            sub = w_down_l[G * 512:(G + 1) * 512, q * 256:(q + 1) * 256]
            blocks.append(np.ascontiguousarray(sub.reshape(4, 128, 2, 128).transpose(1, 0, 2, 3).reshape(128, 1024)))
    return np.stack(blocks).astype(np.float32)


def _pk(v, ntile):
    return np.ascontiguousarray(np.asarray(v, np.float32).reshape(ntile, 128).T)


def _consts(l, j, G, P):
    c = np.zeros((128, NCF), np.float32)

    def put(name, arr):
        arr = np.asarray(arr, np.float32).reshape(128, -1)
        c[:, _CST[name]:_CST[name] + arr.shape[1]] = arr

    put("g_mix", _pk(P["mix_norm_g"][l], 8))
    put("g_mlp", _pk(P["mlp_norm_g"][l], 8))
    put("g_fin", _pk(P["final_norm_g"], 8))
    put("pool_scale", _pk(P["pool_scale"][l], 2))
    wins = np.array([2, 4, 8, 16], np.float32)
    wpp = np.repeat(wins, 64)
    put("invw", _pk(1.0 / wpp, 2))
    put("invwm1", _pk(1.0 / wpp - 1.0, 2))
    put("himask", (np.arange(128) >= 64).astype(np.float32))
    put("dw_w", P["convb_dw_w"][l].T.reshape(2, 128, 31).transpose(1, 0, 2))
    put("dw_b", _pk(P["convb_dw_b"][l], 2))
    put("ln_g", _pk(P["convb_ln_g"][l], 2))
    put("ln_b", _pk(P["convb_ln_b"][l], 2))
    put("rgc_w", P["rg_conv_w"][l].T.reshape(4, 128, 4).transpose(1, 0, 2))
    put("rgc_b", _pk(P["rg_conv_b"][l], 4))
    put("b_a", _pk(P["rg_b_a"][l], 4))
    put("b_x", _pk(P["rg_b_x"][l], 4))
    put("lam", _pk(P["rg_lambda"][l], 4))
    sm = np.zeros((128, 32), np.float32)
    ratio = np.ones((128, 2, 32), np.float32)
    if j == 0:
        sm[:, 16:] = 1.0
        pos = np.arange(1, 17, dtype=np.float32)
        for t in range(2):
            wv = wpp[t * 128:(t + 1) * 128][:, None]
            ratio[:, t, 16:] = wv / np.minimum(pos[None, :], wv)
    put("scanmask", sm)
    put("ratio", ratio)
    pm = np.zeros((128, 8), np.float32)
    put("pm", pm)
    if G is not None:
        put("G", G)
    wg = np.zeros((2, 128, 128), np.float32)
    for g in range(4):
        t, o = divmod(g, 2)
        wg[t, o * 64:(o + 1) * 64, o * 64:(o + 1) * 64] = P["pool_w"][l][g]
    put("Wg", wg.transpose(1, 0, 2))
    cm = (np.eye(256, dtype=np.float32) - np.float32(1.0 / 256.0)).reshape(2, 128, 256)
    put("Cmat", cm.transpose(1, 0, 2))
    put("Jm", np.full((128, 128), 1.0 / 256.0, np.float32))
    return c


def _cbf(l, P):
    c = np.zeros((128, NCB), np.float32)
    c[:, 0:128] = np.eye(128, dtype=np.float32)
    c[:, 128:640] = P["convb_pw_w"][l].reshape(2, 128, 256).transpose(1, 0, 2).reshape(128, 512)
    for nm, key in (("Wa", "rg_w_a"), ("Wx", "rg_w_x")):
        bd = np.zeros((4, 128, 128), np.float32)
        for hd in range(8):
            t, o = divmod(hd, 2)
            bd[t, o * 64:(o + 1) * 64, o * 64:(o + 1) * 64] = P[key][l][hd]
        c[:, _CBF[nm]:_CBF[nm] + 512] = bd.transpose(1, 0, 2).reshape(128, 512)
    return c


_PROG = {}


def _prog(mode, nblocks):
    if (mode, nblocks) not in _PROG:
        _PROG[(mode, nblocks)] = build_program(mode, nblocks)
    return _PROG[(mode, nblocks)]


def kernel(**inputs):
    P = {k: np.asarray(v, np.float32) for k, v in inputs.items()}
    x = P["x"]
    B = x.shape[0]
    hT = []
    for r in range(NCORES):
        b, j = divmod(r, 4)
        seq = np.concatenate([P["meta_tokens"], x[b]], axis=0)
        s0 = 16 + MAIN * j
        if j == 0:
            tok = np.concatenate([np.zeros((16, D), np.float32), seq[0:16 + MAIN]], axis=0)
        else:
            tok = seq[s0 - PRE:s0 + MAIN]
        hT.append(np.ascontiguousarray(tok.T).reshape(KT, 128, NT))
    out = None
    for l in range(2):
        cbf = _cbf(l, P)
        wpre = _pre_stream(P["w_in"][l])
        ncA = _prog("pre", wpre.shape[0])
        mapsA = [{"hT": hT[r], "wstream": wpre, "cst": _consts(l, r % 4, None, P), "cbf": cbf} for r in range(NCORES)]
        resA = run_bass_kernel_spmd(ncA, mapsA, core_ids=list(range(NCORES)))
        G = np.stack([np.asarray(resA.results[r]["endp"], np.float32) for r in range(NCORES)], axis=1)
        mode = "layer" if l == 0 else "last"
        wst = _layer_stream(P["w_in"][l], P["w_out"][l], P["w_up"][l], P["w_down"][l])
        ncB = _prog(mode, wst.shape[0])
        mapsB = []
        for r in range(NCORES):
            b, j = divmod(r, 4)
            c = _consts(l, j, G.reshape(128, 64), P)
            pm = np.zeros((128, 8), np.float32)
            pm[:, 4 * b:4 * b + j] = 1.0
            c[:, _CST["pm"]:_CST["pm"] + 8] = pm
            mapsB.append({"hT": hT[r], "wstream": wst, "cst": c, "cbf": cbf,
                          "ya_in": np.asarray(resA.results[r]["ya_out"]), "arg_in": np.asarray(resA.results[r]["arg_out"]),
                          "brg_in": np.asarray(resA.results[r]["brg_out"]),
                          "gate_in": np.asarray(resA.results[r]["gate_out"])})
        resB = run_bass_kernel_spmd(ncB, mapsB, core_ids=list(range(NCORES)))
        if l == 0:
            hn = [np.asarray(resB.results[r]["hout"], np.float32) for r in range(NCORES)]
            hT = []
            for r in range(NCORES):
                b, j = divmod(r, 4)
                t = hn[r].copy()
                if j == 0:
                    t[:, :, 0:16] = 0.0
                else:
                    t[:, :, 0:PRE] = hn[r - 1][:, :, NT - PRE:NT]
                hT.append(t)
        else:
            out = np.zeros((B, 4 * MAIN, D), np.float32)
            for r in range(NCORES):
                b, j = divmod(r, 4)
                y = np.asarray(resB.results[r]["yout"], np.float32).reshape(D, MAIN)
                out[b, MAIN * j:MAIN * (j + 1), :] = y.T
    return out
```

```python
import contextlib
import numpy as np
import concourse.bass as bass
import concourse.mybir as mybir
from concourse.bass_utils import run_bass_kernel_spmd

F32 = mybir.dt.float32
BF16 = mybir.dt.bfloat16
AF = mybir.ActivationFunctionType
ALU = mybir.AluOpType

NCORES = 8
D = 1024
KT = 8
PRE = 32
MAIN = 2048
NT = PRE + MAIN
NB = 5
BW = NT // NB
PADL = 32
EPS = 1e-6
NRING = 6
GELU_K = 1.5957691216057308

_CST = {}
_off = 0
for _n, _w in [("g_mix", 8), ("g_mlp", 8), ("g_fin", 8), ("pool_scale", 2), ("invw", 2), ("invwm1", 2),
               ("himask", 1), ("dw_w", 62), ("dw_b", 2), ("ln_g", 2), ("ln_b", 2), ("rgc_w", 16),
               ("rgc_b", 4), ("b_a", 4), ("b_x", 4), ("lam", 4), ("scanmask", 32), ("ratio", 64),
               ("pm", 8), ("G", 64), ("Wg", 256), ("Cmat", 512), ("Jm", 128)]:
    _CST[_n] = _off
    _off += _w
NCF = _off
_CBF = {"ident": 0, "pw": 128, "Wa": 640, "Wx": 1152}
NCB = 1664

ENGS = ("pe", "act", "dve", "pool", "sp")


class Op:
    __slots__ = ("idx", "eng", "fn", "deps", "dma_sem", "count", "signal", "waits", "inc")

    def __init__(self, idx, eng, fn, dma_sem, inc):
        self.idx = idx
        self.eng = eng
        self.fn = fn
        self.deps = {}
        self.dma_sem = dma_sem
        self.count = None
        self.signal = False
        self.waits = []
        self.inc = inc


class Sched:
    def __init__(self):
        self.ops = []
        self.last_writer = {}
        self.readers = {}

    def op(self, eng, fn, reads=(), writes=(), dma_sem=None, inc=16):
        o = Op(len(self.ops), eng, fn, dma_sem, inc)
        for k in reads:
            w = self.last_writer.get(k)
            if w is not None:
                o.deps[w] = "raw"
            if isinstance(k, tuple) and k[0] == "ps":
                for r in self.readers.get(k, ()):
                    if self.ops[r].eng != eng and r not in o.deps:
                        o.deps[r] = "rar"
        for k in writes:
            w = self.last_writer.get(k)
            if w is not None and w not in o.deps:
                o.deps[w] = "waw"
            for r in self.readers.get(k, ()):
                if r not in o.deps:
                    o.deps[r] = "war"
        for k in reads:
            self.readers.setdefault(k, []).append(o.idx)
        for k in writes:
            self.last_writer[k] = o.idx
            self.readers[k] = []
        self.ops.append(o)
        return o

    def emit(self, nc, st):
        ops = self.ops
        for o in ops:
            for d, kind in o.deps.items():
                p = ops[d]
                if p.dma_sem is not None or o.dma_sem is not None or p.eng != o.eng:
                    need = True
                else:
                    need = (o.eng != "pe")
                if need:
                    o.waits.append(d)
                    p.signal = True
        cnt = {e: 0 for e in ENGS}
        dcnt = {}
        for o in ops:
            if o.dma_sem is not None:
                dcnt[o.dma_sem] = dcnt.get(o.dma_sem, 0) + o.inc
                o.count = dcnt[o.dma_sem]
            elif o.signal:
                cnt[o.eng] += 1
                o.count = cnt[o.eng]
        sems = {e: st.enter_context(nc.semaphore("s_" + e)) for e in ENGS}
        dsems = {k: st.enter_context(nc.semaphore("d_" + str(k))) for k in sorted(dcnt)}
        block = st.enter_context(nc.Block())
        queues = {e: [o for o in ops if o.eng == e] for e in ENGS}

        def run_queue(eng_name, engine):
            waited = {}
            for o in queues[eng_name]:
                need = {}
                for d in o.waits:
                    p = ops[d]
                    key = ("d", p.dma_sem) if p.dma_sem is not None else ("e", p.eng)
                    if p.count > need.get(key, 0):
                        need[key] = p.count
                for key, val in need.items():
                    if waited.get(key, 0) >= val:
                        continue
                    waited[key] = val
                    engine.wait_ge(dsems[key[1]] if key[0] == "d" else sems[key[1]], val)
                ins = o.fn(engine)
                if o.dma_sem is not None:
                    ins.then_inc(dsems[o.dma_sem], o.inc)
                elif o.signal:
                    ins.then_inc(sems[o.eng], 1)

        @block.tensor
        def _(e):
            run_queue("pe", e)

        @block.scalar
        def _(e):
            run_queue("act", e)

        @block.vector
        def _(e):
            run_queue("dve", e)

        @block.gpsimd
        def _(e):
            run_queue("pool", e)

        @block.sync
        def _(e):
            run_queue("sp", e)


class Pool:
    def __init__(self, nc, name, shape, dtype, n):
        self.t = [nc.alloc_sbuf_tensor(f"{name}{i}", shape, dtype) for i in range(n)]
        self.name = name
        self.i = 0

    def get(self):
        i = self.i % len(self.t)
        self.i += 1
        return self.t[i], (self.name, i)


def build_program(mode, nblocks):
    nc = bass.Bass("TRN2", target_bir_lowering=False)
    hT = nc.dram_tensor("hT", [KT, 128, NT], F32, kind="ExternalInput").ap()
    wstream = nc.dram_tensor("wstream", [nblocks, 128, 1024], F32, kind="ExternalInput").ap()
    cst_d = nc.dram_tensor("cst", [128, NCF], F32, kind="ExternalInput").ap()
    cbf_d = nc.dram_tensor("cbf", [128, NCB], F32, kind="ExternalInput").ap()
    if mode == "pre":
        out_d = nc.dram_tensor("endp", [128, 8], F32, kind="ExternalOutput").ap()
        gate_out = nc.dram_tensor("gate_out", [4, 128, NT], F32, kind="ExternalOutput").ap()
        ya_out = nc.dram_tensor("ya_out", [4, 128, NT], BF16, kind="ExternalOutput").ap()
        arg_out = nc.dram_tensor("arg_out", [4, 128, NT], F32, kind="ExternalOutput").ap()
        brg_out = nc.dram_tensor("brg_out", [4, 128, NT], F32, kind="ExternalOutput").ap()
    else:
        gate_in = nc.dram_tensor("gate_in", [4, 128, NT], F32, kind="ExternalInput").ap()
        ya_in = nc.dram_tensor("ya_in", [4, 128, NT], BF16, kind="ExternalInput").ap()
        arg_in = nc.dram_tensor("arg_in", [4, 128, NT], F32, kind="ExternalInput").ap()
        brg_in = nc.dram_tensor("brg_in", [4, 128, NT], F32, kind="ExternalInput").ap()
    if mode == "pre":
        pass
    elif mode == "layer":
        out_d = nc.dram_tensor("hout", [KT, 128, NT], F32, kind="ExternalOutput").ap()
    else:
        out_d = nc.dram_tensor("yout", [KT, 128, MAIN], F32, kind="ExternalOutput").ap()

    S = Sched()
    h = nc.alloc_sbuf_tensor("h", [128, KT, NT], F32)
    A = nc.alloc_sbuf_tensor("A", [128, KT, NT], BF16)
    Y = nc.alloc_sbuf_tensor("Y", [128, KT, NT], BF16) if mode != "pre" else None
    NSTG = 6 if mode == "pre" else 4
    STG = nc.alloc_sbuf_tensor("STG", [128, NSTG, PADL + NT], BF16)
    ring = [nc.alloc_sbuf_tensor(f"wr{i}", [128, 1024], BF16) for i in range(NRING)]
    cst = nc.alloc_sbuf_tensor("cst_s", [128, NCF], F32)
    cbf = nc.alloc_sbuf_tensor("cbf_s", [128, NCB], BF16)
    pmat = nc.alloc_sbuf_tensor("pmat", [128, 8, 128], BF16)
    rgd = nc.alloc_sbuf_tensor("rgd", [128, 16, 128], BF16)
    ones = nc.alloc_sbuf_tensor("ones", [128, 128], BF16)
    dres = nc.alloc_sbuf_tensor("dres", [128, 62, 128], BF16) if mode == "pre" else None
    small = nc.alloc_sbuf_tensor("small", [128, 64], F32)
    tf = Pool(nc, "tf", [128, BW], F32, 17 if mode == "pre" else 12)
    ths = Pool(nc, "ths", [128, BW], F32, 5)
    tb = Pool(nc, "tb", [128, BW], BF16, 8 if mode == "pre" else 6)
    ps_t = [nc.alloc_psum_tensor(f"ps{i}", [128, 512], F32) for i in range(8)]
    ps_i = [0]

    def PS():
        i = ps_i[0] % 8
        ps_i[0] += 1
        return ps_t[i], ("ps", i)

    def C(name, j=0, w=1):
        o = _CST[name] + j
        return cst[:, o:o + w]

    def bcols(b):
        return slice(b * BW, (b + 1) * BW)

    S.op("sp", lambda e: e.dma_start(out=cst[:], in_=cst_d), writes=["cst"], dma_sem="ldc")
    S.op("pool", lambda e: e.dma_start(out=cbf[:], in_=cbf_d), writes=["cbf"], dma_sem="ldb")
    hT_p = hT.rearrange("k p t -> p k t")
    for b in range(NB):
        S.op("sp", lambda e, b=b: e.dma_start(out=h[:, :, bcols(b)], in_=hT_p[:, :, bcols(b)]),
             writes=[("h", k, b) for k in range(KT)], dma_sem=f"ldh{b}")

    wstate = {"next_dma": 0, "next_use": 0}

    def w_issue():
        i = wstate["next_dma"]
        if i >= nblocks:
            return
        wstate["next_dma"] += 1
        s = i % NRING
        S.op("pool", lambda e, i=i, s=s: e.dma_start(out=ring[s][:], in_=wstream[i]),
             writes=[("w", s)], dma_sem=f"w{s}")

    for _ in range(NRING):
        w_issue()

    def w_acquire():
        i = wstate["next_use"]
        wstate["next_use"] += 1
        s = i % NRING
        return ring[s], ("w", s)

    S.op("dve", lambda e: e.memset(ones[:], 1.0 / 1024.0), writes=["ones"])
    for s4 in range(NSTG):
        S.op("dve", lambda e, s4=s4: e.memset(STG[:, s4, 0:PADL], 0.0), writes=[("stgpad", s4)])
    S.op("act", lambda e: e.activation(out=small[:, 12:16], in_=C("lam", 0, 4), func=AF.Exp, scale=-1.0),
         reads=["cst"], writes=["sm_t"])
    S.op("act", lambda e: e.activation(out=small[:, 16:20], in_=small[:, 12:16], func=AF.Ln, bias=1.0),
         reads=["sm_t"], writes=["sm_t2"])
    S.op("dve", lambda e: e.tensor_scalar(out=small[:, 0:4], in0=small[:, 16:20], scalar1=-8.0, scalar2=None,
                                          op0=ALU.mult), reads=["sm_t2"], writes=["c1"])
    S.op("dve", lambda e: e.tensor_scalar(out=small[:, 4:8], in0=small[:, 16:20], scalar1=-16.0, scalar2=None,
                                          op0=ALU.mult), reads=["sm_t2"], writes=["c2"])
    S.op("dve", lambda e: e.memset(small[:, 8:12], 0.0), writes=["carry"])
    if mode != "pre":
        for r in range(8):
            gE = C("G", r * 8, 4)
            gP = C("G", r * 8 + 4, 4)
            S.op("dve", lambda e, gP=gP: e.tensor_tensor(out=small[:, 20:24], in0=gP, in1=small[:, 8:12], op=ALU.mult),
                 reads=["cst", "carry"], writes=["cc_t"])
            S.op("dve", lambda e, gE=gE: e.tensor_tensor(out=small[:, 24:28], in0=small[:, 20:24], in1=gE, op=ALU.add),
                 reads=["cc_t", "cst"], writes=["cc_u"])
            S.op("dve", lambda e: e.tensor_tensor(out=small[:, 28:32], in0=small[:, 24:28], in1=small[:, 8:12],
                                                  op=ALU.subtract), reads=["cc_u", "carry"], writes=["cc_v"])
            S.op("dve", lambda e, r=r: e.scalar_tensor_tensor(out=small[:, 32:36], in0=small[:, 28:32],
                                                              scalar=C("pm", r), in1=small[:, 8:12],
                                                              op0=ALU.mult, op1=ALU.add),
                 reads=["cc_v", "carry", "cst"], writes=["cc_w"])
            S.op("dve", lambda e: e.tensor_copy(out=small[:, 8:12], in_=small[:, 32:36]), reads=["cc_w"], writes=["carry"])
    for t in range(4):
        for k in range(4):
            S.op("pool", lambda e, t=t, k=k: e.tensor_scalar(out=rgd[:, t * 4 + k, :], in0=cbf[:, 0:128],
                                                             scalar1=C("rgc_w", t * 4 + k), scalar2=1.0,
                                                             op0=ALU.mult, op1=ALU.mult),
                 reads=["cst", "cbf"], writes=[("rgd", t)])
    if mode == "pre":
        for j in range(62):
            S.op("pool", lambda e, j=j: e.tensor_scalar(out=dres[:, j, :], in0=cbf[:, 0:128], scalar1=C("dw_w", j), scalar2=1.0,
                                                        op0=ALU.mult, op1=ALU.mult),
                 reads=["cst", "cbf"], writes=["dres"])
        for t in range(2):
            wg = cst[:, _CST["Wg"] + t * 128:_CST["Wg"] + (t + 1) * 128]
            S.op("dve", lambda e, t=t, wg=wg: e.tensor_copy(out=pmat[:, t, :], in_=wg), reads=["cst"], writes=[("pm_", t)])
            S.op("dve", lambda e, t=t, wg=wg: e.tensor_scalar(out=pmat[:, 2 + t, :], in0=wg, scalar1=C("invwm1", t),
                                                              scalar2=None, op0=ALU.mult), reads=["cst"], writes=[("pm_", t)])
            S.op("dve", lambda e, t=t, wg=wg: e.tensor_scalar(out=pmat[:, 4 + t, :], in0=wg, scalar1=C("invw", t),
                                                              scalar2=None, op0=ALU.mult), reads=["cst"], writes=[("pm_", t)])
            S.op("dve", lambda e, t=t, wg=wg: e.tensor_scalar(out=pmat[:, 6 + t, :], in0=wg, scalar1=C("invw", t),
                                                              scalar2=C("himask"), op0=ALU.mult, op1=ALU.mult),
                 reads=["cst"], writes=[("pm_", t)])

    def rms_to_A(gname):
        for b in range(NB):
            cs = bcols(b)
            ps, pk = PS()
            for k in range(KT):
                sq, sk = tb.get()
                S.op("act", lambda e, k=k, cs=cs, sq=sq: e.activation(out=sq[:], in_=h[:, k, cs], func=AF.Square),
                     reads=[("h", k, b)], writes=[sk])
                S.op("pe", lambda e, k=k, sq=sq, ps=ps: e.matmul(ps[:, 0:BW], lhsT=ones[:], rhs=sq[:],
                                                                 start=(k == 0), stop=(k == KT - 1)),
                     reads=["ones", sk], writes=[pk])
            l, lk = tf.get()
            S.op("act", lambda e, ps=ps, l=l: e.activation(out=l[:], in_=ps[:, 0:BW], func=AF.Ln, bias=EPS),
                 reads=[pk], writes=[lk])
            S.op("act", lambda e, ps=ps, l=l: e.activation(out=ps[:, 0:BW], in_=l[:], func=AF.Exp, scale=-0.5),
                 reads=[lk], writes=[pk])
            for k in range(KT):
                S.op("dve", lambda e, k=k, cs=cs, ps=ps: e.scalar_tensor_tensor(
                    out=A[:, k, cs], in0=h[:, k, cs], scalar=C(gname, k), in1=ps[:, 0:BW],
                    op0=ALU.mult, op1=ALU.mult),
                    reads=[("h", k, b), pk, "cst"], writes=[("A", k, b)])

    def proj(wt, wk, wsel, rhs_fn, nk, b):
        ps, pk = PS()
        for k in range(nk):
            rap, rkey = rhs_fn(k)
            S.op("pe", lambda e, k=k, rap=rap, ps=ps: e.matmul(ps[:, 0:BW], lhsT=wsel(wt, k), rhs=rap,
                                                               start=(k == 0), stop=(k == nk - 1)),
                 reads=[wk, rkey], writes=[pk])
        return ps, pk

    def w8(wt, k):
        return wt[:, k * 128:(k + 1) * 128]

    def a_rhs(b):
        return lambda k: (A[:, k, bcols(b)], ("A", k, b))

    def stg_cols(slot, b, shift):
        c0 = PADL + b * BW - shift
        return STG[:, slot, c0:c0 + BW]

    def stg_keys(slot, b):
        ks = [("stg", slot, b)]
        ks.append(("stg", slot, b - 1) if b > 0 else ("stgpad", slot))
        return ks

    def inproj_to_stg(slot):
        wt, wk = w_acquire()
        for b in range(NB):
            ps, pk = proj(wt, wk, w8, a_rhs(b), KT, b)
            S.op("act", lambda e, ps=ps, b=b: e.activation(out=STG[:, slot, PADL + b * BW:PADL + (b + 1) * BW],
                                                           in_=ps[:, 0:BW], func=AF.Copy),
                 reads=[pk], writes=[("stg", slot, b)])
        w_issue()

    extra_outs = []
    if mode == "pre":
        rms_to_A("g_mix")
    else:
        for t in range(4):
            S.op("sp", lambda e, t=t: e.dma_start(out=Y[:, t, :], in_=ya_in[t]),
                 writes=[("Y", t, b) for b in range(NB)], dma_sem=f"ldy{t}")

    if mode == "pre":
        inproj_to_stg(0)
        inproj_to_stg(1)
        def pool_tile(t):
            wlo, whi = (2, 4) if t == 0 else (8, 16)
            for b in range(NB):
                ps, pk = PS()
                for k in range(whi):
                    mat = pmat[:, 2 + t, :] if k == 0 else (pmat[:, 4 + t, :] if k < wlo else pmat[:, 6 + t, :])
                    S.op("pe", lambda e, k=k, mat=mat, ps=ps, b=b: e.matmul(ps[:, 0:BW], lhsT=mat, rhs=stg_cols(t, b, k),
                                                                           start=(k == 0), stop=(k == whi - 1)),
                         reads=[("pm_", t)] + stg_keys(t, b), writes=[pk])
                yt, kyt = tb.get()
                S.op("act", lambda e, ps=ps, yt=yt: e.activation(out=yt[:], in_=ps[:, 0:BW],
                                                                 func=AF.Identity, scale=C("pool_scale", t)),
                     reads=[pk, "cst"], writes=[kyt])
                if b == 0:
                    y0 = (yt, kyt)
                else:
                    S.op("sp", lambda e, yt=yt, b=b: e.dma_start(out=ya_out[t][:, bcols(b)], in_=yt[:]), reads=[kyt],
                         writes=[("oy", t, b)], dma_sem=f"sy{kyt[1]}")
                    extra_outs.append(("oy", t, b))
            psS, pkS = PS()
            for k in range(whi):
                mat = pmat[:, 4 + t, :] if k < wlo else pmat[:, 6 + t, :]
                S.op("pe", lambda e, k=k, mat=mat, psS=psS: e.matmul(psS[:, 0:PRE], lhsT=mat,
                                                                    rhs=STG[:, t, PADL - k:PADL - k + PRE],
                                                                    start=(k == 0), stop=(k == whi - 1)),
                     reads=[("pm_", t)] + stg_keys(t, 0), writes=[pkS])
            psX, pkX = PS()
            S.op("pe", lambda e, psX=psX: e.matmul(psX[:, 0:PRE], lhsT=pmat[:, t, :], rhs=STG[:, t, PADL:PADL + PRE],
                                                   start=True, stop=True),
                 reads=[("pm_", t)] + stg_keys(t, 0), writes=[pkX])
            t1, k1 = tf.get()
            S.op("dve", lambda e, psS=psS, t1=t1: e.tensor_tensor(out=t1[:, 0:PRE], in0=psS[:, 0:PRE],
                                                                  in1=C("ratio", t * 32, 32), op=ALU.mult),
                 reads=[pkS, "cst"], writes=[k1])
            t2, k2 = tf.get()
            S.op("dve", lambda e, psX=psX, t1=t1, t2=t2: e.tensor_tensor(out=t2[:, 0:PRE], in0=t1[:, 0:PRE],
                                                                         in1=psX[:, 0:PRE], op=ALU.subtract),
                 reads=[pkX, k1], writes=[k2])
            yt, kyt = y0
            S.op("act", lambda e, t2=t2, yt=yt: e.activation(out=yt[:, 0:PRE], in_=t2[:, 0:PRE], func=AF.Identity,
                                                             scale=C("pool_scale", t)),
                 reads=[k2, "cst"], writes=[kyt])
            S.op("sp", lambda e, yt=yt: e.dma_start(out=ya_out[t][:, bcols(0)], in_=yt[:]), reads=[kyt],
                 writes=[("oy", t, 0)], dma_sem=f"sy{kyt[1]}")
            extra_outs.append(("oy", t, 0))

        pool_tile(0)
        pool_tile(1)

    if mode == "pre":
        S.op("dve", lambda e: e.memset(small[:, 40:44], 0.0), writes=[("rsum", t) for t in range(4)])
    rg_slots = [2, 3, 2, 3] if mode != "pre" else [0, 1, 2, 3]
    if mode == "pre":
        for t in range(4):
            inproj_to_stg(rg_slots[t])
    else:
        pass

    rg_prev = {}

    def rg_group(tiles, b, cgws):
        U = {t: {} for t in tiles}
        if mode != "pre":
            for t in tiles:
                a, ka = tf.get()
                bb, kb = tf.get()
                S.op("sp", lambda e, a=a, t=t: e.dma_start(out=a[:], in_=arg_in[t][:, bcols(b)]), writes=[ka],
                     dma_sem=f"so{ka[1]}")
                S.op("sp", lambda e, bb=bb, t=t: e.dma_start(out=bb[:], in_=brg_in[t][:, bcols(b)]), writes=[kb],
                     dma_sem=f"so{kb[1]}")
                U[t].update(a=a, ka=ka, xc=bb, kxc=kb)
        else:
            for t in tiles:
                slot = rg_slots[t]
                ps_c, pk_c = PS()
                for k in range(4):
                    S.op("pe", lambda e, k=k, ps_c=ps_c, t=t, slot=slot: e.matmul(
                        ps_c[:, 0:BW], lhsT=rgd[:, t * 4 + k, :], rhs=stg_cols(slot, b, 3 - k),
                        start=(k == 0), stop=(k == 3)),
                        reads=[("rgd", t)] + stg_keys(slot, b), writes=[pk_c])
                U[t].update(ps_c=ps_c, pk_c=pk_c)
            for t in tiles:
                u = U[t]
                xc, kxc = tf.get()
                xcb, kxcb = tb.get()
                S.op("dve", lambda e, ps_c=u["ps_c"], xc=xc, t=t: e.tensor_scalar(out=xc[:], in0=ps_c[:, 0:BW], scalar1=C("rgc_b", t),
                                                                                 scalar2=None, op0=ALU.add),
                     reads=[u["pk_c"], "cst"], writes=[kxc])
                S.op("dve", lambda e, xc=xc, xcb=xcb: e.tensor_copy(out=xcb[:], in_=xc[:]), reads=[kxc], writes=[kxcb])
                u.update(xc=xc, kxc=kxc, xcb=xcb, kxcb=kxcb)
            for t in tiles:
                u = U[t]
                ps_a, pk_a = PS()
                S.op("pe", lambda e, ps_a=ps_a, xcb=u["xcb"], t=t: e.matmul(
                    ps_a[:, 0:BW], lhsT=cbf[:, _CBF["Wa"] + t * 128:_CBF["Wa"] + (t + 1) * 128], rhs=xcb[:], start=True, stop=True),
                    reads=["cbf", u["kxcb"]], writes=[pk_a])
                ps_x, pk_x = PS()
                S.op("pe", lambda e, ps_x=ps_x, xcb=u["xcb"], t=t: e.matmul(
                    ps_x[:, 0:BW], lhsT=cbf[:, _CBF["Wx"] + t * 128:_CBF["Wx"] + (t + 1) * 128], rhs=xcb[:], start=True, stop=True),
                    reads=["cbf", u["kxcb"]], writes=[pk_x])
                u.update(ps_a=ps_a, pk_a=pk_a, ps_x=ps_x, pk_x=pk_x)
            for t in tiles:
                u = U[t]
                r, kr = tf.get()
                if mode == "pre":
                    lo = PRE if b == 0 else 0
                    if b == 0:
                        S.op("act", lambda e, ps_a=u["ps_a"], r=r, t=t: e.activation(out=r[:, 0:PRE], in_=ps_a[:, 0:PRE],
                                                                                    func=AF.Sigmoid, bias=C("b_a", t)),
                             reads=[u["pk_a"], "cst"], writes=[kr])
                    S.op("act", lambda e, ps_a=u["ps_a"], r=r, lo=lo, t=t: e.activation(
                        out=r[:, lo:BW], in_=ps_a[:, lo:BW], func=AF.Sigmoid, bias=C("b_a", t),
                        accum_out=small[:, 44 + t:45 + t]),
                        reads=[u["pk_a"], "cst"], writes=[kr, ("racc", t)])
                    S.op("dve", lambda e, t=t: e.tensor_tensor(out=small[:, 40 + t:41 + t], in0=small[:, 40 + t:41 + t],
                                                               in1=small[:, 44 + t:45 + t], op=ALU.add),
                         reads=[("racc", t), ("rsum", t)], writes=[("rsum", t)])
                else:
                    S.op("act", lambda e, ps_a=u["ps_a"], r=r, t=t: e.activation(out=r[:], in_=ps_a[:, 0:BW], func=AF.Sigmoid,
                                                                                bias=C("b_a", t)),
                         reads=[u["pk_a"], "cst"], writes=[kr])
                u.update(r=r, kr=kr)
            for t in tiles:
                u = U[t]
                a, ka = tf.get()
                S.op("act", lambda e, r=u["r"], a=a, t=t: e.activation(out=a[:], in_=r[:], func=AF.Exp, scale=small[:, t:t + 1]),
                     reads=[u["kr"], "c1"], writes=[ka])
                u.update(a=a, ka=ka)
            for t in tiles:
                u = U[t]
                S.op("dve", lambda e, r=u["r"], a=u["a"]: e.tensor_tensor(out=r[:], in0=a[:], in1=a[:], op=ALU.mult),
                     reads=[u["ka"], u["kr"]], writes=[u["kr"]])
            for t in tiles:
                u = U[t]
                S.op("act", lambda e, ps_a=u["ps_a"], ps_x=u["ps_x"], t=t: e.activation(
                    out=ps_a[:, 0:BW], in_=ps_x[:, 0:BW], func=AF.Sigmoid, bias=C("b_x", t)),
                    reads=[u["pk_x"], "cst"], writes=[u["pk_a"]])
            for t in tiles:
                u = U[t]
                S.op("act", lambda e, r=u["r"], ps_x=u["ps_x"]: e.activation(out=ps_x[:, 0:BW], in_=r[:], func=AF.Sqrt,
                                                                             scale=-1.0, bias=1.0),
                     reads=[u["kr"]], writes=[u["pk_x"]])
            for t in tiles:
                u = U[t]
                S.op("dve", lambda e, ps_a=u["ps_a"], xc=u["xc"]: e.tensor_tensor(out=xc[:], in0=ps_a[:, 0:BW], in1=xc[:], op=ALU.mult),
                     reads=[u["pk_a"], u["kxc"]], writes=[u["kxc"]])
            for t in tiles:
                u = U[t]
                S.op("dve", lambda e, ps_x=u["ps_x"], xc=u["xc"]: e.tensor_tensor(out=xc[:], in0=ps_x[:, 0:BW], in1=xc[:], op=ALU.mult),
                     reads=[u["pk_x"], u["kxc"]], writes=[u["kxc"]])
        if mode == "pre":
            if b == 0:
                for t in tiles:
                    u = U[t]
                    S.op("dve", lambda e, bb=u["xc"]: e.tensor_tensor(out=bb[:, 0:PRE], in0=bb[:, 0:PRE], in1=C("scanmask", 0, 32),
                                                                      op=ALU.mult), reads=[u["kxc"], "cst"], writes=[u["kxc"]])
            for t in tiles:
                u = U[t]
                S.op("sp", lambda e, a=u["a"], t=t: e.dma_start(out=arg_out[t][:, bcols(b)], in_=a[:]), reads=[u["ka"]],
                     writes=[("oar", t, b)], dma_sem=f"so{u['ka'][1]}")
                S.op("sp", lambda e, bb=u["xc"], t=t: e.dma_start(out=brg_out[t][:, bcols(b)], in_=bb[:]), reads=[u["kxc"]],
                     writes=[("obr", t, b)], dma_sem=f"so{u['kxc'][1]}")
                extra_outs.append(("oar", t, b))
                extra_outs.append(("obr", t, b))
        if b == 0:
            if mode != "pre":
                for t in tiles:
                    u = U[t]
                    S.op("dve", lambda e, bb=u["xc"], a=u["a"], t=t: e.scalar_tensor_tensor(
                        out=bb[:, PRE:PRE + 1], in0=a[:, PRE:PRE + 1], scalar=small[:, 8 + t:9 + t],
                        in1=bb[:, PRE:PRE + 1], op0=ALU.mult, op1=ALU.add),
                        reads=[u["kxc"], u["ka"], "carry"], writes=[u["kxc"]])
        for t in tiles:
            u = U[t]
            hs, khs = ths.get()
            if t not in rg_prev:
                S.op("dve", lambda e, hs=hs, a=u["a"], bb=u["xc"]: e.tensor_tensor_scan(
                    out=hs[:], data0=a[:], data1=bb[:], initial=0.0, op0=ALU.mult, op1=ALU.add),
                    reads=[u["ka"], u["kxc"]], writes=[khs])
            else:
                ph, pkh = rg_prev[t]
                S.op("dve", lambda e, hs=hs, a=u["a"], bb=u["xc"], ph=ph: e.tensor_tensor_scan(
                    out=hs[:], data0=a[:], data1=bb[:], initial=ph[:, BW - 1:BW], op0=ALU.mult, op1=ALU.add),
                    reads=[u["ka"], u["kxc"], pkh], writes=[khs])
            rg_prev[t] = (hs, khs)
            u.update(hs=hs, khs=khs)
        if mode == "pre":
            if b == NB - 1:
                for t in tiles:
                    S.op("dve", lambda e, hs=U[t]["hs"], t=t: e.tensor_copy(out=small[:, 56 + t:57 + t], in_=hs[:, BW - 1:BW]),
                         reads=[U[t]["khs"]], writes=[("endst", t)])
            return
        for t in tiles:
            u = U[t]
            g, kg = tf.get()
            S.op("sp", lambda e, g=g, t=t: e.dma_start(out=g[:], in_=gate_in[t][:, bcols(b)]), writes=[kg], dma_sem=f"so{kg[1]}")
            S.op("dve", lambda e, g=g, hs=u["hs"], t=t: e.tensor_tensor(out=Y[:, 4 + t, bcols(b)], in0=g[:], in1=hs[:], op=ALU.mult),
                 reads=[kg, u["khs"]], writes=[("Y", 4 + t, b)])

    def gate_block(b, cgws):
        U = {t: {} for t in range(4)}
        for t in range(4):
            wt, wk = cgws[t]
            ps_g, pk_g = proj(wt, wk, w8, a_rhs(b), KT, b)
            U[t].update(ps_g=ps_g, pk_g=pk_g)
        for t in range(4):
            u = U[t]
            x2, kx2 = tf.get()
            S.op("act", lambda e, ps_g=u["ps_g"], x2=x2: e.activation(out=x2[:], in_=ps_g[:, 0:BW], func=AF.Square),
                 reads=[u["pk_g"]], writes=[kx2])
            u.update(x2=x2, kx2=kx2)
        for t in range(4):
            u = U[t]
            S.op("dve", lambda e, x2=u["x2"]: e.tensor_scalar(out=x2[:], in0=x2[:], scalar1=0.044715 * GELU_K, scalar2=GELU_K,
                                                              op0=ALU.mult, op1=ALU.add), reads=[u["kx2"]], writes=[u["kx2"]])
        for t in range(4):
            u = U[t]
            S.op("dve", lambda e, ps_g=u["ps_g"], x2=u["x2"]: e.tensor_tensor(out=x2[:], in0=ps_g[:, 0:BW], in1=x2[:], op=ALU.mult),
                 reads=[u["pk_g"], u["kx2"]], writes=[u["kx2"]])
        for t in range(4):
            u = U[t]
            S.op("act", lambda e, q=u["x2"]: e.activation(out=q[:], in_=q[:], func=AF.Sigmoid),
                 reads=[u["kx2"]], writes=[u["kx2"]])
        for t in range(4):
            u = U[t]
            S.op("dve", lambda e, ps_g=u["ps_g"], q=u["x2"]: e.tensor_tensor(out=q[:], in0=ps_g[:, 0:BW], in1=q[:], op=ALU.mult),
                 reads=[u["pk_g"], u["kx2"]], writes=[u["kx2"]])
            S.op("sp", lambda e, q=u["x2"], t=t: e.dma_start(out=gate_out[t][:, bcols(b)], in_=q[:]), reads=[u["kx2"]],
                 writes=[("og", t, b)], dma_sem=f"so{u['kx2'][1]}")
            extra_outs.append(("og", t, b))

    if mode == "pre":
        for m in range(2):
            wv, kv = w_acquire()
            wg_, kg_ = w_acquire()
            for b in range(NB):
                psv, pkv = proj(wv, kv, w8, a_rhs(b), KT, b)
                psg, pkg = proj(wg_, kg_, w8, a_rhs(b), KT, b)
                sg, ksg = tf.get()
                S.op("act", lambda e, psg=psg, sg=sg: e.activation(out=sg[:], in_=psg[:, 0:BW], func=AF.Sigmoid),
                     reads=[pkg], writes=[ksg])
                S.op("dve", lambda e, psv=psv, sg=sg, b=b, m=m: e.tensor_tensor(
                    out=STG[:, 4 + m, PADL + b * BW:PADL + (b + 1) * BW], in0=psv[:, 0:BW], in1=sg[:], op=ALU.mult),
                    reads=[pkv, ksg], writes=[("stg", 4 + m, b)])
            w_issue()
            w_issue()
        cg_w = [w_acquire() for t in range(4)]

        dg_i = [0]

        def conformer_block(b):
            cen_in = []
            for m in range(2):
                ps, pk = PS()
                for k in range(31):
                    S.op("pe", lambda e, m=m, k=k, ps=ps: e.matmul(ps[:, 0:BW], lhsT=dres[:, m * 31 + k, :],
                                                                  rhs=stg_cols(4 + m, b, 30 - k),
                                                                  start=(k == 0), stop=(k == 30)),
                         reads=["dres"] + stg_keys(4 + m, b), writes=[pk])
                c, kc = tf.get()
                S.op("act", lambda e, ps=ps, c=c, m=m: e.activation(out=c[:], in_=ps[:, 0:BW], func=AF.Identity,
                                                                    bias=C("dw_b", m)),
                     reads=[pk, "cst"], writes=[kc])
                cen_in.append((c, kc))
            cens = []
            for m in range(2):
                ps, pk = PS()
                for k in range(2):
                    c, kc = cen_in[k]
                    o = _CST["Cmat"] + k * 256 + m * 128
                    S.op("pe", lambda e, ps=ps, c=c, o=o, k=k: e.matmul(ps[:, 0:BW], lhsT=cst[:, o:o + 128], rhs=c[:],
                                                                       start=(k == 0), stop=(k == 1)),
                         reads=["cst", kc], writes=[pk])
                cens.append((ps, pk))
            psv, pkv = PS()
            for m in range(2):
                ps, pk = cens[m]
                sq, ksq = tf.get()
                S.op("act", lambda e, ps=ps, sq=sq: e.activation(out=sq[:], in_=ps[:, 0:BW], func=AF.Square),
                     reads=[pk], writes=[ksq])
                S.op("pe", lambda e, psv=psv, sq=sq, m=m: e.matmul(psv[:, 0:BW], lhsT=cst[:, _CST["Jm"]:_CST["Jm"] + 128],
                                                                  rhs=sq[:], start=(m == 0), stop=(m == 1)),
                     reads=["cst", ksq], writes=[pkv])
            l, kl = tf.get()
            S.op("act", lambda e, psv=psv, l=l: e.activation(out=l[:], in_=psv[:, 0:BW], func=AF.Ln, bias=EPS),
                 reads=[pkv], writes=[kl])
            rs, krs = tf.get()
            S.op("act", lambda e, l=l, rs=rs: e.activation(out=rs[:], in_=l[:], func=AF.Exp, scale=-0.5),
                 reads=[kl], writes=[krs])
            sls = []
            for m in range(2):
                ps, pk = cens[m]
                xn, kxn = tf.get()
                S.op("dve", lambda e, ps=ps, rs=rs, xn=xn: e.tensor_tensor(out=xn[:], in0=ps[:, 0:BW], in1=rs[:], op=ALU.mult),
                     reads=[pk, krs], writes=[kxn])
                sl, ksl = tb.get()
                S.op("act", lambda e, xn=xn, sl=sl, m=m: e.activation(out=sl[:], in_=xn[:], func=AF.Silu,
                                                                      scale=C("ln_g", m), bias=C("ln_b", m)),
                     reads=[kxn, "cst"], writes=[ksl])
                sls.append((sl, ksl))
            for m in range(2):
                ps, pk = PS()
                for k in range(2):
                    sl, ksl = sls[k]
                    o = _CBF["pw"] + k * 256 + m * 128
                    S.op("pe", lambda e, ps=ps, sl=sl, o=o, k=k: e.matmul(ps[:, 0:BW], lhsT=cbf[:, o:o + 128], rhs=sl[:],
                                                                         start=(k == 0), stop=(k == 1)),
                         reads=["cbf", ksl], writes=[pk])
                yt, kyt = tb.get()
                S.op("act", lambda e, ps=ps, yt=yt: e.activation(out=yt[:], in_=ps[:, 0:BW], func=AF.Copy),
                     reads=[pk], writes=[kyt])
                S.op("sp", lambda e, yt=yt, m=m: e.dma_start(out=ya_out[2 + m][:, bcols(b)], in_=yt[:]), reads=[kyt],
                     writes=[("oy", 2 + m, b)], dma_sem=f"sy{kyt[1]}")
                extra_outs.append(("oy", 2 + m, b))

        for b in range(NB):
            rg_group([0, 1, 2, 3], b, None)
            conformer_block(b)
            gate_block(b, cg_w)
        for t in range(4):
            w_issue()
        S.op("dve", lambda e: e.tensor_tensor(out=small[:, 52:56], in0=small[:, 40:44], in1=small[:, 0:4], op=ALU.mult),
             reads=[("rsum", t) for t in range(4)] + ["c1"], writes=["plog"])
        S.op("act", lambda e: e.activation(out=small[:, 52:56], in_=small[:, 52:56], func=AF.Exp),
             reads=["plog"], writes=["pval"])
        S.op("sp", lambda e: e.dma_start(out=out_d[:, 0:4], in_=small[:, 56:60]),
             reads=[("endst", t) for t in range(4)], writes=["o0"], dma_sem="st0")
        S.op("sp", lambda e: e.dma_start(out=out_d[:, 4:8], in_=small[:, 52:56]), reads=["pval"], writes=["o1"], dma_sem="st1")
        S.op("sp", lambda e: e.nop(), reads=["o0", "o1"] + extra_outs)
    else:
        for b in range(NB):
            rg_group([0, 1], b, None)
            rg_group([2, 3], b, None)

        for mt in range(KT):
            wt, wk = w_acquire()
            for b in range(NB):
                ps, pk = proj(wt, wk, w8, lambda k, b=b: (Y[:, k, bcols(b)], ("Y", k, b)), KT, b)
                S.op("dve", lambda e, ps=ps, mt=mt, b=b: e.tensor_tensor(out=h[:, mt, bcols(b)], in0=ps[:, 0:BW],
                                                                        in1=h[:, mt, bcols(b)], op=ALU.add),
                     reads=[pk, ("h", mt, b)], writes=[("h", mt, b)])
            w_issue()

        rms_to_A("g_mlp")
        for G in range(8):
            mo = (G % 2) * 4
            for j in range(4):
                wt, wk = w_acquire()
                for b in range(NB):
                    ps, pk = proj(wt, wk, w8, a_rhs(b), KT, b)
                    r, kr = tf.get()
                    S.op("act", lambda e, ps=ps, r=r: e.activation(out=r[:], in_=ps[:, 0:BW], func=AF.Relu),
                         reads=[pk], writes=[kr])
                    S.op("dve", lambda e, ps=ps, r=r, j=j, b=b, mo=mo: e.tensor_tensor(
                        out=Y[:, mo + j, bcols(b)], in0=ps[:, 0:BW], in1=r[:], op=ALU.mult),
                        reads=[pk, kr], writes=[("Y", mo + j, b)])
                w_issue()
            def down(wt, wk, q, b):
                for mm in range(2):
                    mt = 2 * q + mm
                    ps, pk = proj(wt, wk, lambda wt_, k, mm=mm: wt_[:, (k * 2 + mm) * 128:(k * 2 + mm + 1) * 128],
                                  lambda k, b=b, mo=mo: (Y[:, mo + k, bcols(b)], ("Y", mo + k, b)), 4, b)
                    S.op("dve", lambda e, ps=ps, mt=mt, b=b: e.tensor_tensor(out=h[:, mt, bcols(b)], in0=ps[:, 0:BW],
                                                                            in1=h[:, mt, bcols(b)], op=ALU.add),
                         reads=[pk, ("h", mt, b)], writes=[("h", mt, b)])

            if G < 7:
                for q in range(4):
                    wt, wk = w_acquire()
                    for b in range(NB):
                        down(wt, wk, q, b)
                    w_issue()
            else:
                wq = [w_acquire() for q in range(4)]
                for b in range(NB):
                    for q in range(4):
                        down(wq[q][0], wq[q][1], q, b)
                for q in range(4):
                    w_issue()

        finals = []
        if mode == "layer":
            out_p = out_d.rearrange("k p t -> p k t")
            for b in range(NB):
                S.op("sp", lambda e, b=b: e.dma_start(out=out_p[:, :, bcols(b)], in_=h[:, :, bcols(b)]),
                     reads=[("h", k, b) for k in range(KT)], writes=[("o", b)], dma_sem=f"st{b}")
                finals.append(("o", b))
        else:
            opool = tf
            for b in range(NB):
                cs = bcols(b)
                ps, pk = PS()
                for k in range(KT):
                    sq, sk = tb.get()
                    S.op("act", lambda e, k=k, cs=cs, sq=sq: e.activation(out=sq[:], in_=h[:, k, cs], func=AF.Square),
                         reads=[("h", k, b)], writes=[sk])
                    S.op("pe", lambda e, k=k, sq=sq, ps=ps: e.matmul(ps[:, 0:BW], lhsT=ones[:], rhs=sq[:],
                                                                     start=(k == 0), stop=(k == KT - 1)),
                         reads=["ones", sk], writes=[pk])
                l, lk = tf.get()
                S.op("act", lambda e, ps=ps, l=l: e.activation(out=l[:], in_=ps[:, 0:BW], func=AF.Ln, bias=EPS),
                     reads=[pk], writes=[lk])
                S.op("act", lambda e, ps=ps, l=l: e.activation(out=ps[:, 0:BW], in_=l[:], func=AF.Exp, scale=-0.5),
                     reads=[lk], writes=[pk])
                lo = PRE if b == 0 else 0
                for k in range(KT):
                    o, ok = opool.get()
                    S.op("dve", lambda e, k=k, cs=cs, ps=ps, o=o: e.scalar_tensor_tensor(
                        out=o[:], in0=h[:, k, cs], scalar=C("g_fin", k), in1=ps[:, 0:BW], op0=ALU.mult, op1=ALU.mult),
                        reads=[("h", k, b), pk, "cst"], writes=[ok])
                    d0 = b * BW + lo - PRE
                    S.op("sp", lambda e, k=k, o=o, lo=lo, d0=d0: e.dma_start(out=out_d[k][:, d0:d0 + BW - lo], in_=o[:, lo:BW]),
                         reads=[ok], writes=[("o", k, b)], dma_sem=f"sto{ok[1]}")
                    finals.append(("o", k, b))
        S.op("sp", lambda e: e.nop(), reads=finals)

    assert wstate["next_use"] == nblocks, (wstate, nblocks)
    with contextlib.ExitStack() as st:
        S.emit(nc, st)
    return nc


def _blk(w):
    return np.ascontiguousarray(w.reshape(8, 128, 128).transpose(1, 0, 2).reshape(128, 1024))


def _cx_blocks(w_in_l):
    return [_blk(w_in_l[:, 1280 + 128 * t:1280 + 128 * (t + 1)]) for t in range(4)]


def _pre_stream(w_in_l):
    col = lambda c0: _blk(w_in_l[:, c0:c0 + 128])
    blocks = [col(0), col(128)]
    blocks += _cx_blocks(w_in_l)
    blocks += [col(256), col(512), col(384), col(640)]
    blocks += [col(768 + 128 * t) for t in range(4)]
    return np.stack(blocks).astype(np.float32)


def _layer_stream(w_in_l, w_out_l, w_up_l, w_down_l):
    blocks = [_blk(w_out_l[:, 128 * m:128 * (m + 1)]) for m in range(8)]
    for G in range(8):
        for j in range(4):
            c0 = G * 512 + j * 128
            blocks.append(_blk(w_up_l[:, c0:c0 + 128]))
        for q in range(4):
            sub = w_down_l[G * 512:(G + 1) * 512, q * 256:(q + 1) * 256]
            blocks.append(np.ascontiguousarray(sub.reshape(4, 128, 2, 128).transpose(1, 0, 2, 3).reshape(128, 1024)))
    return np.stack(blocks).astype(np.float32)


def _pk(v, ntile):
    return np.ascontiguousarray(np.asarray(v, np.float32).reshape(ntile, 128).T)


def _consts(l, j, G, P):
    c = np.zeros((128, NCF), np.float32)

    def put(name, arr):
        arr = np.asarray(arr, np.float32).reshape(128, -1)
        c[:, _CST[name]:_CST[name] + arr.shape[1]] = arr

    put("g_mix", _pk(P["mix_norm_g"][l], 8))
    put("g_mlp", _pk(P["mlp_norm_g"][l], 8))
    put("g_fin", _pk(P["final_norm_g"], 8))
    put("pool_scale", _pk(P["pool_scale"][l], 2))
    wins = np.array([2, 4, 8, 16], np.float32)
    wpp = np.repeat(wins, 64)
    put("invw", _pk(1.0 / wpp, 2))
    put("invwm1", _pk(1.0 / wpp - 1.0, 2))
    put("himask", (np.arange(128) >= 64).astype(np.float32))
    put("dw_w", P["convb_dw_w"][l].T.reshape(2, 128, 31).transpose(1, 0, 2))
    put("dw_b", _pk(P["convb_dw_b"][l], 2))
    put("ln_g", _pk(P["convb_ln_g"][l], 2))
    put("ln_b", _pk(P["convb_ln_b"][l], 2))
    put("rgc_w", P["rg_conv_w"][l].T.reshape(4, 128, 4).transpose(1, 0, 2))
    put("rgc_b", _pk(P["rg_conv_b"][l], 4))
    put("b_a", _pk(P["rg_b_a"][l], 4))
    put("b_x", _pk(P["rg_b_x"][l], 4))
    put("lam", _pk(P["rg_lambda"][l], 4))
    sm = np.zeros((128, 32), np.float32)
    ratio = np.ones((128, 2, 32), np.float32)
    if j == 0:
        sm[:, 16:] = 1.0
        pos = np.arange(1, 17, dtype=np.float32)
        for t in range(2):
            wv = wpp[t * 128:(t + 1) * 128][:, None]
            ratio[:, t, 16:] = wv / np.minimum(pos[None, :], wv)
    put("scanmask", sm)
    put("ratio", ratio)
    pm = np.zeros((128, 8), np.float32)
    put("pm", pm)
    if G is not None:
        put("G", G)
    wg = np.zeros((2, 128, 128), np.float32)
    for g in range(4):
        t, o = divmod(g, 2)
        wg[t, o * 64:(o + 1) * 64, o * 64:(o + 1) * 64] = P["pool_w"][l][g]
    put("Wg", wg.transpose(1, 0, 2))
    cm = (np.eye(256, dtype=np.float32) - np.float32(1.0 / 256.0)).reshape(2, 128, 256)
    put("Cmat", cm.transpose(1, 0, 2))
    put("Jm", np.full((128, 128), 1.0 / 256.0, np.float32))
    return c


def _cbf(l, P):
    c = np.zeros((128, NCB), np.float32)
    c[:, 0:128] = np.eye(128, dtype=np.float32)
    c[:, 128:640] = P["convb_pw_w"][l].reshape(2, 128, 256).transpose(1, 0, 2).reshape(128, 512)
    for nm, key in (("Wa", "rg_w_a"), ("Wx", "rg_w_x")):
        bd = np.zeros((4, 128, 128), np.float32)
        for hd in range(8):
            t, o = divmod(hd, 2)
            bd[t, o * 64:(o + 1) * 64, o * 64:(o + 1) * 64] = P[key][l][hd]
        c[:, _CBF[nm]:_CBF[nm] + 512] = bd.transpose(1, 0, 2).reshape(128, 512)
    return c


_PROG = {}


def _prog(mode, nblocks):
    if (mode, nblocks) not in _PROG:
        _PROG[(mode, nblocks)] = build_program(mode, nblocks)
    return _PROG[(mode, nblocks)]


def kernel(**inputs):
    P = {k: np.asarray(v, np.float32) for k, v in inputs.items()}
    x = P["x"]
    B = x.shape[0]
    hT = []
    for r in range(NCORES):
        b, j = divmod(r, 4)
        seq = np.concatenate([P["meta_tokens"], x[b]], axis=0)
        s0 = 16 + MAIN * j
        if j == 0:
            tok = np.concatenate([np.zeros((16, D), np.float32), seq[0:16 + MAIN]], axis=0)
        else:
            tok = seq[s0 - PRE:s0 + MAIN]
        hT.append(np.ascontiguousarray(tok.T).reshape(KT, 128, NT))
    out = None
    for l in range(2):
        cbf = _cbf(l, P)
        wpre = _pre_stream(P["w_in"][l])
        ncA = _prog("pre", wpre.shape[0])
        mapsA = [{"hT": hT[r], "wstream": wpre, "cst": _consts(l, r % 4, None, P), "cbf": cbf} for r in range(NCORES)]
        resA = run_bass_kernel_spmd(ncA, mapsA, core_ids=list(range(NCORES)))
        G = np.stack([np.asarray(resA.results[r]["endp"], np.float32) for r in range(NCORES)], axis=1)
        mode = "layer" if l == 0 else "last"
        wst = _layer_stream(P["w_in"][l], P["w_out"][l], P["w_up"][l], P["w_down"][l])
        ncB = _prog(mode, wst.shape[0])
        mapsB = []
        for r in range(NCORES):
            b, j = divmod(r, 4)
            c = _consts(l, j, G.reshape(128, 64), P)
            pm = np.zeros((128, 8), np.float32)
            pm[:, 4 * b:4 * b + j] = 1.0
            c[:, _CST["pm"]:_CST["pm"] + 8] = pm
            mapsB.append({"hT": hT[r], "wstream": wst, "cst": c, "cbf": cbf,
                          "ya_in": np.asarray(resA.results[r]["ya_out"]), "arg_in": np.asarray(resA.results[r]["arg_out"]),
                          "brg_in": np.asarray(resA.results[r]["brg_out"]),
                          "gate_in": np.asarray(resA.results[r]["gate_out"])})
        resB = run_bass_kernel_spmd(ncB, mapsB, core_ids=list(range(NCORES)))
        if l == 0:
            hn = [np.asarray(resB.results[r]["hout"], np.float32) for r in range(NCORES)]
            hT = []
            for r in range(NCORES):
                b, j = divmod(r, 4)
                t = hn[r].copy()
                if j == 0:
                    t[:, :, 0:16] = 0.0
                else:
                    t[:, :, 0:PRE] = hn[r - 1][:, :, NT - PRE:NT]
                hT.append(t)
        else:
            out = np.zeros((B, 4 * MAIN, D), np.float32)
            for r in range(NCORES):
                b, j = divmod(r, 4)
                y = np.asarray(resB.results[r]["yout"], np.float32).reshape(D, MAIN)
                out[b, MAIN * j:MAIN * (j + 1), :] = y.T
    return out
```

```python
import contextlib
import numpy as np
import concourse.bass as bass
import concourse.mybir as mybir
from concourse.bass_utils import run_bass_kernel_spmd

F32 = mybir.dt.float32
BF16 = mybir.dt.bfloat16
AF = mybir.ActivationFunctionType
ALU = mybir.AluOpType

NCORES = 8
D = 1024
KT = 8
PRE = 32
MAIN = 2048
NT = PRE + MAIN
NB = 5
BW = NT // NB
PADL = 32
EPS = 1e-6
NRING = 6
GELU_K = 1.5957691216057308

_CST = {}
_off = 0
for _n, _w in [("g_mix", 8), ("g_mlp", 8), ("g_fin", 8), ("pool_scale", 2), ("invw", 2), ("invwm1", 2),
               ("himask", 1), ("dw_w", 62), ("dw_b", 2), ("ln_g", 2), ("ln_b", 2), ("rgc_w", 16),
               ("rgc_b", 4), ("b_a", 4), ("b_x", 4), ("lam", 4), ("scanmask", 32), ("ratio", 64),
               ("pm", 8), ("G", 64), ("Wg", 256), ("Cmat", 512), ("Jm", 128)]:
    _CST[_n] = _off
    _off += _w
NCF = _off
_CBF = {"ident": 0, "pw": 128, "Wa": 640, "Wx": 1152}
NCB = 1664

ENGS = ("pe", "act", "dve", "pool", "sp")


class Op:
    __slots__ = ("idx", "eng", "fn", "deps", "dma_sem", "count", "signal", "waits", "inc")

    def __init__(self, idx, eng, fn, dma_sem, inc):
        self.idx = idx
        self.eng = eng
        self.fn = fn
        self.deps = {}
        self.dma_sem = dma_sem
        self.count = None
        self.signal = False
        self.waits = []
        self.inc = inc


class Sched:
    def __init__(self):
        self.ops = []
        self.last_writer = {}
        self.readers = {}

    def op(self, eng, fn, reads=(), writes=(), dma_sem=None, inc=16):
        o = Op(len(self.ops), eng, fn, dma_sem, inc)
        for k in reads:
            w = self.last_writer.get(k)
            if w is not None:
                o.deps[w] = "raw"
            if isinstance(k, tuple) and k[0] == "ps":
                for r in self.readers.get(k, ()):
                    if self.ops[r].eng != eng and r not in o.deps:
                        o.deps[r] = "rar"
        for k in writes:
            w = self.last_writer.get(k)
            if w is not None and w not in o.deps:
                o.deps[w] = "waw"
            for r in self.readers.get(k, ()):
                if r not in o.deps:
                    o.deps[r] = "war"
        for k in reads:
            self.readers.setdefault(k, []).append(o.idx)
        for k in writes:
            self.last_writer[k] = o.idx
            self.readers[k] = []
        self.ops.append(o)
        return o

    def emit(self, nc, st):
        ops = self.ops
        for o in ops:
            for d, kind in o.deps.items():
                p = ops[d]
                if p.dma_sem is not None or o.dma_sem is not None or p.eng != o.eng:
                    need = True
                else:
                    need = (o.eng != "pe")
                if need:
                    o.waits.append(d)
                    p.signal = True
        cnt = {e: 0 for e in ENGS}
        dcnt = {}
        for o in ops:
            if o.dma_sem is not None:
                dcnt[o.dma_sem] = dcnt.get(o.dma_sem, 0) + o.inc
                o.count = dcnt[o.dma_sem]
            elif o.signal:
                cnt[o.eng] += 1
                o.count = cnt[o.eng]
        sems = {e: st.enter_context(nc.semaphore("s_" + e)) for e in ENGS}
        dsems = {k: st.enter_context(nc.semaphore("d_" + str(k))) for k in sorted(dcnt)}
        block = st.enter_context(nc.Block())
        queues = {e: [o for o in ops if o.eng == e] for e in ENGS}

        def run_queue(eng_name, engine):
            waited = {}
            for o in queues[eng_name]:
                need = {}
                for d in o.waits:
                    p = ops[d]
                    key = ("d", p.dma_sem) if p.dma_sem is not None else ("e", p.eng)
                    if p.count > need.get(key, 0):
                        need[key] = p.count
                for key, val in need.items():
                    if waited.get(key, 0) >= val:
                        continue
                    waited[key] = val
                    engine.wait_ge(dsems[key[1]] if key[0] == "d" else sems[key[1]], val)
                ins = o.fn(engine)
                if o.dma_sem is not None:
                    ins.then_inc(dsems[o.dma_sem], o.inc)
                elif o.signal:
                    ins.then_inc(sems[o.eng], 1)

        @block.tensor
        def _(e):
            run_queue("pe", e)

        @block.scalar
        def _(e):
            run_queue("act", e)

        @block.vector
        def _(e):
            run_queue("dve", e)

        @block.gpsimd
        def _(e):
            run_queue("pool", e)

        @block.sync
        def _(e):
            run_queue("sp", e)


class Pool:
    def __init__(self, nc, name, shape, dtype, n):
        self.t = [nc.alloc_sbuf_tensor(f"{name}{i}", shape, dtype) for i in range(n)]
        self.name = name
        self.i = 0

    def get(self):
        i = self.i % len(self.t)
        self.i += 1
        return self.t[i], (self.name, i)


def build_program(mode, nblocks):
    nc = bass.Bass("TRN2", target_bir_lowering=False)
    hT = nc.dram_tensor("hT", [KT, 128, NT], F32, kind="ExternalInput").ap()
    wstream = nc.dram_tensor("wstream", [nblocks, 128, 1024], F32, kind="ExternalInput").ap()
    cst_d = nc.dram_tensor("cst", [128, NCF], F32, kind="ExternalInput").ap()
    cbf_d = nc.dram_tensor("cbf", [128, NCB], F32, kind="ExternalInput").ap()
    if mode == "pre":
        out_d = nc.dram_tensor("endp", [128, 8], F32, kind="ExternalOutput").ap()
        gate_out = nc.dram_tensor("gate_out", [4, 128, NT], F32, kind="ExternalOutput").ap()
        ya_out = nc.dram_tensor("ya_out", [4, 128, NT], BF16, kind="ExternalOutput").ap()
        arg_out = nc.dram_tensor("arg_out", [4, 128, NT], F32, kind="ExternalOutput").ap()
        brg_out = nc.dram_tensor("brg_out", [4, 128, NT], F32, kind="ExternalOutput").ap()
    else:
        gate_in = nc.dram_tensor("gate_in", [4, 128, NT], F32, kind="ExternalInput").ap()
        ya_in = nc.dram_tensor("ya_in", [4, 128, NT], BF16, kind="ExternalInput").ap()
        arg_in = nc.dram_tensor("arg_in", [4, 128, NT], F32, kind="ExternalInput").ap()
        brg_in = nc.dram_tensor("brg_in", [4, 128, NT], F32, kind="ExternalInput").ap()
    if mode == "pre":
        pass
    elif mode == "layer":
        out_d = nc.dram_tensor("hout", [KT, 128, NT], F32, kind="ExternalOutput").ap()
    else:
        out_d = nc.dram_tensor("yout", [KT, 128, MAIN], F32, kind="ExternalOutput").ap()

    S = Sched()
    h = nc.alloc_sbuf_tensor("h", [128, KT, NT], F32)
    A = nc.alloc_sbuf_tensor("A", [128, KT, NT], BF16)
    Y = nc.alloc_sbuf_tensor("Y", [128, KT, NT], BF16) if mode != "pre" else None
    NSTG = 6 if mode == "pre" else 4
    STG = nc.alloc_sbuf_tensor("STG", [128, NSTG, PADL + NT], BF16)
    ring = [nc.alloc_sbuf_tensor(f"wr{i}", [128, 1024], BF16) for i in range(NRING)]
    cst = nc.alloc_sbuf_tensor("cst_s", [128, NCF], F32)
    cbf = nc.alloc_sbuf_tensor("cbf_s", [128, NCB], BF16)
    pmat = nc.alloc_sbuf_tensor("pmat", [128, 8, 128], BF16)
    rgd = nc.alloc_sbuf_tensor("rgd", [128, 16, 128], BF16)
    ones = nc.alloc_sbuf_tensor("ones", [128, 128], BF16)
    dres = nc.alloc_sbuf_tensor("dres", [128, 62, 128], BF16) if mode == "pre" else None
    small = nc.alloc_sbuf_tensor("small", [128, 64], F32)
    tf = Pool(nc, "tf", [128, BW], F32, 17 if mode == "pre" else 12)
    ths = Pool(nc, "ths", [128, BW], F32, 5)
    tb = Pool(nc, "tb", [128, BW], BF16, 8 if mode == "pre" else 6)
    ps_t = [nc.alloc_psum_tensor(f"ps{i}", [128, 512], F32) for i in range(8)]
    ps_i = [0]

    def run_gen(g):
        for _ in g:
            pass

    def PS():
        i = ps_i[0] % 8
        ps_i[0] += 1
        return ps_t[i], ("ps", i)

    def C(name, j=0, w=1):
        o = _CST[name] + j
        return cst[:, o:o + w]

    def bcols(b):
        return slice(b * BW, (b + 1) * BW)

    S.op("sp", lambda e: e.dma_start(out=cst[:], in_=cst_d), writes=["cst"], dma_sem="ldc")
    S.op("pool", lambda e: e.dma_start(out=cbf[:], in_=cbf_d), writes=["cbf"], dma_sem="ldb")
    hT_p = hT.rearrange("k p t -> p k t")
    for b in range(NB):
        S.op("sp", lambda e, b=b: e.dma_start(out=h[:, :, bcols(b)], in_=hT_p[:, :, bcols(b)]),
             writes=[("h", k, b) for k in range(KT)], dma_sem=f"ldh{b}")

    wstate = {"next_dma": 0, "next_use": 0}

    def w_issue():
        i = wstate["next_dma"]
        if i >= nblocks:
            return
        wstate["next_dma"] += 1
        s = i % NRING
        S.op("pool", lambda e, i=i, s=s: e.dma_start(out=ring[s][:], in_=wstream[i]),
             writes=[("w", s)], dma_sem=f"w{s}")

    for _ in range(NRING):
        w_issue()

    def w_acquire():
        i = wstate["next_use"]
        wstate["next_use"] += 1
        s = i % NRING
        return ring[s], ("w", s)

    S.op("dve", lambda e: e.memset(ones[:], 1.0 / 1024.0), writes=["ones"])
    for s4 in range(NSTG):
        S.op("dve", lambda e, s4=s4: e.memset(STG[:, s4, 0:PADL], 0.0), writes=[("stgpad", s4)])
    S.op("act", lambda e: e.activation(out=small[:, 12:16], in_=C("lam", 0, 4), func=AF.Exp, scale=-1.0),
         reads=["cst"], writes=["sm_t"])
    S.op("act", lambda e: e.activation(out=small[:, 16:20], in_=small[:, 12:16], func=AF.Ln, bias=1.0),
         reads=["sm_t"], writes=["sm_t2"])
    S.op("dve", lambda e: e.tensor_scalar(out=small[:, 0:4], in0=small[:, 16:20], scalar1=-8.0, scalar2=None,
                                          op0=ALU.mult), reads=["sm_t2"], writes=["c1"])
    S.op("dve", lambda e: e.tensor_scalar(out=small[:, 4:8], in0=small[:, 16:20], scalar1=-16.0, scalar2=None,
                                          op0=ALU.mult), reads=["sm_t2"], writes=["c2"])
    S.op("dve", lambda e: e.memset(small[:, 8:12], 0.0), writes=["carry"])
    if mode != "pre":
        for r in range(8):
            gE = C("G", r * 8, 4)
            gP = C("G", r * 8 + 4, 4)
            S.op("dve", lambda e, gP=gP: e.tensor_tensor(out=small[:, 20:24], in0=gP, in1=small[:, 8:12], op=ALU.mult),
                 reads=["cst", "carry"], writes=["cc_t"])
            S.op("dve", lambda e, gE=gE: e.tensor_tensor(out=small[:, 24:28], in0=small[:, 20:24], in1=gE, op=ALU.add),
                 reads=["cc_t", "cst"], writes=["cc_u"])
            S.op("dve", lambda e: e.tensor_tensor(out=small[:, 28:32], in0=small[:, 24:28], in1=small[:, 8:12],
                                                  op=ALU.subtract), reads=["cc_u", "carry"], writes=["cc_v"])
            S.op("dve", lambda e, r=r: e.scalar_tensor_tensor(out=small[:, 32:36], in0=small[:, 28:32],
                                                              scalar=C("pm", r), in1=small[:, 8:12],
                                                              op0=ALU.mult, op1=ALU.add),
                 reads=["cc_v", "carry", "cst"], writes=["cc_w"])
            S.op("dve", lambda e: e.tensor_copy(out=small[:, 8:12], in_=small[:, 32:36]), reads=["cc_w"], writes=["carry"])
    for t in range(4):
        for k in range(4):
            S.op("pool", lambda e, t=t, k=k: e.tensor_scalar(out=rgd[:, t * 4 + k, :], in0=cbf[:, 0:128],
                                                             scalar1=C("rgc_w", t * 4 + k), scalar2=1.0,
                                                             op0=ALU.mult, op1=ALU.mult),
                 reads=["cst", "cbf"], writes=[("rgd", t)])
    if mode == "pre":
        for j in range(62):
            S.op("pool", lambda e, j=j: e.tensor_scalar(out=dres[:, j, :], in0=cbf[:, 0:128], scalar1=C("dw_w", j), scalar2=1.0,
                                                        op0=ALU.mult, op1=ALU.mult),
                 reads=["cst", "cbf"], writes=["dres"])
        for t in range(2):
            wg = cst[:, _CST["Wg"] + t * 128:_CST["Wg"] + (t + 1) * 128]
            S.op("dve", lambda e, t=t, wg=wg: e.tensor_copy(out=pmat[:, t, :], in_=wg), reads=["cst"], writes=[("pm_", t)])
            S.op("dve", lambda e, t=t, wg=wg: e.tensor_scalar(out=pmat[:, 2 + t, :], in0=wg, scalar1=C("invwm1", t),
                                                              scalar2=None, op0=ALU.mult), reads=["cst"], writes=[("pm_", t)])
            S.op("dve", lambda e, t=t, wg=wg: e.tensor_scalar(out=pmat[:, 4 + t, :], in0=wg, scalar1=C("invw", t),
                                                              scalar2=None, op0=ALU.mult), reads=["cst"], writes=[("pm_", t)])
            S.op("dve", lambda e, t=t, wg=wg: e.tensor_scalar(out=pmat[:, 6 + t, :], in0=wg, scalar1=C("invw", t),
                                                              scalar2=C("himask"), op0=ALU.mult, op1=ALU.mult),
                 reads=["cst"], writes=[("pm_", t)])

    def rms_to_A(gname):
        for b in range(NB):
            cs = bcols(b)
            ps, pk = PS()
            for k in range(KT):
                sq, sk = tb.get()
                S.op("act", lambda e, k=k, cs=cs, sq=sq: e.activation(out=sq[:], in_=h[:, k, cs], func=AF.Square),
                     reads=[("h", k, b)], writes=[sk])
                S.op("pe", lambda e, k=k, sq=sq, ps=ps: e.matmul(ps[:, 0:BW], lhsT=ones[:], rhs=sq[:],
                                                                 start=(k == 0), stop=(k == KT - 1)),
                     reads=["ones", sk], writes=[pk])
            l, lk = tf.get()
            S.op("act", lambda e, ps=ps, l=l: e.activation(out=l[:], in_=ps[:, 0:BW], func=AF.Ln, bias=EPS),
                 reads=[pk], writes=[lk])
            S.op("act", lambda e, ps=ps, l=l: e.activation(out=ps[:, 0:BW], in_=l[:], func=AF.Exp, scale=-0.5),
                 reads=[lk], writes=[pk])
            for k in range(KT):
                S.op("dve", lambda e, k=k, cs=cs, ps=ps: e.scalar_tensor_tensor(
                    out=A[:, k, cs], in0=h[:, k, cs], scalar=C(gname, k), in1=ps[:, 0:BW],
                    op0=ALU.mult, op1=ALU.mult),
                    reads=[("h", k, b), pk, "cst"], writes=[("A", k, b)])

    def proj(wt, wk, wsel, rhs_fn, nk, b):
        ps, pk = PS()
        for k in range(nk):
            rap, rkey = rhs_fn(k)
            S.op("pe", lambda e, k=k, rap=rap, ps=ps: e.matmul(ps[:, 0:BW], lhsT=wsel(wt, k), rhs=rap,
                                                               start=(k == 0), stop=(k == nk - 1)),
                 reads=[wk, rkey], writes=[pk])
        return ps, pk

    def w8(wt, k):
        return wt[:, k * 128:(k + 1) * 128]

    def a_rhs(b):
        return lambda k: (A[:, k, bcols(b)], ("A", k, b))

    def stg_cols(slot, b, shift):
        c0 = PADL + b * BW - shift
        return STG[:, slot, c0:c0 + BW]

    def stg_keys(slot, b):
        ks = [("stg", slot, b)]
        ks.append(("stg", slot, b - 1) if b > 0 else ("stgpad", slot))
        return ks

    def inproj_to_stg(slot):
        wt, wk = w_acquire()
        for b in range(NB):
            ps, pk = proj(wt, wk, w8, a_rhs(b), KT, b)
            S.op("act", lambda e, ps=ps, b=b: e.activation(out=STG[:, slot, PADL + b * BW:PADL + (b + 1) * BW],
                                                           in_=ps[:, 0:BW], func=AF.Copy),
                 reads=[pk], writes=[("stg", slot, b)])
        w_issue()

    extra_outs = []
    if mode == "pre":
        rms_to_A("g_mix")
    else:
        for t in range(4):
            S.op("sp", lambda e, t=t: e.dma_start(out=Y[:, t, :], in_=ya_in[t]),
                 writes=[("Y", t, b) for b in range(NB)], dma_sem=f"ldy{t}")

    if mode == "pre":
        inproj_to_stg(0)
        inproj_to_stg(1)
        def pool_tile(t):
            wlo, whi = (2, 4) if t == 0 else (8, 16)
            for b in range(NB):
                ps, pk = PS()
                for k in range(whi):
                    mat = pmat[:, 2 + t, :] if k == 0 else (pmat[:, 4 + t, :] if k < wlo else pmat[:, 6 + t, :])
                    S.op("pe", lambda e, k=k, mat=mat, ps=ps, b=b: e.matmul(ps[:, 0:BW], lhsT=mat, rhs=stg_cols(t, b, k),
                                                                           start=(k == 0), stop=(k == whi - 1)),
                         reads=[("pm_", t)] + stg_keys(t, b), writes=[pk])
                yt, kyt = tb.get()
                S.op("act", lambda e, ps=ps, yt=yt: e.activation(out=yt[:], in_=ps[:, 0:BW],
                                                                 func=AF.Identity, scale=C("pool_scale", t)),
                     reads=[pk, "cst"], writes=[kyt])
                if b == 0:
                    y0 = (yt, kyt)
                else:
                    S.op("sp", lambda e, yt=yt, b=b: e.dma_start(out=ya_out[t][:, bcols(b)], in_=yt[:]), reads=[kyt],
                         writes=[("oy", t, b)], dma_sem=f"sy{kyt[1]}")
                    extra_outs.append(("oy", t, b))
            psS, pkS = PS()
            for k in range(whi):
                mat = pmat[:, 4 + t, :] if k < wlo else pmat[:, 6 + t, :]
                S.op("pe", lambda e, k=k, mat=mat, psS=psS: e.matmul(psS[:, 0:PRE], lhsT=mat,
                                                                    rhs=STG[:, t, PADL - k:PADL - k + PRE],
                                                                    start=(k == 0), stop=(k == whi - 1)),
                     reads=[("pm_", t)] + stg_keys(t, 0), writes=[pkS])
            psX, pkX = PS()
            S.op("pe", lambda e, psX=psX: e.matmul(psX[:, 0:PRE], lhsT=pmat[:, t, :], rhs=STG[:, t, PADL:PADL + PRE],
                                                   start=True, stop=True),
                 reads=[("pm_", t)] + stg_keys(t, 0), writes=[pkX])
            t1, k1 = tf.get()
            S.op("dve", lambda e, psS=psS, t1=t1: e.tensor_tensor(out=t1[:, 0:PRE], in0=psS[:, 0:PRE],
                                                                  in1=C("ratio", t * 32, 32), op=ALU.mult),
                 reads=[pkS, "cst"], writes=[k1])
            t2, k2 = tf.get()
            S.op("dve", lambda e, psX=psX, t1=t1, t2=t2: e.tensor_tensor(out=t2[:, 0:PRE], in0=t1[:, 0:PRE],
                                                                         in1=psX[:, 0:PRE], op=ALU.subtract),
                 reads=[pkX, k1], writes=[k2])
            yt, kyt = y0
            S.op("act", lambda e, t2=t2, yt=yt: e.activation(out=yt[:, 0:PRE], in_=t2[:, 0:PRE], func=AF.Identity,
                                                             scale=C("pool_scale", t)),
                 reads=[k2, "cst"], writes=[kyt])
            S.op("sp", lambda e, yt=yt: e.dma_start(out=ya_out[t][:, bcols(0)], in_=yt[:]), reads=[kyt],
                 writes=[("oy", t, 0)], dma_sem=f"sy{kyt[1]}")
            extra_outs.append(("oy", t, 0))

        pool_tile(0)
        pool_tile(1)

    if mode == "pre":
        S.op("dve", lambda e: e.memset(small[:, 40:44], 0.0), writes=[("rsum", t) for t in range(4)])
    rg_slots = [2, 3, 2, 3] if mode != "pre" else [0, 1, 2, 3]
    if mode == "pre":
        for t in range(4):
            inproj_to_stg(rg_slots[t])
    else:
        pass

    rg_prev = {}

    def rg_group(tiles, b, cgws):
        U = {t: {} for t in tiles}
        if mode != "pre":
            for t in tiles:
                a, ka = tf.get()
                bb, kb = tf.get()
                S.op("sp", lambda e, a=a, t=t: e.dma_start(out=a[:], in_=arg_in[t][:, bcols(b)]), writes=[ka],
                     dma_sem=f"so{ka[1]}")
                S.op("sp", lambda e, bb=bb, t=t: e.dma_start(out=bb[:], in_=brg_in[t][:, bcols(b)]), writes=[kb],
                     dma_sem=f"so{kb[1]}")
                U[t].update(a=a, ka=ka, xc=bb, kxc=kb)
        else:
            for t in tiles:
                slot = rg_slots[t]
                ps_c, pk_c = PS()
                for k in range(4):
                    S.op("pe", lambda e, k=k, ps_c=ps_c, t=t, slot=slot: e.matmul(
                        ps_c[:, 0:BW], lhsT=rgd[:, t * 4 + k, :], rhs=stg_cols(slot, b, 3 - k),
                        start=(k == 0), stop=(k == 3)),
                        reads=[("rgd", t)] + stg_keys(slot, b), writes=[pk_c])
                U[t].update(ps_c=ps_c, pk_c=pk_c)
            yield
            for t in tiles:
                u = U[t]
                xc, kxc = tf.get()
                xcb, kxcb = tb.get()
                S.op("dve", lambda e, ps_c=u["ps_c"], xc=xc, t=t: e.tensor_scalar(out=xc[:], in0=ps_c[:, 0:BW], scalar1=C("rgc_b", t),
                                                                                 scalar2=None, op0=ALU.add),
                     reads=[u["pk_c"], "cst"], writes=[kxc])
                S.op("dve", lambda e, xc=xc, xcb=xcb: e.tensor_copy(out=xcb[:], in_=xc[:]), reads=[kxc], writes=[kxcb])
                u.update(xc=xc, kxc=kxc, xcb=xcb, kxcb=kxcb)
            for t in tiles:
                u = U[t]
                ps_a, pk_a = PS()
                S.op("pe", lambda e, ps_a=ps_a, xcb=u["xcb"], t=t: e.matmul(
                    ps_a[:, 0:BW], lhsT=cbf[:, _CBF["Wa"] + t * 128:_CBF["Wa"] + (t + 1) * 128], rhs=xcb[:], start=True, stop=True),
                    reads=["cbf", u["kxcb"]], writes=[pk_a])
                ps_x, pk_x = PS()
                S.op("pe", lambda e, ps_x=ps_x, xcb=u["xcb"], t=t: e.matmul(
                    ps_x[:, 0:BW], lhsT=cbf[:, _CBF["Wx"] + t * 128:_CBF["Wx"] + (t + 1) * 128], rhs=xcb[:], start=True, stop=True),
                    reads=["cbf", u["kxcb"]], writes=[pk_x])
                u.update(ps_a=ps_a, pk_a=pk_a, ps_x=ps_x, pk_x=pk_x)
            for t in tiles:
                u = U[t]
                r, kr = tf.get()
                if mode == "pre":
                    lo = PRE if b == 0 else 0
                    if b == 0:
                        S.op("act", lambda e, ps_a=u["ps_a"], r=r, t=t: e.activation(out=r[:, 0:PRE], in_=ps_a[:, 0:PRE],
                                                                                    func=AF.Sigmoid, bias=C("b_a", t)),
                             reads=[u["pk_a"], "cst"], writes=[kr])
                    S.op("act", lambda e, ps_a=u["ps_a"], r=r, lo=lo, t=t: e.activation(
                        out=r[:, lo:BW], in_=ps_a[:, lo:BW], func=AF.Sigmoid, bias=C("b_a", t),
                        accum_out=small[:, 44 + t:45 + t]),
                        reads=[u["pk_a"], "cst"], writes=[kr, ("racc", t)])
                    S.op("dve", lambda e, t=t: e.tensor_tensor(out=small[:, 40 + t:41 + t], in0=small[:, 40 + t:41 + t],
                                                               in1=small[:, 44 + t:45 + t], op=ALU.add),
                         reads=[("racc", t), ("rsum", t)], writes=[("rsum", t)])
                else:
                    S.op("act", lambda e, ps_a=u["ps_a"], r=r, t=t: e.activation(out=r[:], in_=ps_a[:, 0:BW], func=AF.Sigmoid,
                                                                                bias=C("b_a", t)),
                         reads=[u["pk_a"], "cst"], writes=[kr])
                u.update(r=r, kr=kr)
            for t in tiles:
                u = U[t]
                a, ka = tf.get()
                S.op("act", lambda e, r=u["r"], a=a, t=t: e.activation(out=a[:], in_=r[:], func=AF.Exp, scale=small[:, t:t + 1]),
                     reads=[u["kr"], "c1"], writes=[ka])
                u.update(a=a, ka=ka)
            for t in tiles:
                u = U[t]
                S.op("dve", lambda e, r=u["r"], a=u["a"]: e.tensor_tensor(out=r[:], in0=a[:], in1=a[:], op=ALU.mult),
                     reads=[u["ka"], u["kr"]], writes=[u["kr"]])
            for t in tiles:
                u = U[t]
                S.op("act", lambda e, ps_a=u["ps_a"], ps_x=u["ps_x"], t=t: e.activation(
                    out=ps_a[:, 0:BW], in_=ps_x[:, 0:BW], func=AF.Sigmoid, bias=C("b_x", t)),
                    reads=[u["pk_x"], "cst"], writes=[u["pk_a"]])
            for t in tiles:
                u = U[t]
                S.op("act", lambda e, r=u["r"], ps_x=u["ps_x"]: e.activation(out=ps_x[:, 0:BW], in_=r[:], func=AF.Sqrt,
                                                                             scale=-1.0, bias=1.0),
                     reads=[u["kr"]], writes=[u["pk_x"]])
            for t in tiles:
                u = U[t]
                S.op("dve", lambda e, ps_a=u["ps_a"], xc=u["xc"]: e.tensor_tensor(out=xc[:], in0=ps_a[:, 0:BW], in1=xc[:], op=ALU.mult),
                     reads=[u["pk_a"], u["kxc"]], writes=[u["kxc"]])
            for t in tiles:
                u = U[t]
                S.op("dve", lambda e, ps_x=u["ps_x"], xc=u["xc"]: e.tensor_tensor(out=xc[:], in0=ps_x[:, 0:BW], in1=xc[:], op=ALU.mult),
                     reads=[u["pk_x"], u["kxc"]], writes=[u["kxc"]])
        if mode == "pre":
            if b == 0:
                for t in tiles:
                    u = U[t]
                    S.op("dve", lambda e, bb=u["xc"]: e.tensor_tensor(out=bb[:, 0:PRE], in0=bb[:, 0:PRE], in1=C("scanmask", 0, 32),
                                                                      op=ALU.mult), reads=[u["kxc"], "cst"], writes=[u["kxc"]])
            for t in tiles:
                u = U[t]
                S.op("sp", lambda e, a=u["a"], t=t: e.dma_start(out=arg_out[t][:, bcols(b)], in_=a[:]), reads=[u["ka"]],
                     writes=[("oar", t, b)], dma_sem=f"so{u['ka'][1]}")
                S.op("sp", lambda e, bb=u["xc"], t=t: e.dma_start(out=brg_out[t][:, bcols(b)], in_=bb[:]), reads=[u["kxc"]],
                     writes=[("obr", t, b)], dma_sem=f"so{u['kxc'][1]}")
                extra_outs.append(("oar", t, b))
                extra_outs.append(("obr", t, b))
        if b == 0:
            if mode != "pre":
                for t in tiles:
                    u = U[t]
                    S.op("dve", lambda e, bb=u["xc"], a=u["a"], t=t: e.scalar_tensor_tensor(
                        out=bb[:, PRE:PRE + 1], in0=a[:, PRE:PRE + 1], scalar=small[:, 8 + t:9 + t],
                        in1=bb[:, PRE:PRE + 1], op0=ALU.mult, op1=ALU.add),
                        reads=[u["kxc"], u["ka"], "carry"], writes=[u["kxc"]])
        for t in tiles:
            u = U[t]
            hs, khs = ths.get()
            if t not in rg_prev:
                S.op("dve", lambda e, hs=hs, a=u["a"], bb=u["xc"]: e.tensor_tensor_scan(
                    out=hs[:], data0=a[:], data1=bb[:], initial=0.0, op0=ALU.mult, op1=ALU.add),
                    reads=[u["ka"], u["kxc"]], writes=[khs])
            else:
                ph, pkh = rg_prev[t]
                S.op("dve", lambda e, hs=hs, a=u["a"], bb=u["xc"], ph=ph: e.tensor_tensor_scan(
                    out=hs[:], data0=a[:], data1=bb[:], initial=ph[:, BW - 1:BW], op0=ALU.mult, op1=ALU.add),
                    reads=[u["ka"], u["kxc"], pkh], writes=[khs])
            rg_prev[t] = (hs, khs)
            u.update(hs=hs, khs=khs)
        if mode == "pre":
            if b == NB - 1:
                for t in tiles:
                    S.op("dve", lambda e, hs=U[t]["hs"], t=t: e.tensor_copy(out=small[:, 56 + t:57 + t], in_=hs[:, BW - 1:BW]),
                         reads=[U[t]["khs"]], writes=[("endst", t)])
            return
        for t in tiles:
            u = U[t]
            g, kg = tf.get()
            S.op("sp", lambda e, g=g, t=t: e.dma_start(out=g[:], in_=gate_in[t][:, bcols(b)]), writes=[kg], dma_sem=f"so{kg[1]}")
            S.op("dve", lambda e, g=g, hs=u["hs"], t=t: e.tensor_tensor(out=Y[:, 4 + t, bcols(b)], in0=g[:], in1=hs[:], op=ALU.mult),
                 reads=[kg, u["khs"]], writes=[("Y", 4 + t, b)])

    def gate_block(b, cgws):
        U = {t: {} for t in range(4)}
        for t in range(4):
            wt, wk = cgws[t]
            ps_g, pk_g = proj(wt, wk, w8, a_rhs(b), KT, b)
            U[t].update(ps_g=ps_g, pk_g=pk_g)
        yield
        for t in range(4):
            u = U[t]
            x2, kx2 = tf.get()
            S.op("act", lambda e, ps_g=u["ps_g"], x2=x2: e.activation(out=x2[:], in_=ps_g[:, 0:BW], func=AF.Square),
                 reads=[u["pk_g"]], writes=[kx2])
            u.update(x2=x2, kx2=kx2)
        for t in range(4):
            u = U[t]
            S.op("dve", lambda e, x2=u["x2"]: e.tensor_scalar(out=x2[:], in0=x2[:], scalar1=0.044715 * GELU_K, scalar2=GELU_K,
                                                              op0=ALU.mult, op1=ALU.add), reads=[u["kx2"]], writes=[u["kx2"]])
        for t in range(4):
            u = U[t]
            S.op("dve", lambda e, ps_g=u["ps_g"], x2=u["x2"]: e.tensor_tensor(out=x2[:], in0=ps_g[:, 0:BW], in1=x2[:], op=ALU.mult),
                 reads=[u["pk_g"], u["kx2"]], writes=[u["kx2"]])
        for t in range(4):
            u = U[t]
            S.op("act", lambda e, q=u["x2"]: e.activation(out=q[:], in_=q[:], func=AF.Sigmoid),
                 reads=[u["kx2"]], writes=[u["kx2"]])
        for t in range(4):
            u = U[t]
            S.op("dve", lambda e, ps_g=u["ps_g"], q=u["x2"]: e.tensor_tensor(out=q[:], in0=ps_g[:, 0:BW], in1=q[:], op=ALU.mult),
                 reads=[u["pk_g"], u["kx2"]], writes=[u["kx2"]])
            S.op("sp", lambda e, q=u["x2"], t=t: e.dma_start(out=gate_out[t][:, bcols(b)], in_=q[:]), reads=[u["kx2"]],
                 writes=[("og", t, b)], dma_sem=f"so{u['kx2'][1]}")
            extra_outs.append(("og", t, b))

    if mode == "pre":
        for m in range(2):
            wv, kv = w_acquire()
            wg_, kg_ = w_acquire()
            for b in range(NB):
                psv, pkv = proj(wv, kv, w8, a_rhs(b), KT, b)
                psg, pkg = proj(wg_, kg_, w8, a_rhs(b), KT, b)
                sg, ksg = tf.get()
                S.op("act", lambda e, psg=psg, sg=sg: e.activation(out=sg[:], in_=psg[:, 0:BW], func=AF.Sigmoid),
                     reads=[pkg], writes=[ksg])
                S.op("dve", lambda e, psv=psv, sg=sg, b=b, m=m: e.tensor_tensor(
                    out=STG[:, 4 + m, PADL + b * BW:PADL + (b + 1) * BW], in0=psv[:, 0:BW], in1=sg[:], op=ALU.mult),
                    reads=[pkv, ksg], writes=[("stg", 4 + m, b)])
            w_issue()
            w_issue()
        cg_w = [w_acquire() for t in range(4)]

        dg_i = [0]

        def conformer_block(b):
            cen_in = []
            for m in range(2):
                ps, pk = PS()
                for k in range(31):
                    S.op("pe", lambda e, m=m, k=k, ps=ps: e.matmul(ps[:, 0:BW], lhsT=dres[:, m * 31 + k, :],
                                                                  rhs=stg_cols(4 + m, b, 30 - k),
                                                                  start=(k == 0), stop=(k == 30)),
                         reads=["dres"] + stg_keys(4 + m, b), writes=[pk])
                c, kc = tf.get()
                S.op("act", lambda e, ps=ps, c=c, m=m: e.activation(out=c[:], in_=ps[:, 0:BW], func=AF.Identity,
                                                                    bias=C("dw_b", m)),
                     reads=[pk, "cst"], writes=[kc])
                cen_in.append((c, kc))
            yield
            cens = []
            for m in range(2):
                ps, pk = PS()
                for k in range(2):
                    c, kc = cen_in[k]
                    o = _CST["Cmat"] + k * 256 + m * 128
                    S.op("pe", lambda e, ps=ps, c=c, o=o, k=k: e.matmul(ps[:, 0:BW], lhsT=cst[:, o:o + 128], rhs=c[:],
                                                                       start=(k == 0), stop=(k == 1)),
                         reads=["cst", kc], writes=[pk])
                cens.append((ps, pk))
            psv, pkv = PS()
            for m in range(2):
                ps, pk = cens[m]
                sq, ksq = tf.get()
                S.op("act", lambda e, ps=ps, sq=sq: e.activation(out=sq[:], in_=ps[:, 0:BW], func=AF.Square),
                     reads=[pk], writes=[ksq])
                S.op("pe", lambda e, psv=psv, sq=sq, m=m: e.matmul(psv[:, 0:BW], lhsT=cst[:, _CST["Jm"]:_CST["Jm"] + 128],
                                                                  rhs=sq[:], start=(m == 0), stop=(m == 1)),
                     reads=["cst", ksq], writes=[pkv])
            yield
            l, kl = tf.get()
            S.op("act", lambda e, psv=psv, l=l: e.activation(out=l[:], in_=psv[:, 0:BW], func=AF.Ln, bias=EPS),
                 reads=[pkv], writes=[kl])
            rs, krs = tf.get()
            S.op("act", lambda e, l=l, rs=rs: e.activation(out=rs[:], in_=l[:], func=AF.Exp, scale=-0.5),
                 reads=[kl], writes=[krs])
            sls = []
            for m in range(2):
                ps, pk = cens[m]
                xn, kxn = tf.get()
                S.op("dve", lambda e, ps=ps, rs=rs, xn=xn: e.tensor_tensor(out=xn[:], in0=ps[:, 0:BW], in1=rs[:], op=ALU.mult),
                     reads=[pk, krs], writes=[kxn])
                sl, ksl = tb.get()
                S.op("act", lambda e, xn=xn, sl=sl, m=m: e.activation(out=sl[:], in_=xn[:], func=AF.Silu,
                                                                      scale=C("ln_g", m), bias=C("ln_b", m)),
                     reads=[kxn, "cst"], writes=[ksl])
                sls.append((sl, ksl))
            for m in range(2):
                ps, pk = PS()
                for k in range(2):
                    sl, ksl = sls[k]
                    o = _CBF["pw"] + k * 256 + m * 128
                    S.op("pe", lambda e, ps=ps, sl=sl, o=o, k=k: e.matmul(ps[:, 0:BW], lhsT=cbf[:, o:o + 128], rhs=sl[:],
                                                                         start=(k == 0), stop=(k == 1)),
                         reads=["cbf", ksl], writes=[pk])
                yt, kyt = tb.get()
                S.op("act", lambda e, ps=ps, yt=yt: e.activation(out=yt[:], in_=ps[:, 0:BW], func=AF.Copy),
                     reads=[pk], writes=[kyt])
                S.op("sp", lambda e, yt=yt, m=m: e.dma_start(out=ya_out[2 + m][:, bcols(b)], in_=yt[:]), reads=[kyt],
                     writes=[("oy", 2 + m, b)], dma_sem=f"sy{kyt[1]}")
                extra_outs.append(("oy", 2 + m, b))

        for b in range(NB):
            g_rg = rg_group([0, 1, 2, 3], b, None)
            next(g_rg)
            g_cf = conformer_block(b)
            next(g_cf)
            run_gen(g_rg)
            next(g_cf)
            g_gt = gate_block(b, cg_w)
            next(g_gt)
            run_gen(g_cf)
            run_gen(g_gt)
        for t in range(4):
            w_issue()
        S.op("dve", lambda e: e.tensor_tensor(out=small[:, 52:56], in0=small[:, 40:44], in1=small[:, 0:4], op=ALU.mult),
             reads=[("rsum", t) for t in range(4)] + ["c1"], writes=["plog"])
        S.op("act", lambda e: e.activation(out=small[:, 52:56], in_=small[:, 52:56], func=AF.Exp),
             reads=["plog"], writes=["pval"])
        S.op("sp", lambda e: e.dma_start(out=out_d[:, 0:4], in_=small[:, 56:60]),
             reads=[("endst", t) for t in range(4)], writes=["o0"], dma_sem="st0")
        S.op("sp", lambda e: e.dma_start(out=out_d[:, 4:8], in_=small[:, 52:56]), reads=["pval"], writes=["o1"], dma_sem="st1")
        S.op("sp", lambda e: e.nop(), reads=["o0", "o1"] + extra_outs)
    else:
        for b in range(NB):
            run_gen(rg_group([0, 1], b, None))
            run_gen(rg_group([2, 3], b, None))

        for mt in range(KT):
            wt, wk = w_acquire()
            for b in range(NB):
                ps, pk = proj(wt, wk, w8, lambda k, b=b: (Y[:, k, bcols(b)], ("Y", k, b)), KT, b)
                S.op("dve", lambda e, ps=ps, mt=mt, b=b: e.tensor_tensor(out=h[:, mt, bcols(b)], in0=ps[:, 0:BW],
                                                                        in1=h[:, mt, bcols(b)], op=ALU.add),
                     reads=[pk, ("h", mt, b)], writes=[("h", mt, b)])
            w_issue()

        rms_to_A("g_mlp")
        for G in range(8):
            mo = (G % 2) * 4
            for j in range(4):
                wt, wk = w_acquire()
                for b in range(NB):
                    ps, pk = proj(wt, wk, w8, a_rhs(b), KT, b)
                    r, kr = tf.get()
                    S.op("act", lambda e, ps=ps, r=r: e.activation(out=r[:], in_=ps[:, 0:BW], func=AF.Relu),
                         reads=[pk], writes=[kr])
                    S.op("dve", lambda e, ps=ps, r=r, j=j, b=b, mo=mo: e.tensor_tensor(
                        out=Y[:, mo + j, bcols(b)], in0=ps[:, 0:BW], in1=r[:], op=ALU.mult),
                        reads=[pk, kr], writes=[("Y", mo + j, b)])
                w_issue()
            def down(wt, wk, q, b):
                for mm in range(2):
                    mt = 2 * q + mm
                    ps, pk = proj(wt, wk, lambda wt_, k, mm=mm: wt_[:, (k * 2 + mm) * 128:(k * 2 + mm + 1) * 128],
                                  lambda k, b=b, mo=mo: (Y[:, mo + k, bcols(b)], ("Y", mo + k, b)), 4, b)
                    S.op("dve", lambda e, ps=ps, mt=mt, b=b: e.tensor_tensor(out=h[:, mt, bcols(b)], in0=ps[:, 0:BW],
                                                                            in1=h[:, mt, bcols(b)], op=ALU.add),
                         reads=[pk, ("h", mt, b)], writes=[("h", mt, b)])

            if G < 7:
                for q in range(4):
                    wt, wk = w_acquire()
                    for b in range(NB):
                        down(wt, wk, q, b)
                    w_issue()
            else:
                wq = [w_acquire() for q in range(4)]
                for b in range(NB):
                    for q in range(4):
                        down(wq[q][0], wq[q][1], q, b)
                for q in range(4):
                    w_issue()

        finals = []
        if mode == "layer":
            out_p = out_d.rearrange("k p t -> p k t")
            for b in range(NB):
                S.op("sp", lambda e, b=b: e.dma_start(out=out_p[:, :, bcols(b)], in_=h[:, :, bcols(b)]),
                     reads=[("h", k, b) for k in range(KT)], writes=[("o", b)], dma_sem=f"st{b}")
                finals.append(("o", b))
        else:
            opool = tf
            for b in range(NB):
                cs = bcols(b)
                ps, pk = PS()
                for k in range(KT):
                    sq, sk = tb.get()
                    S.op("act", lambda e, k=k, cs=cs, sq=sq: e.activation(out=sq[:], in_=h[:, k, cs], func=AF.Square),
                         reads=[("h", k, b)], writes=[sk])
                    S.op("pe", lambda e, k=k, sq=sq, ps=ps: e.matmul(ps[:, 0:BW], lhsT=ones[:], rhs=sq[:],
                                                                     start=(k == 0), stop=(k == KT - 1)),
                         reads=["ones", sk], writes=[pk])
                l, lk = tf.get()
                S.op("act", lambda e, ps=ps, l=l: e.activation(out=l[:], in_=ps[:, 0:BW], func=AF.Ln, bias=EPS),
                     reads=[pk], writes=[lk])
                S.op("act", lambda e, ps=ps, l=l: e.activation(out=ps[:, 0:BW], in_=l[:], func=AF.Exp, scale=-0.5),
                     reads=[lk], writes=[pk])
                lo = PRE if b == 0 else 0
                for k in range(KT):
                    o, ok = opool.get()
                    S.op("dve", lambda e, k=k, cs=cs, ps=ps, o=o: e.scalar_tensor_tensor(
                        out=o[:], in0=h[:, k, cs], scalar=C("g_fin", k), in1=ps[:, 0:BW], op0=ALU.mult, op1=ALU.mult),
                        reads=[("h", k, b), pk, "cst"], writes=[ok])
                    d0 = b * BW + lo - PRE
                    S.op("sp", lambda e, k=k, o=o, lo=lo, d0=d0: e.dma_start(out=out_d[k][:, d0:d0 + BW - lo], in_=o[:, lo:BW]),
                         reads=[ok], writes=[("o", k, b)], dma_sem=f"sto{ok[1]}")
                    finals.append(("o", k, b))
        S.op("sp", lambda e: e.nop(), reads=finals)

    assert wstate["next_use"] == nblocks, (wstate, nblocks)
    with contextlib.ExitStack() as st:
        S.emit(nc, st)
    return nc


def _blk(w):
    return np.ascontiguousarray(w.reshape(8, 128, 128).transpose(1, 0, 2).reshape(128, 1024))


def _cx_blocks(w_in_l):
    return [_blk(w_in_l[:, 1280 + 128 * t:1280 + 128 * (t + 1)]) for t in range(4)]


def _pre_stream(w_in_l):
    col = lambda c0: _blk(w_in_l[:, c0:c0 + 128])
    blocks = [col(0), col(128)]
    blocks += _cx_blocks(w_in_l)
    blocks += [col(256), col(512), col(384), col(640)]
    blocks += [col(768 + 128 * t) for t in range(4)]
    return np.stack(blocks).astype(np.float32)


def _layer_stream(w_in_l, w_out_l, w_up_l, w_down_l):
    blocks = [_blk(w_out_l[:, 128 * m:128 * (m + 1)]) for m in range(8)]
    for G in range(8):
        for j in range(4):
            c0 = G * 512 + j * 128
            blocks.append(_blk(w_up_l[:, c0:c0 + 128]))
        for q in range(4):
            sub = w_down_l[G * 512:(G + 1) * 512, q * 256:(q + 1) * 256]
            blocks.append(np.ascontiguousarray(sub.reshape(4, 128, 2, 128).transpose(1, 0, 2, 3).reshape(128, 1024)))
    return np.stack(blocks).astype(np.float32)


def _pk(v, ntile):
    return np.ascontiguousarray(np.asarray(v, np.float32).reshape(ntile, 128).T)


def _consts(l, j, G, P):
    c = np.zeros((128, NCF), np.float32)

    def put(name, arr):
        arr = np.asarray(arr, np.float32).reshape(128, -1)
        c[:, _CST[name]:_CST[name] + arr.shape[1]] = arr

    put("g_mix", _pk(P["mix_norm_g"][l], 8))
    put("g_mlp", _pk(P["mlp_norm_g"][l], 8))
    put("g_fin", _pk(P["final_norm_g"], 8))
    put("pool_scale", _pk(P["pool_scale"][l], 2))
    wins = np.array([2, 4, 8, 16], np.float32)
    wpp = np.repeat(wins, 64)
    put("invw", _pk(1.0 / wpp, 2))
    put("invwm1", _pk(1.0 / wpp - 1.0, 2))
    put("himask", (np.arange(128) >= 64).astype(np.float32))
    put("dw_w", P["convb_dw_w"][l].T.reshape(2, 128, 31).transpose(1, 0, 2))
    put("dw_b", _pk(P["convb_dw_b"][l], 2))
    put("ln_g", _pk(P["convb_ln_g"][l], 2))
    put("ln_b", _pk(P["convb_ln_b"][l], 2))
    put("rgc_w", P["rg_conv_w"][l].T.reshape(4, 128, 4).transpose(1, 0, 2))
    put("rgc_b", _pk(P["rg_conv_b"][l], 4))
    put("b_a", _pk(P["rg_b_a"][l], 4))
    put("b_x", _pk(P["rg_b_x"][l], 4))
    put("lam", _pk(P["rg_lambda"][l], 4))
    sm = np.zeros((128, 32), np.float32)
    ratio = np.ones((128, 2, 32), np.float32)
    if j == 0:
        sm[:, 16:] = 1.0
        pos = np.arange(1, 17, dtype=np.float32)
        for t in range(2):
            wv = wpp[t * 128:(t + 1) * 128][:, None]
            ratio[:, t, 16:] = wv / np.minimum(pos[None, :], wv)
    put("scanmask", sm)
    put("ratio", ratio)
    pm = np.zeros((128, 8), np.float32)
    put("pm", pm)
    if G is not None:
        put("G", G)
    wg = np.zeros((2, 128, 128), np.float32)
    for g in range(4):
        t, o = divmod(g, 2)
        wg[t, o * 64:(o + 1) * 64, o * 64:(o + 1) * 64] = P["pool_w"][l][g]
    put("Wg", wg.transpose(1, 0, 2))
    cm = (np.eye(256, dtype=np.float32) - np.float32(1.0 / 256.0)).reshape(2, 128, 256)
    put("Cmat", cm.transpose(1, 0, 2))
    put("Jm", np.full((128, 128), 1.0 / 256.0, np.float32))
    return c


def _cbf(l, P):
    c = np.zeros((128, NCB), np.float32)
    c[:, 0:128] = np.eye(128, dtype=np.float32)
    c[:, 128:640] = P["convb_pw_w"][l].reshape(2, 128, 256).transpose(1, 0, 2).reshape(128, 512)
    for nm, key in (("Wa", "rg_w_a"), ("Wx", "rg_w_x")):
        bd = np.zeros((4, 128, 128), np.float32)
        for hd in range(8):
            t, o = divmod(hd, 2)
            bd[t, o * 64:(o + 1) * 64, o * 64:(o + 1) * 64] = P[key][l][hd]
        c[:, _CBF[nm]:_CBF[nm] + 512] = bd.transpose(1, 0, 2).reshape(128, 512)
    return c


_PROG = {}


def _prog(mode, nblocks):
    if (mode, nblocks) not in _PROG:
        _PROG[(mode, nblocks)] = build_program(mode, nblocks)
    return _PROG[(mode, nblocks)]


def kernel(**inputs):
    P = {k: np.asarray(v, np.float32) for k, v in inputs.items()}
    x = P["x"]
    B = x.shape[0]
    hT = []
    for r in range(NCORES):
        b, j = divmod(r, 4)
        seq = np.concatenate([P["meta_tokens"], x[b]], axis=0)
        s0 = 16 + MAIN * j
        if j == 0:
            tok = np.concatenate([np.zeros((16, D), np.float32), seq[0:16 + MAIN]], axis=0)
        else:
            tok = seq[s0 - PRE:s0 + MAIN]
        hT.append(np.ascontiguousarray(tok.T).reshape(KT, 128, NT))
    out = None
    for l in range(2):
        cbf = _cbf(l, P)
        wpre = _pre_stream(P["w_in"][l])
        ncA = _prog("pre", wpre.shape[0])
        mapsA = [{"hT": hT[r], "wstream": wpre, "cst": _consts(l, r % 4, None, P), "cbf": cbf} for r in range(NCORES)]
        resA = run_bass_kernel_spmd(ncA, mapsA, core_ids=list(range(NCORES)))
        G = np.stack([np.asarray(resA.results[r]["endp"], np.float32) for r in range(NCORES)], axis=1)
        mode = "layer" if l == 0 else "last"
        wst = _layer_stream(P["w_in"][l], P["w_out"][l], P["w_up"][l], P["w_down"][l])
        ncB = _prog(mode, wst.shape[0])
        mapsB = []
        for r in range(NCORES):
            b, j = divmod(r, 4)
            c = _consts(l, j, G.reshape(128, 64), P)
            pm = np.zeros((128, 8), np.float32)
            pm[:, 4 * b:4 * b + j] = 1.0
            c[:, _CST["pm"]:_CST["pm"] + 8] = pm
            mapsB.append({"hT": hT[r], "wstream": wst, "cst": c, "cbf": cbf,
                          "ya_in": np.asarray(resA.results[r]["ya_out"]), "arg_in": np.asarray(resA.results[r]["arg_out"]),
                          "brg_in": np.asarray(resA.results[r]["brg_out"]),
                          "gate_in": np.asarray(resA.results[r]["gate_out"])})
        resB = run_bass_kernel_spmd(ncB, mapsB, core_ids=list(range(NCORES)))
        if l == 0:
            hn = [np.asarray(resB.results[r]["hout"], np.float32) for r in range(NCORES)]
            hT = []
            for r in range(NCORES):
                b, j = divmod(r, 4)
                t = hn[r].copy()
                if j == 0:
                    t[:, :, 0:16] = 0.0
                else:
                    t[:, :, 0:PRE] = hn[r - 1][:, :, NT - PRE:NT]
                hT.append(t)
        else:
            out = np.zeros((B, 4 * MAIN, D), np.float32)
            for r in range(NCORES):
                b, j = divmod(r, 4)
                y = np.asarray(resB.results[r]["yout"], np.float32).reshape(D, MAIN)
                out[b, MAIN * j:MAIN * (j + 1), :] = y.T
    return out
```

```python
import contextlib
import numpy as np
import concourse.bass as bass
import concourse.mybir as mybir
from concourse.bass_utils import run_bass_kernel_spmd

F32 = mybir.dt.float32
BF16 = mybir.dt.bfloat16
AF = mybir.ActivationFunctionType
ALU = mybir.AluOpType

NCORES = 8
D = 1024
KT = 8
PRE = 32
MAIN = 2048
NT = PRE + MAIN
NB = 5
BW = NT // NB
PADL = 32
EPS = 1e-6
NRING = 6
GELU_K = 1.5957691216057308

_CST = {}
_off = 0
for _n, _w in [("g_mix", 8), ("g_mlp", 8), ("g_fin", 8), ("pool_scale", 2), ("invw", 2), ("invwm1", 2),
               ("himask", 1), ("dw_w", 62), ("dw_b", 2), ("ln_g", 2), ("ln_b", 2), ("rgc_w", 16),
               ("rgc_b", 4), ("b_a", 4), ("b_x", 4), ("lam", 4), ("scanmask", 32), ("ratio", 64),
               ("pm", 8), ("G", 64), ("Wg", 256), ("Cmat", 512), ("Jm", 128)]:
    _CST[_n] = _off
    _off += _w
NCF = _off
_CBF = {"ident": 0, "pw": 128, "Wa": 640, "Wx": 1152}
NCB = 1664

ENGS = ("pe", "act", "dve", "pool", "sp")


class Op:
    __slots__ = ("idx", "eng", "fn", "deps", "dma_sem", "count", "signal", "waits", "inc")

    def __init__(self, idx, eng, fn, dma_sem, inc):
        self.idx = idx
        self.eng = eng
        self.fn = fn
        self.deps = {}
        self.dma_sem = dma_sem
        self.count = None
        self.signal = False
        self.waits = []
        self.inc = inc


class Sched:
    def __init__(self):
        self.ops = []
        self.last_writer = {}
        self.readers = {}

    def op(self, eng, fn, reads=(), writes=(), dma_sem=None, inc=16):
        o = Op(len(self.ops), eng, fn, dma_sem, inc)
        for k in reads:
            w = self.last_writer.get(k)
            if w is not None:
                o.deps[w] = "raw"
            if isinstance(k, tuple) and k[0] == "ps":
                for r in self.readers.get(k, ()):
                    if self.ops[r].eng != eng and r not in o.deps:
                        o.deps[r] = "rar"
        for k in writes:
            w = self.last_writer.get(k)
            if w is not None and w not in o.deps:
                o.deps[w] = "waw"
            for r in self.readers.get(k, ()):
                if r not in o.deps:
                    o.deps[r] = "war"
        for k in reads:
            self.readers.setdefault(k, []).append(o.idx)
        for k in writes:
            self.last_writer[k] = o.idx
            self.readers[k] = []
        self.ops.append(o)
        return o

    def emit(self, nc, st):
        ops = self.ops
        for o in ops:
            for d, kind in o.deps.items():
                p = ops[d]
                if p.dma_sem is not None or o.dma_sem is not None or p.eng != o.eng:
                    need = True
                else:
                    need = (o.eng != "pe")
                if need:
                    o.waits.append(d)
                    p.signal = True
        cnt = {e: 0 for e in ENGS}
        dcnt = {}
        for o in ops:
            if o.dma_sem is not None:
                dcnt[o.dma_sem] = dcnt.get(o.dma_sem, 0) + o.inc
                o.count = dcnt[o.dma_sem]
            elif o.signal:
                cnt[o.eng] += 1
                o.count = cnt[o.eng]
        sems = {e: st.enter_context(nc.semaphore("s_" + e)) for e in ENGS}
        dsems = {k: st.enter_context(nc.semaphore("d_" + str(k))) for k in sorted(dcnt)}
        block = st.enter_context(nc.Block())
        queues = {e: [o for o in ops if o.eng == e] for e in ENGS}

        def run_queue(eng_name, engine):
            waited = {}
            for o in queues[eng_name]:
                need = {}
                for d in o.waits:
                    p = ops[d]
                    key = ("d", p.dma_sem) if p.dma_sem is not None else ("e", p.eng)
                    if p.count > need.get(key, 0):
                        need[key] = p.count
                for key, val in need.items():
                    if waited.get(key, 0) >= val:
                        continue
                    waited[key] = val
                    engine.wait_ge(dsems[key[1]] if key[0] == "d" else sems[key[1]], val)
                ins = o.fn(engine)
                if o.dma_sem is not None:
                    ins.then_inc(dsems[o.dma_sem], o.inc)
                elif o.signal:
                    ins.then_inc(sems[o.eng], 1)

        @block.tensor
        def _(e):
            run_queue("pe", e)

        @block.scalar
        def _(e):
            run_queue("act", e)

        @block.vector
        def _(e):
            run_queue("dve", e)

        @block.gpsimd
        def _(e):
            run_queue("pool", e)

        @block.sync
        def _(e):
            run_queue("sp", e)


class Pool:
    def __init__(self, nc, name, shape, dtype, n):
        self.t = [nc.alloc_sbuf_tensor(f"{name}{i}", shape, dtype) for i in range(n)]
        self.name = name
        self.i = 0

    def get(self):
        i = self.i % len(self.t)
        self.i += 1
        return self.t[i], (self.name, i)


def build_program(mode, nblocks):
    nc = bass.Bass("TRN2", target_bir_lowering=False)
    hT = nc.dram_tensor("hT", [KT, 128, NT], F32, kind="ExternalInput").ap()
    wstream = nc.dram_tensor("wstream", [nblocks, 128, 1024], F32, kind="ExternalInput").ap()
    cst_d = nc.dram_tensor("cst", [128, NCF], F32, kind="ExternalInput").ap()
    cbf_d = nc.dram_tensor("cbf", [128, NCB], F32, kind="ExternalInput").ap()
    if mode == "pre":
        out_d = nc.dram_tensor("endp", [128, 8], F32, kind="ExternalOutput").ap()
        gate_out = nc.dram_tensor("gate_out", [4, 128, NT], F32, kind="ExternalOutput").ap()
        ya_out = nc.dram_tensor("ya_out", [4, 128, NT], BF16, kind="ExternalOutput").ap()
        arg_out = nc.dram_tensor("arg_out", [4, 128, NT], F32, kind="ExternalOutput").ap()
        brg_out = nc.dram_tensor("brg_out", [4, 128, NT], F32, kind="ExternalOutput").ap()
    else:
        gate_in = nc.dram_tensor("gate_in", [4, 128, NT], F32, kind="ExternalInput").ap()
        ya_in = nc.dram_tensor("ya_in", [4, 128, NT], BF16, kind="ExternalInput").ap()
        arg_in = nc.dram_tensor("arg_in", [4, 128, NT], F32, kind="ExternalInput").ap()
        brg_in = nc.dram_tensor("brg_in", [4, 128, NT], F32, kind="ExternalInput").ap()
    if mode == "pre":
        pass
    elif mode == "layer":
        out_d = nc.dram_tensor("hout", [KT, 128, NT], F32, kind="ExternalOutput").ap()
    else:
        out_d = nc.dram_tensor("yout", [KT, 128, MAIN], F32, kind="ExternalOutput").ap()

    S = Sched()
    h = nc.alloc_sbuf_tensor("h", [128, KT, NT], F32)
    A = nc.alloc_sbuf_tensor("A", [128, KT, NT], BF16)
    Y = nc.alloc_sbuf_tensor("Y", [128, KT, NT], BF16) if mode != "pre" else None
    NSTG = 6 if mode == "pre" else 4
    STG = nc.alloc_sbuf_tensor("STG", [128, NSTG, PADL + NT], BF16)
    ring = [nc.alloc_sbuf_tensor(f"wr{i}", [128, 1024], BF16) for i in range(NRING)]
    cst = nc.alloc_sbuf_tensor("cst_s", [128, NCF], F32)
    cbf = nc.alloc_sbuf_tensor("cbf_s", [128, NCB], BF16)
    pmat = nc.alloc_sbuf_tensor("pmat", [128, 8, 128], BF16)
    rgd = nc.alloc_sbuf_tensor("rgd", [128, 16, 128], BF16)
    ones = nc.alloc_sbuf_tensor("ones", [128, 128], BF16)
    dres = nc.alloc_sbuf_tensor("dres", [128, 62, 128], BF16) if mode == "pre" else None
    small = nc.alloc_sbuf_tensor("small", [128, 64], F32)
    tf = Pool(nc, "tf", [128, BW], F32, 17 if mode == "pre" else 12)
    ths = Pool(nc, "ths", [128, BW], F32, 5)
    tb = Pool(nc, "tb", [128, BW], BF16, 8 if mode == "pre" else 6)
    ps_t = [nc.alloc_psum_tensor(f"ps{i}", [128, 512], F32) for i in range(8)]
    ps_i = [0]

    def run_gen(g):
        for _ in g:
            pass

    def PS():
        i = ps_i[0] % 8
        ps_i[0] += 1
        return ps_t[i], ("ps", i)

    def C(name, j=0, w=1):
        o = _CST[name] + j
        return cst[:, o:o + w]

    def bcols(b):
        return slice(b * BW, (b + 1) * BW)

    S.op("sp", lambda e: e.dma_start(out=cst[:], in_=cst_d), writes=["cst"], dma_sem="ldc")
    S.op("pool", lambda e: e.dma_start(out=cbf[:], in_=cbf_d), writes=["cbf"], dma_sem="ldb")
    hT_p = hT.rearrange("k p t -> p k t")

    def load_h(blocks):
        for b in blocks:
            S.op("sp", lambda e, b=b: e.dma_start(out=h[:, :, bcols(b)], in_=hT_p[:, :, bcols(b)]),
                 writes=[("h", k, b) for k in range(KT)], dma_sem=f"ldh{b}")

    if mode == "pre":
        load_h(range(NB))

    wstate = {"next_dma": 0, "next_use": 0}

    def w_issue():
        i = wstate["next_dma"]
        if i >= nblocks:
            return
        wstate["next_dma"] += 1
        s = i % NRING
        S.op("pool", lambda e, i=i, s=s: e.dma_start(out=ring[s][:], in_=wstream[i]),
             writes=[("w", s)], dma_sem=f"w{s}")

    for _ in range(NRING):
        w_issue()

    def w_acquire():
        i = wstate["next_use"]
        wstate["next_use"] += 1
        s = i % NRING
        return ring[s], ("w", s)

    S.op("dve", lambda e: e.memset(ones[:], 1.0 / 1024.0), writes=["ones"])
    for s4 in range(NSTG):
        S.op("dve", lambda e, s4=s4: e.memset(STG[:, s4, 0:PADL], 0.0), writes=[("stgpad", s4)])
    S.op("act", lambda e: e.activation(out=small[:, 12:16], in_=C("lam", 0, 4), func=AF.Exp, scale=-1.0),
         reads=["cst"], writes=["sm_t"])
    S.op("act", lambda e: e.activation(out=small[:, 16:20], in_=small[:, 12:16], func=AF.Ln, bias=1.0),
         reads=["sm_t"], writes=["sm_t2"])
    S.op("dve", lambda e: e.tensor_scalar(out=small[:, 0:4], in0=small[:, 16:20], scalar1=-8.0, scalar2=None,
                                          op0=ALU.mult), reads=["sm_t2"], writes=["c1"])
    S.op("dve", lambda e: e.tensor_scalar(out=small[:, 4:8], in0=small[:, 16:20], scalar1=-16.0, scalar2=None,
                                          op0=ALU.mult), reads=["sm_t2"], writes=["c2"])
    S.op("dve", lambda e: e.memset(small[:, 8:12], 0.0), writes=["carry"])
    if mode != "pre":
        for r in range(8):
            gE = C("G", r * 8, 4)
            gP = C("G", r * 8 + 4, 4)
            S.op("dve", lambda e, gP=gP: e.tensor_tensor(out=small[:, 20:24], in0=gP, in1=small[:, 8:12], op=ALU.mult),
                 reads=["cst", "carry"], writes=["cc_t"])
            S.op("dve", lambda e, gE=gE: e.tensor_tensor(out=small[:, 24:28], in0=small[:, 20:24], in1=gE, op=ALU.add),
                 reads=["cc_t", "cst"], writes=["cc_u"])
            S.op("dve", lambda e: e.tensor_tensor(out=small[:, 28:32], in0=small[:, 24:28], in1=small[:, 8:12],
                                                  op=ALU.subtract), reads=["cc_u", "carry"], writes=["cc_v"])
            S.op("dve", lambda e, r=r: e.scalar_tensor_tensor(out=small[:, 32:36], in0=small[:, 28:32],
                                                              scalar=C("pm", r), in1=small[:, 8:12],
                                                              op0=ALU.mult, op1=ALU.add),
                 reads=["cc_v", "carry", "cst"], writes=["cc_w"])
            S.op("dve", lambda e: e.tensor_copy(out=small[:, 8:12], in_=small[:, 32:36]), reads=["cc_w"], writes=["carry"])
    for t in range(4):
        for k in range(4):
            S.op("pool", lambda e, t=t, k=k: e.tensor_scalar(out=rgd[:, t * 4 + k, :], in0=cbf[:, 0:128],
                                                             scalar1=C("rgc_w", t * 4 + k), scalar2=1.0,
                                                             op0=ALU.mult, op1=ALU.mult),
                 reads=["cst", "cbf"], writes=[("rgd", t)])
    if mode == "pre":
        for j in range(62):
            S.op("pool", lambda e, j=j: e.tensor_scalar(out=dres[:, j, :], in0=cbf[:, 0:128], scalar1=C("dw_w", j), scalar2=1.0,
                                                        op0=ALU.mult, op1=ALU.mult),
                 reads=["cst", "cbf"], writes=["dres"])
        for t in range(2):
            wg = cst[:, _CST["Wg"] + t * 128:_CST["Wg"] + (t + 1) * 128]
            S.op("dve", lambda e, t=t, wg=wg: e.tensor_copy(out=pmat[:, t, :], in_=wg), reads=["cst"], writes=[("pm_", t)])
            S.op("dve", lambda e, t=t, wg=wg: e.tensor_scalar(out=pmat[:, 2 + t, :], in0=wg, scalar1=C("invwm1", t),
                                                              scalar2=None, op0=ALU.mult), reads=["cst"], writes=[("pm_", t)])
            S.op("dve", lambda e, t=t, wg=wg: e.tensor_scalar(out=pmat[:, 4 + t, :], in0=wg, scalar1=C("invw", t),
                                                              scalar2=None, op0=ALU.mult), reads=["cst"], writes=[("pm_", t)])
            S.op("dve", lambda e, t=t, wg=wg: e.tensor_scalar(out=pmat[:, 6 + t, :], in0=wg, scalar1=C("invw", t),
                                                              scalar2=C("himask"), op0=ALU.mult, op1=ALU.mult),
                 reads=["cst"], writes=[("pm_", t)])

    def rms_to_A(gname):
        for b in range(NB):
            cs = bcols(b)
            ps, pk = PS()
            for k in range(KT):
                sq, sk = tb.get()
                S.op("act", lambda e, k=k, cs=cs, sq=sq: e.activation(out=sq[:], in_=h[:, k, cs], func=AF.Square),
                     reads=[("h", k, b)], writes=[sk])
                S.op("pe", lambda e, k=k, sq=sq, ps=ps: e.matmul(ps[:, 0:BW], lhsT=ones[:], rhs=sq[:],
                                                                 start=(k == 0), stop=(k == KT - 1)),
                     reads=["ones", sk], writes=[pk])
            l, lk = tf.get()
            S.op("act", lambda e, ps=ps, l=l: e.activation(out=l[:], in_=ps[:, 0:BW], func=AF.Ln, bias=EPS),
                 reads=[pk], writes=[lk])
            S.op("act", lambda e, ps=ps, l=l: e.activation(out=ps[:, 0:BW], in_=l[:], func=AF.Exp, scale=-0.5),
                 reads=[lk], writes=[pk])
            for k in range(KT):
                S.op("dve", lambda e, k=k, cs=cs, ps=ps: e.scalar_tensor_tensor(
                    out=A[:, k, cs], in0=h[:, k, cs], scalar=C(gname, k), in1=ps[:, 0:BW],
                    op0=ALU.mult, op1=ALU.mult),
                    reads=[("h", k, b), pk, "cst"], writes=[("A", k, b)])

    def proj(wt, wk, wsel, rhs_fn, nk, b):
        ps, pk = PS()
        for k in range(nk):
            rap, rkey = rhs_fn(k)
            S.op("pe", lambda e, k=k, rap=rap, ps=ps: e.matmul(ps[:, 0:BW], lhsT=wsel(wt, k), rhs=rap,
                                                               start=(k == 0), stop=(k == nk - 1)),
                 reads=[wk, rkey], writes=[pk])
        return ps, pk

    def w8(wt, k):
        return wt[:, k * 128:(k + 1) * 128]

    def a_rhs(b):
        return lambda k: (A[:, k, bcols(b)], ("A", k, b))

    def stg_cols(slot, b, shift):
        c0 = PADL + b * BW - shift
        return STG[:, slot, c0:c0 + BW]

    def stg_keys(slot, b):
        ks = [("stg", slot, b)]
        ks.append(("stg", slot, b - 1) if b > 0 else ("stgpad", slot))
        return ks

    def inproj_to_stg(slot):
        wt, wk = w_acquire()
        for b in range(NB):
            ps, pk = proj(wt, wk, w8, a_rhs(b), KT, b)
            S.op("act", lambda e, ps=ps, b=b: e.activation(out=STG[:, slot, PADL + b * BW:PADL + (b + 1) * BW],
                                                           in_=ps[:, 0:BW], func=AF.Copy),
                 reads=[pk], writes=[("stg", slot, b)])
        w_issue()

    extra_outs = []
    if mode == "pre":
        rms_to_A("g_mix")
    else:
        for t in range(4):
            S.op("sp", lambda e, t=t: e.dma_start(out=Y[:, t, :], in_=ya_in[t]),
                 writes=[("Y", t, b) for b in range(NB)], dma_sem=f"ldy{t}")

    if mode == "pre":
        inproj_to_stg(0)
        inproj_to_stg(1)
        def pool_tile(t):
            wlo, whi = (2, 4) if t == 0 else (8, 16)
            for b in range(NB):
                ps, pk = PS()
                for k in range(whi):
                    mat = pmat[:, 2 + t, :] if k == 0 else (pmat[:, 4 + t, :] if k < wlo else pmat[:, 6 + t, :])
                    S.op("pe", lambda e, k=k, mat=mat, ps=ps, b=b: e.matmul(ps[:, 0:BW], lhsT=mat, rhs=stg_cols(t, b, k),
                                                                           start=(k == 0), stop=(k == whi - 1)),
                         reads=[("pm_", t)] + stg_keys(t, b), writes=[pk])
                yt, kyt = tb.get()
                S.op("act", lambda e, ps=ps, yt=yt: e.activation(out=yt[:], in_=ps[:, 0:BW],
                                                                 func=AF.Identity, scale=C("pool_scale", t)),
                     reads=[pk, "cst"], writes=[kyt])
                if b == 0:
                    y0 = (yt, kyt)
                else:
                    S.op("sp", lambda e, yt=yt, b=b: e.dma_start(out=ya_out[t][:, bcols(b)], in_=yt[:]), reads=[kyt],
                         writes=[("oy", t, b)], dma_sem=f"sy{kyt[1]}")
                    extra_outs.append(("oy", t, b))
            psS, pkS = PS()
            for k in range(whi):
                mat = pmat[:, 4 + t, :] if k < wlo else pmat[:, 6 + t, :]
                S.op("pe", lambda e, k=k, mat=mat, psS=psS: e.matmul(psS[:, 0:PRE], lhsT=mat,
                                                                    rhs=STG[:, t, PADL - k:PADL - k + PRE],
                                                                    start=(k == 0), stop=(k == whi - 1)),
                     reads=[("pm_", t)] + stg_keys(t, 0), writes=[pkS])
            psX, pkX = PS()
            S.op("pe", lambda e, psX=psX: e.matmul(psX[:, 0:PRE], lhsT=pmat[:, t, :], rhs=STG[:, t, PADL:PADL + PRE],
                                                   start=True, stop=True),
                 reads=[("pm_", t)] + stg_keys(t, 0), writes=[pkX])
            t1, k1 = tf.get()
            S.op("dve", lambda e, psS=psS, t1=t1: e.tensor_tensor(out=t1[:, 0:PRE], in0=psS[:, 0:PRE],
                                                                  in1=C("ratio", t * 32, 32), op=ALU.mult),
                 reads=[pkS, "cst"], writes=[k1])
            t2, k2 = tf.get()
            S.op("dve", lambda e, psX=psX, t1=t1, t2=t2: e.tensor_tensor(out=t2[:, 0:PRE], in0=t1[:, 0:PRE],
                                                                         in1=psX[:, 0:PRE], op=ALU.subtract),
                 reads=[pkX, k1], writes=[k2])
            yt, kyt = y0
            S.op("act", lambda e, t2=t2, yt=yt: e.activation(out=yt[:, 0:PRE], in_=t2[:, 0:PRE], func=AF.Identity,
                                                             scale=C("pool_scale", t)),
                 reads=[k2, "cst"], writes=[kyt])
            S.op("sp", lambda e, yt=yt: e.dma_start(out=ya_out[t][:, bcols(0)], in_=yt[:]), reads=[kyt],
                 writes=[("oy", t, 0)], dma_sem=f"sy{kyt[1]}")
            extra_outs.append(("oy", t, 0))

        pool_tile(0)
        pool_tile(1)

    if mode == "pre":
        S.op("dve", lambda e: e.memset(small[:, 40:44], 0.0), writes=[("rsum", t) for t in range(4)])
    rg_slots = [2, 3, 2, 3] if mode != "pre" else [0, 1, 2, 3]
    if mode == "pre":
        for t in range(4):
            inproj_to_stg(rg_slots[t])
    else:
        pass

    rg_prev = {}

    def rg_group(tiles, b, cgws):
        U = {t: {} for t in tiles}
        if mode != "pre":
            for t in tiles:
                a, ka = tf.get()
                bb, kb = tf.get()
                S.op("sp", lambda e, a=a, t=t: e.dma_start(out=a[:], in_=arg_in[t][:, bcols(b)]), writes=[ka],
                     dma_sem=f"so{ka[1]}")
                S.op("sp", lambda e, bb=bb, t=t: e.dma_start(out=bb[:], in_=brg_in[t][:, bcols(b)]), writes=[kb],
                     dma_sem=f"so{kb[1]}")
                U[t].update(a=a, ka=ka, xc=bb, kxc=kb)
        else:
            for t in tiles:
                slot = rg_slots[t]
                ps_c, pk_c = PS()
                for k in range(4):
                    S.op("pe", lambda e, k=k, ps_c=ps_c, t=t, slot=slot: e.matmul(
                        ps_c[:, 0:BW], lhsT=rgd[:, t * 4 + k, :], rhs=stg_cols(slot, b, 3 - k),
                        start=(k == 0), stop=(k == 3)),
                        reads=[("rgd", t)] + stg_keys(slot, b), writes=[pk_c])
                U[t].update(ps_c=ps_c, pk_c=pk_c)
            yield
            for t in tiles:
                u = U[t]
                xc, kxc = tf.get()
                xcb, kxcb = tb.get()
                S.op("dve", lambda e, ps_c=u["ps_c"], xc=xc, t=t: e.tensor_scalar(out=xc[:], in0=ps_c[:, 0:BW], scalar1=C("rgc_b", t),
                                                                                 scalar2=None, op0=ALU.add),
                     reads=[u["pk_c"], "cst"], writes=[kxc])
                S.op("dve", lambda e, xc=xc, xcb=xcb: e.tensor_copy(out=xcb[:], in_=xc[:]), reads=[kxc], writes=[kxcb])
                u.update(xc=xc, kxc=kxc, xcb=xcb, kxcb=kxcb)
            for t in tiles:
                u = U[t]
                ps_a, pk_a = PS()
                S.op("pe", lambda e, ps_a=ps_a, xcb=u["xcb"], t=t: e.matmul(
                    ps_a[:, 0:BW], lhsT=cbf[:, _CBF["Wa"] + t * 128:_CBF["Wa"] + (t + 1) * 128], rhs=xcb[:], start=True, stop=True),
                    reads=["cbf", u["kxcb"]], writes=[pk_a])
                ps_x, pk_x = PS()
                S.op("pe", lambda e, ps_x=ps_x, xcb=u["xcb"], t=t: e.matmul(
                    ps_x[:, 0:BW], lhsT=cbf[:, _CBF["Wx"] + t * 128:_CBF["Wx"] + (t + 1) * 128], rhs=xcb[:], start=True, stop=True),
                    reads=["cbf", u["kxcb"]], writes=[pk_x])
                u.update(ps_a=ps_a, pk_a=pk_a, ps_x=ps_x, pk_x=pk_x)
            for t in tiles:
                u = U[t]
                r, kr = tf.get()
                if mode == "pre":
                    lo = PRE if b == 0 else 0
                    if b == 0:
                        S.op("act", lambda e, ps_a=u["ps_a"], r=r, t=t: e.activation(out=r[:, 0:PRE], in_=ps_a[:, 0:PRE],
                                                                                    func=AF.Sigmoid, bias=C("b_a", t)),
                             reads=[u["pk_a"], "cst"], writes=[kr])
                    S.op("act", lambda e, ps_a=u["ps_a"], r=r, lo=lo, t=t: e.activation(
                        out=r[:, lo:BW], in_=ps_a[:, lo:BW], func=AF.Sigmoid, bias=C("b_a", t),
                        accum_out=small[:, 44 + t:45 + t]),
                        reads=[u["pk_a"], "cst"], writes=[kr, ("racc", t)])
                    S.op("dve", lambda e, t=t: e.tensor_tensor(out=small[:, 40 + t:41 + t], in0=small[:, 40 + t:41 + t],
                                                               in1=small[:, 44 + t:45 + t], op=ALU.add),
                         reads=[("racc", t), ("rsum", t)], writes=[("rsum", t)])
                else:
                    S.op("act", lambda e, ps_a=u["ps_a"], r=r, t=t: e.activation(out=r[:], in_=ps_a[:, 0:BW], func=AF.Sigmoid,
                                                                                bias=C("b_a", t)),
                         reads=[u["pk_a"], "cst"], writes=[kr])
                u.update(r=r, kr=kr)
            for t in tiles:
                u = U[t]
                a, ka = tf.get()
                S.op("act", lambda e, r=u["r"], a=a, t=t: e.activation(out=a[:], in_=r[:], func=AF.Exp, scale=small[:, t:t + 1]),
                     reads=[u["kr"], "c1"], writes=[ka])
                u.update(a=a, ka=ka)
            for t in tiles:
                u = U[t]
                S.op("dve", lambda e, r=u["r"], a=u["a"]: e.tensor_tensor(out=r[:], in0=a[:], in1=a[:], op=ALU.mult),
                     reads=[u["ka"], u["kr"]], writes=[u["kr"]])
            for t in tiles:
                u = U[t]
                S.op("act", lambda e, ps_a=u["ps_a"], ps_x=u["ps_x"], t=t: e.activation(
                    out=ps_a[:, 0:BW], in_=ps_x[:, 0:BW], func=AF.Sigmoid, bias=C("b_x", t)),
                    reads=[u["pk_x"], "cst"], writes=[u["pk_a"]])
            for t in tiles:
                u = U[t]
                S.op("act", lambda e, r=u["r"], ps_x=u["ps_x"]: e.activation(out=ps_x[:, 0:BW], in_=r[:], func=AF.Sqrt,
                                                                             scale=-1.0, bias=1.0),
                     reads=[u["kr"]], writes=[u["pk_x"]])
            for t in tiles:
                u = U[t]
                S.op("dve", lambda e, ps_a=u["ps_a"], xc=u["xc"]: e.tensor_tensor(out=xc[:], in0=ps_a[:, 0:BW], in1=xc[:], op=ALU.mult),
                     reads=[u["pk_a"], u["kxc"]], writes=[u["kxc"]])
            for t in tiles:
                u = U[t]
                S.op("dve", lambda e, ps_x=u["ps_x"], xc=u["xc"]: e.tensor_tensor(out=xc[:], in0=ps_x[:, 0:BW], in1=xc[:], op=ALU.mult),
                     reads=[u["pk_x"], u["kxc"]], writes=[u["kxc"]])
        if mode == "pre":
            if b == 0:
                for t in tiles:
                    u = U[t]
                    S.op("dve", lambda e, bb=u["xc"]: e.tensor_tensor(out=bb[:, 0:PRE], in0=bb[:, 0:PRE], in1=C("scanmask", 0, 32),
                                                                      op=ALU.mult), reads=[u["kxc"], "cst"], writes=[u["kxc"]])
            for t in tiles:
                u = U[t]
                S.op("sp", lambda e, a=u["a"], t=t: e.dma_start(out=arg_out[t][:, bcols(b)], in_=a[:]), reads=[u["ka"]],
                     writes=[("oar", t, b)], dma_sem=f"so{u['ka'][1]}")
                S.op("sp", lambda e, bb=u["xc"], t=t: e.dma_start(out=brg_out[t][:, bcols(b)], in_=bb[:]), reads=[u["kxc"]],
                     writes=[("obr", t, b)], dma_sem=f"so{u['kxc'][1]}")
                extra_outs.append(("oar", t, b))
                extra_outs.append(("obr", t, b))
        if b == 0:
            if mode != "pre":
                for t in tiles:
                    u = U[t]
                    S.op("dve", lambda e, bb=u["xc"], a=u["a"], t=t: e.scalar_tensor_tensor(
                        out=bb[:, PRE:PRE + 1], in0=a[:, PRE:PRE + 1], scalar=small[:, 8 + t:9 + t],
                        in1=bb[:, PRE:PRE + 1], op0=ALU.mult, op1=ALU.add),
                        reads=[u["kxc"], u["ka"], "carry"], writes=[u["kxc"]])
        for t in tiles:
            u = U[t]
            hs, khs = ths.get()
            if t not in rg_prev:
                S.op("dve", lambda e, hs=hs, a=u["a"], bb=u["xc"]: e.tensor_tensor_scan(
                    out=hs[:], data0=a[:], data1=bb[:], initial=0.0, op0=ALU.mult, op1=ALU.add),
                    reads=[u["ka"], u["kxc"]], writes=[khs])
            else:
                ph, pkh = rg_prev[t]
                S.op("dve", lambda e, hs=hs, a=u["a"], bb=u["xc"], ph=ph: e.tensor_tensor_scan(
                    out=hs[:], data0=a[:], data1=bb[:], initial=ph[:, BW - 1:BW], op0=ALU.mult, op1=ALU.add),
                    reads=[u["ka"], u["kxc"], pkh], writes=[khs])
            rg_prev[t] = (hs, khs)
            u.update(hs=hs, khs=khs)
        if mode == "pre":
            if b == NB - 1:
                for t in tiles:
                    S.op("dve", lambda e, hs=U[t]["hs"], t=t: e.tensor_copy(out=small[:, 56 + t:57 + t], in_=hs[:, BW - 1:BW]),
                         reads=[U[t]["khs"]], writes=[("endst", t)])
            return
        for t in tiles:
            u = U[t]
            g, kg = tf.get()
            S.op("sp", lambda e, g=g, t=t: e.dma_start(out=g[:], in_=gate_in[t][:, bcols(b)]), writes=[kg], dma_sem=f"so{kg[1]}")
            S.op("dve", lambda e, g=g, hs=u["hs"], t=t: e.tensor_tensor(out=Y[:, 4 + t, bcols(b)], in0=g[:], in1=hs[:], op=ALU.mult),
                 reads=[kg, u["khs"]], writes=[("Y", 4 + t, b)])

    def gate_block(b, cgws):
        U = {t: {} for t in range(4)}
        for t in range(4):
            wt, wk = cgws[t]
            ps_g, pk_g = proj(wt, wk, w8, a_rhs(b), KT, b)
            U[t].update(ps_g=ps_g, pk_g=pk_g)
        yield
        for t in range(4):
            u = U[t]
            x2, kx2 = tf.get()
            S.op("act", lambda e, ps_g=u["ps_g"], x2=x2: e.activation(out=x2[:], in_=ps_g[:, 0:BW], func=AF.Square),
                 reads=[u["pk_g"]], writes=[kx2])
            u.update(x2=x2, kx2=kx2)
        for t in range(4):
            u = U[t]
            S.op("dve", lambda e, x2=u["x2"]: e.tensor_scalar(out=x2[:], in0=x2[:], scalar1=0.044715 * GELU_K, scalar2=GELU_K,
                                                              op0=ALU.mult, op1=ALU.add), reads=[u["kx2"]], writes=[u["kx2"]])
        for t in range(4):
            u = U[t]
            S.op("dve", lambda e, ps_g=u["ps_g"], x2=u["x2"]: e.tensor_tensor(out=x2[:], in0=ps_g[:, 0:BW], in1=x2[:], op=ALU.mult),
                 reads=[u["pk_g"], u["kx2"]], writes=[u["kx2"]])
        for t in range(4):
            u = U[t]
            S.op("act", lambda e, q=u["x2"]: e.activation(out=q[:], in_=q[:], func=AF.Sigmoid),
                 reads=[u["kx2"]], writes=[u["kx2"]])
        for t in range(4):
            u = U[t]
            S.op("dve", lambda e, ps_g=u["ps_g"], q=u["x2"]: e.tensor_tensor(out=q[:], in0=ps_g[:, 0:BW], in1=q[:], op=ALU.mult),
                 reads=[u["pk_g"], u["kx2"]], writes=[u["kx2"]])
            S.op("sp", lambda e, q=u["x2"], t=t: e.dma_start(out=gate_out[t][:, bcols(b)], in_=q[:]), reads=[u["kx2"]],
                 writes=[("og", t, b)], dma_sem=f"so{u['kx2'][1]}")
            extra_outs.append(("og", t, b))

    if mode == "pre":
        for m in range(2):
            wv, kv = w_acquire()
            wg_, kg_ = w_acquire()
            for b in range(NB):
                psv, pkv = proj(wv, kv, w8, a_rhs(b), KT, b)
                psg, pkg = proj(wg_, kg_, w8, a_rhs(b), KT, b)
                sg, ksg = tf.get()
                S.op("act", lambda e, psg=psg, sg=sg: e.activation(out=sg[:], in_=psg[:, 0:BW], func=AF.Sigmoid),
                     reads=[pkg], writes=[ksg])
                S.op("dve", lambda e, psv=psv, sg=sg, b=b, m=m: e.tensor_tensor(
                    out=STG[:, 4 + m, PADL + b * BW:PADL + (b + 1) * BW], in0=psv[:, 0:BW], in1=sg[:], op=ALU.mult),
                    reads=[pkv, ksg], writes=[("stg", 4 + m, b)])
            w_issue()
            w_issue()
        cg_w = [w_acquire() for t in range(4)]

        dg_i = [0]

        def conformer_block(b):
            cen_in = []
            for m in range(2):
                ps, pk = PS()
                for k in range(31):
                    S.op("pe", lambda e, m=m, k=k, ps=ps: e.matmul(ps[:, 0:BW], lhsT=dres[:, m * 31 + k, :],
                                                                  rhs=stg_cols(4 + m, b, 30 - k),
                                                                  start=(k == 0), stop=(k == 30)),
                         reads=["dres"] + stg_keys(4 + m, b), writes=[pk])
                c, kc = tf.get()
                S.op("act", lambda e, ps=ps, c=c, m=m: e.activation(out=c[:], in_=ps[:, 0:BW], func=AF.Identity,
                                                                    bias=C("dw_b", m)),
                     reads=[pk, "cst"], writes=[kc])
                cen_in.append((c, kc))
            yield
            cens = []
            for m in range(2):
                ps, pk = PS()
                for k in range(2):
                    c, kc = cen_in[k]
                    o = _CST["Cmat"] + k * 256 + m * 128
                    S.op("pe", lambda e, ps=ps, c=c, o=o, k=k: e.matmul(ps[:, 0:BW], lhsT=cst[:, o:o + 128], rhs=c[:],
                                                                       start=(k == 0), stop=(k == 1)),
                         reads=["cst", kc], writes=[pk])
                cens.append((ps, pk))
            psv, pkv = PS()
            for m in range(2):
                ps, pk = cens[m]
                sq, ksq = tf.get()
                S.op("act", lambda e, ps=ps, sq=sq: e.activation(out=sq[:], in_=ps[:, 0:BW], func=AF.Square),
                     reads=[pk], writes=[ksq])
                S.op("pe", lambda e, psv=psv, sq=sq, m=m: e.matmul(psv[:, 0:BW], lhsT=cst[:, _CST["Jm"]:_CST["Jm"] + 128],
                                                                  rhs=sq[:], start=(m == 0), stop=(m == 1)),
                     reads=["cst", ksq], writes=[pkv])
            yield
            l, kl = tf.get()
            S.op("act", lambda e, psv=psv, l=l: e.activation(out=l[:], in_=psv[:, 0:BW], func=AF.Ln, bias=EPS),
                 reads=[pkv], writes=[kl])
            rs, krs = tf.get()
            S.op("act", lambda e, l=l, rs=rs: e.activation(out=rs[:], in_=l[:], func=AF.Exp, scale=-0.5),
                 reads=[kl], writes=[krs])
            sls = []
            for m in range(2):
                ps, pk = cens[m]
                xn, kxn = tf.get()
                S.op("dve", lambda e, ps=ps, rs=rs, xn=xn: e.tensor_tensor(out=xn[:], in0=ps[:, 0:BW], in1=rs[:], op=ALU.mult),
                     reads=[pk, krs], writes=[kxn])
                sl, ksl = tb.get()
                S.op("act", lambda e, xn=xn, sl=sl, m=m: e.activation(out=sl[:], in_=xn[:], func=AF.Silu,
                                                                      scale=C("ln_g", m), bias=C("ln_b", m)),
                     reads=[kxn, "cst"], writes=[ksl])
                sls.append((sl, ksl))
            for m in range(2):
                ps, pk = PS()
                for k in range(2):
                    sl, ksl = sls[k]
                    o = _CBF["pw"] + k * 256 + m * 128
                    S.op("pe", lambda e, ps=ps, sl=sl, o=o, k=k: e.matmul(ps[:, 0:BW], lhsT=cbf[:, o:o + 128], rhs=sl[:],
                                                                         start=(k == 0), stop=(k == 1)),
                         reads=["cbf", ksl], writes=[pk])
                yt, kyt = tb.get()
                S.op("act", lambda e, ps=ps, yt=yt: e.activation(out=yt[:], in_=ps[:, 0:BW], func=AF.Copy),
                     reads=[pk], writes=[kyt])
                S.op("sp", lambda e, yt=yt, m=m: e.dma_start(out=ya_out[2 + m][:, bcols(b)], in_=yt[:]), reads=[kyt],
                     writes=[("oy", 2 + m, b)], dma_sem=f"sy{kyt[1]}")
                extra_outs.append(("oy", 2 + m, b))

        for b in range(NB):
            g_rg = rg_group([0, 1, 2, 3], b, None)
            next(g_rg)
            g_cf = conformer_block(b)
            next(g_cf)
            run_gen(g_rg)
            next(g_cf)
            g_gt = gate_block(b, cg_w)
            next(g_gt)
            run_gen(g_cf)
            run_gen(g_gt)
        for t in range(4):
            w_issue()
        S.op("dve", lambda e: e.tensor_tensor(out=small[:, 52:56], in0=small[:, 40:44], in1=small[:, 0:4], op=ALU.mult),
             reads=[("rsum", t) for t in range(4)] + ["c1"], writes=["plog"])
        S.op("act", lambda e: e.activation(out=small[:, 52:56], in_=small[:, 52:56], func=AF.Exp),
             reads=["plog"], writes=["pval"])
        S.op("sp", lambda e: e.dma_start(out=out_d[:, 0:4], in_=small[:, 56:60]),
             reads=[("endst", t) for t in range(4)], writes=["o0"], dma_sem="st0")
        S.op("sp", lambda e: e.dma_start(out=out_d[:, 4:8], in_=small[:, 52:56]), reads=["pval"], writes=["o1"], dma_sem="st1")
        S.op("sp", lambda e: e.nop(), reads=["o0", "o1"] + extra_outs)
    else:
        for b in range(NB):
            run_gen(rg_group([0, 1], b, None))
            run_gen(rg_group([2, 3], b, None))
            load_h([b])

        for mt in range(KT):
            wt, wk = w_acquire()
            for b in range(NB):
                ps, pk = proj(wt, wk, w8, lambda k, b=b: (Y[:, k, bcols(b)], ("Y", k, b)), KT, b)
                S.op("dve", lambda e, ps=ps, mt=mt, b=b: e.tensor_tensor(out=h[:, mt, bcols(b)], in0=ps[:, 0:BW],
                                                                        in1=h[:, mt, bcols(b)], op=ALU.add),
                     reads=[pk, ("h", mt, b)], writes=[("h", mt, b)])
            w_issue()

        rms_to_A("g_mlp")
        for G in range(8):
            mo = (G % 2) * 4
            for j in range(4):
                wt, wk = w_acquire()
                for b in range(NB):
                    ps, pk = proj(wt, wk, w8, a_rhs(b), KT, b)
                    r, kr = tf.get()
                    S.op("act", lambda e, ps=ps, r=r: e.activation(out=r[:], in_=ps[:, 0:BW], func=AF.Relu),
                         reads=[pk], writes=[kr])
                    S.op("dve", lambda e, ps=ps, r=r, j=j, b=b, mo=mo: e.tensor_tensor(
                        out=Y[:, mo + j, bcols(b)], in0=ps[:, 0:BW], in1=r[:], op=ALU.mult),
                        reads=[pk, kr], writes=[("Y", mo + j, b)])
                w_issue()
            def down(wt, wk, q, b):
                for mm in range(2):
                    mt = 2 * q + mm
                    ps, pk = proj(wt, wk, lambda wt_, k, mm=mm: wt_[:, (k * 2 + mm) * 128:(k * 2 + mm + 1) * 128],
                                  lambda k, b=b, mo=mo: (Y[:, mo + k, bcols(b)], ("Y", mo + k, b)), 4, b)
                    S.op("dve", lambda e, ps=ps, mt=mt, b=b: e.tensor_tensor(out=h[:, mt, bcols(b)], in0=ps[:, 0:BW],
                                                                            in1=h[:, mt, bcols(b)], op=ALU.add),
                         reads=[pk, ("h", mt, b)], writes=[("h", mt, b)])

            if G < 7:
                for q in range(4):
                    wt, wk = w_acquire()
                    for b in range(NB):
                        down(wt, wk, q, b)
                    w_issue()
            else:
                wq = [w_acquire() for q in range(4)]
                for b in range(NB):
                    for q in range(4):
                        down(wq[q][0], wq[q][1], q, b)
                for q in range(4):
                    w_issue()

        finals = []
        if mode == "layer":
            out_p = out_d.rearrange("k p t -> p k t")
            for b in range(NB):
                S.op("sp", lambda e, b=b: e.dma_start(out=out_p[:, :, bcols(b)], in_=h[:, :, bcols(b)]),
                     reads=[("h", k, b) for k in range(KT)], writes=[("o", b)], dma_sem=f"st{b}")
                finals.append(("o", b))
        else:
            opool = tf
            for b in range(NB):
                cs = bcols(b)
                ps, pk = PS()
                for k in range(KT):
                    sq, sk = tb.get()
                    S.op("act", lambda e, k=k, cs=cs, sq=sq: e.activation(out=sq[:], in_=h[:, k, cs], func=AF.Square),
                         reads=[("h", k, b)], writes=[sk])
                    S.op("pe", lambda e, k=k, sq=sq, ps=ps: e.matmul(ps[:, 0:BW], lhsT=ones[:], rhs=sq[:],
                                                                     start=(k == 0), stop=(k == KT - 1)),
                         reads=["ones", sk], writes=[pk])
                l, lk = tf.get()
                S.op("act", lambda e, ps=ps, l=l: e.activation(out=l[:], in_=ps[:, 0:BW], func=AF.Ln, bias=EPS),
                     reads=[pk], writes=[lk])
                S.op("act", lambda e, ps=ps, l=l: e.activation(out=ps[:, 0:BW], in_=l[:], func=AF.Exp, scale=-0.5),
                     reads=[lk], writes=[pk])
                lo = PRE if b == 0 else 0
                for k in range(KT):
                    o, ok = opool.get()
                    S.op("dve", lambda e, k=k, cs=cs, ps=ps, o=o: e.scalar_tensor_tensor(
                        out=o[:], in0=h[:, k, cs], scalar=C("g_fin", k), in1=ps[:, 0:BW], op0=ALU.mult, op1=ALU.mult),
                        reads=[("h", k, b), pk, "cst"], writes=[ok])
                    d0 = b * BW + lo - PRE
                    S.op("sp", lambda e, k=k, o=o, lo=lo, d0=d0: e.dma_start(out=out_d[k][:, d0:d0 + BW - lo], in_=o[:, lo:BW]),
                         reads=[ok], writes=[("o", k, b)], dma_sem=f"sto{ok[1]}")
                    finals.append(("o", k, b))
        S.op("sp", lambda e: e.nop(), reads=finals)

    assert wstate["next_use"] == nblocks, (wstate, nblocks)
    with contextlib.ExitStack() as st:
        S.emit(nc, st)
    return nc


def _blk(w):
    return np.ascontiguousarray(w.reshape(8, 128, 128).transpose(1, 0, 2).reshape(128, 1024))


def _cx_blocks(w_in_l):
    return [_blk(w_in_l[:, 1280 + 128 * t:1280 + 128 * (t + 1)]) for t in range(4)]


def _pre_stream(w_in_l):
    col = lambda c0: _blk(w_in_l[:, c0:c0 + 128])
    blocks = [col(0), col(128)]
    blocks += _cx_blocks(w_in_l)
    blocks += [col(256), col(512), col(384), col(640)]
    blocks += [col(768 + 128 * t) for t in range(4)]
    return np.stack(blocks).astype(np.float32)


def _layer_stream(w_in_l, w_out_l, w_up_l, w_down_l):
    blocks = [_blk(w_out_l[:, 128 * m:128 * (m + 1)]) for m in range(8)]
    for G in range(8):
        for j in range(4):
            c0 = G * 512 + j * 128
            blocks.append(_blk(w_up_l[:, c0:c0 + 128]))
        for q in range(4):
            sub = w_down_l[G * 512:(G + 1) * 512, q * 256:(q + 1) * 256]
            blocks.append(np.ascontiguousarray(sub.reshape(4, 128, 2, 128).transpose(1, 0, 2, 3).reshape(128, 1024)))
    return np.stack(blocks).astype(np.float32)


def _pk(v, ntile):
    return np.ascontiguousarray(np.asarray(v, np.float32).reshape(ntile, 128).T)


def _consts(l, j, G, P):
    c = np.zeros((128, NCF), np.float32)

    def put(name, arr):
        arr = np.asarray(arr, np.float32).reshape(128, -1)
        c[:, _CST[name]:_CST[name] + arr.shape[1]] = arr

    put("g_mix", _pk(P["mix_norm_g"][l], 8))
    put("g_mlp", _pk(P["mlp_norm_g"][l], 8))
    put("g_fin", _pk(P["final_norm_g"], 8))
    put("pool_scale", _pk(P["pool_scale"][l], 2))
    wins = np.array([2, 4, 8, 16], np.float32)
    wpp = np.repeat(wins, 64)
    put("invw", _pk(1.0 / wpp, 2))
    put("invwm1", _pk(1.0 / wpp - 1.0, 2))
    put("himask", (np.arange(128) >= 64).astype(np.float32))
    put("dw_w", P["convb_dw_w"][l].T.reshape(2, 128, 31).transpose(1, 0, 2))
    put("dw_b", _pk(P["convb_dw_b"][l], 2))
    put("ln_g", _pk(P["convb_ln_g"][l], 2))
    put("ln_b", _pk(P["convb_ln_b"][l], 2))
    put("rgc_w", P["rg_conv_w"][l].T.reshape(4, 128, 4).transpose(1, 0, 2))
    put("rgc_b", _pk(P["rg_conv_b"][l], 4))
    put("b_a", _pk(P["rg_b_a"][l], 4))
    put("b_x", _pk(P["rg_b_x"][l], 4))
    put("lam", _pk(P["rg_lambda"][l], 4))
    sm = np.zeros((128, 32), np.float32)
    ratio = np.ones((128, 2, 32), np.float32)
    if j == 0:
        sm[:, 16:] = 1.0
        pos = np.arange(1, 17, dtype=np.float32)
        for t in range(2):
            wv = wpp[t * 128:(t + 1) * 128][:, None]
            ratio[:, t, 16:] = wv / np.minimum(pos[None, :], wv)
    put("scanmask", sm)
    put("ratio", ratio)
    pm = np.zeros((128, 8), np.float32)
    put("pm", pm)
    if G is not None:
        put("G", G)
    wg = np.zeros((2, 128, 128), np.float32)
    for g in range(4):
        t, o = divmod(g, 2)
        wg[t, o * 64:(o + 1) * 64, o * 64:(o + 1) * 64] = P["pool_w"][l][g]
    put("Wg", wg.transpose(1, 0, 2))
    cm = (np.eye(256, dtype=np.float32) - np.float32(1.0 / 256.0)).reshape(2, 128, 256)
    put("Cmat", cm.transpose(1, 0, 2))
    put("Jm", np.full((128, 128), 1.0 / 256.0, np.float32))
    return c


def _cbf(l, P):
    c = np.zeros((128, NCB), np.float32)
    c[:, 0:128] = np.eye(128, dtype=np.float32)
    c[:, 128:640] = P["convb_pw_w"][l].reshape(2, 128, 256).transpose(1, 0, 2).reshape(128, 512)
    for nm, key in (("Wa", "rg_w_a"), ("Wx", "rg_w_x")):
        bd = np.zeros((4, 128, 128), np.float32)
        for hd in range(8):
            t, o = divmod(hd, 2)
            bd[t, o * 64:(o + 1) * 64, o * 64:(o + 1) * 64] = P[key][l][hd]
        c[:, _CBF[nm]:_CBF[nm] + 512] = bd.transpose(1, 0, 2).reshape(128, 512)
    return c


_PROG = {}


def _prog(mode, nblocks):
    if (mode, nblocks) not in _PROG:
        _PROG[(mode, nblocks)] = build_program(mode, nblocks)
    return _PROG[(mode, nblocks)]


def kernel(**inputs):
    P = {k: np.asarray(v, np.float32) for k, v in inputs.items()}
    x = P["x"]
    B = x.shape[0]
    hT = []
    for r in range(NCORES):
        b, j = divmod(r, 4)
        seq = np.concatenate([P["meta_tokens"], x[b]], axis=0)
        s0 = 16 + MAIN * j
        if j == 0:
            tok = np.concatenate([np.zeros((16, D), np.float32), seq[0:16 + MAIN]], axis=0)
        else:
            tok = seq[s0 - PRE:s0 + MAIN]
        hT.append(np.ascontiguousarray(tok.T).reshape(KT, 128, NT))
    out = None
    for l in range(2):
        cbf = _cbf(l, P)
        wpre = _pre_stream(P["w_in"][l])
        ncA = _prog("pre", wpre.shape[0])
        mapsA = [{"hT": hT[r], "wstream": wpre, "cst": _consts(l, r % 4, None, P), "cbf": cbf} for r in range(NCORES)]
        resA = run_bass_kernel_spmd(ncA, mapsA, core_ids=list(range(NCORES)))
        G = np.stack([np.asarray(resA.results[r]["endp"], np.float32) for r in range(NCORES)], axis=1)
        mode = "layer" if l == 0 else "last"
        wst = _layer_stream(P["w_in"][l], P["w_out"][l], P["w_up"][l], P["w_down"][l])
        ncB = _prog(mode, wst.shape[0])
        mapsB = []
        for r in range(NCORES):
            b, j = divmod(r, 4)
            c = _consts(l, j, G.reshape(128, 64), P)
            pm = np.zeros((128, 8), np.float32)
            pm[:, 4 * b:4 * b + j] = 1.0
            c[:, _CST["pm"]:_CST["pm"] + 8] = pm
            mapsB.append({"hT": hT[r], "wstream": wst, "cst": c, "cbf": cbf,
                          "ya_in": np.asarray(resA.results[r]["ya_out"]), "arg_in": np.asarray(resA.results[r]["arg_out"]),
                          "brg_in": np.asarray(resA.results[r]["brg_out"]),
                          "gate_in": np.asarray(resA.results[r]["gate_out"])})
        resB = run_bass_kernel_spmd(ncB, mapsB, core_ids=list(range(NCORES)))
        if l == 0:
            hn = [np.asarray(resB.results[r]["hout"], np.float32) for r in range(NCORES)]
            hT = []
            for r in range(NCORES):
                b, j = divmod(r, 4)
                t = hn[r].copy()
                if j == 0:
                    t[:, :, 0:16] = 0.0
                else:
                    t[:, :, 0:PRE] = hn[r - 1][:, :, NT - PRE:NT]
                hT.append(t)
        else:
            out = np.zeros((B, 4 * MAIN, D), np.float32)
            for r in range(NCORES):
                b, j = divmod(r, 4)
                y = np.asarray(resB.results[r]["yout"], np.float32).reshape(D, MAIN)
                out[b, MAIN * j:MAIN * (j + 1), :] = y.T
    return out
```

```python
import contextlib
import numpy as np
import concourse.bass as bass
import concourse.mybir as mybir
from concourse.bass_utils import run_bass_kernel_spmd

F32 = mybir.dt.float32
BF16 = mybir.dt.bfloat16
AF = mybir.ActivationFunctionType
ALU = mybir.AluOpType

NCORES = 8
D = 1024
KT = 8
PRE = 32
MAIN = 2048
NT = PRE + MAIN
NB = 5
BW = NT // NB
PADL = 32
EPS = 1e-6
NRING = 6
GELU_K = 1.5957691216057308

_CST = {}
_off = 0
for _n, _w in [("g_mix", 8), ("g_mlp", 8), ("g_fin", 8), ("pool_scale", 2), ("invw", 2), ("invwm1", 2),
               ("himask", 1), ("dw_w", 62), ("dw_b", 2), ("ln_g", 2), ("ln_b", 2), ("rgc_w", 16),
               ("rgc_b", 4), ("b_a", 4), ("b_x", 4), ("lam", 4), ("scanmask", 32), ("ratio", 64),
               ("pm", 8), ("G", 64), ("Wg", 256), ("Cmat", 512), ("Jm", 128)]:
    _CST[_n] = _off
    _off += _w
NCF = _off
_CBF = {"ident": 0, "pw": 128, "Wa": 640, "Wx": 1152}
NCB = 1664

ENGS = ("pe", "act", "dve", "pool", "sp")


class Op:
    __slots__ = ("idx", "eng", "fn", "deps", "dma_sem", "count", "signal", "waits", "inc")

    def __init__(self, idx, eng, fn, dma_sem, inc):
        self.idx = idx
        self.eng = eng
        self.fn = fn
        self.deps = {}
        self.dma_sem = dma_sem
        self.count = None
        self.signal = False
        self.waits = []
        self.inc = inc


class Sched:
    def __init__(self):
        self.ops = []
        self.last_writer = {}
        self.readers = {}

    def op(self, eng, fn, reads=(), writes=(), dma_sem=None, inc=16):
        o = Op(len(self.ops), eng, fn, dma_sem, inc)
        for k in reads:
            w = self.last_writer.get(k)
            if w is not None:
                o.deps[w] = "raw"
            if isinstance(k, tuple) and k[0] == "ps":
                for r in self.readers.get(k, ()):
                    if self.ops[r].eng != eng and r not in o.deps:
                        o.deps[r] = "rar"
        for k in writes:
            w = self.last_writer.get(k)
            if w is not None and w not in o.deps:
                o.deps[w] = "waw"
            for r in self.readers.get(k, ()):
                if r not in o.deps:
                    o.deps[r] = "war"
        for k in reads:
            self.readers.setdefault(k, []).append(o.idx)
        for k in writes:
            self.last_writer[k] = o.idx
            self.readers[k] = []
        self.ops.append(o)
        return o

    def emit(self, nc, st):
        ops = self.ops
        for o in ops:
            for d, kind in o.deps.items():
                p = ops[d]
                if p.dma_sem is not None or o.dma_sem is not None or p.eng != o.eng:
                    need = True
                else:
                    need = (o.eng != "pe")
                if need:
                    o.waits.append(d)
                    p.signal = True
        cnt = {e: 0 for e in ENGS}
        dcnt = {}
        for o in ops:
            if o.dma_sem is not None:
                dcnt[o.dma_sem] = dcnt.get(o.dma_sem, 0) + o.inc
                o.count = dcnt[o.dma_sem]
            elif o.signal:
                cnt[o.eng] += 1
                o.count = cnt[o.eng]
        sems = {e: st.enter_context(nc.semaphore("s_" + e)) for e in ENGS}
        dsems = {k: st.enter_context(nc.semaphore("d_" + str(k))) for k in sorted(dcnt)}
        block = st.enter_context(nc.Block())
        queues = {e: [o for o in ops if o.eng == e] for e in ENGS}

        def run_queue(eng_name, engine):
            waited = {}
            for o in queues[eng_name]:
                need = {}
                for d in o.waits:
                    p = ops[d]
                    key = ("d", p.dma_sem) if p.dma_sem is not None else ("e", p.eng)
                    if p.count > need.get(key, 0):
                        need[key] = p.count
                for key, val in need.items():
                    if waited.get(key, 0) >= val:
                        continue
                    waited[key] = val
                    engine.wait_ge(dsems[key[1]] if key[0] == "d" else sems[key[1]], val)
                ins = o.fn(engine)
                if o.dma_sem is not None:
                    ins.then_inc(dsems[o.dma_sem], o.inc)
                elif o.signal:
                    ins.then_inc(sems[o.eng], 1)

        @block.tensor
        def _(e):
            run_queue("pe", e)

        @block.scalar
        def _(e):
            run_queue("act", e)

        @block.vector
        def _(e):
            run_queue("dve", e)

        @block.gpsimd
        def _(e):
            run_queue("pool", e)

        @block.sync
        def _(e):
            run_queue("sp", e)


class Pool:
    def __init__(self, nc, name, shape, dtype, n):
        self.t = [nc.alloc_sbuf_tensor(f"{name}{i}", shape, dtype) for i in range(n)]
        self.name = name
        self.i = 0

    def get(self):
        i = self.i % len(self.t)
        self.i += 1
        return self.t[i], (self.name, i)


def build_program(mode, nblocks):
    nc = bass.Bass("TRN2", target_bir_lowering=False)
    hT = nc.dram_tensor("hT", [KT, 128, NT], F32, kind="ExternalInput").ap()
    wstream = nc.dram_tensor("wstream", [nblocks, 128, 1024], F32, kind="ExternalInput").ap()
    cst_d = nc.dram_tensor("cst", [128, NCF], F32, kind="ExternalInput").ap()
    cbf_d = nc.dram_tensor("cbf", [128, NCB], F32, kind="ExternalInput").ap()
    if mode == "pre":
        out_d = nc.dram_tensor("endp", [128, 8], F32, kind="ExternalOutput").ap()
        gate_out = nc.dram_tensor("gate_out", [4, 128, NT], F32, kind="ExternalOutput").ap()
        ya_out = nc.dram_tensor("ya_out", [4, 128, NT], BF16, kind="ExternalOutput").ap()
        arg_out = nc.dram_tensor("arg_out", [4, 128, NT], F32, kind="ExternalOutput").ap()
        brg_out = nc.dram_tensor("brg_out", [4, 128, NT], F32, kind="ExternalOutput").ap()
    else:
        gate_in = nc.dram_tensor("gate_in", [4, 128, NT], F32, kind="ExternalInput").ap()
        ya_in = nc.dram_tensor("ya_in", [4, 128, NT], BF16, kind="ExternalInput").ap()
        arg_in = nc.dram_tensor("arg_in", [4, 128, NT], F32, kind="ExternalInput").ap()
        brg_in = nc.dram_tensor("brg_in", [4, 128, NT], F32, kind="ExternalInput").ap()
    if mode == "pre":
        pass
    elif mode == "layer":
        out_d = nc.dram_tensor("hout", [KT, 128, NT], F32, kind="ExternalOutput").ap()
    else:
        out_d = nc.dram_tensor("yout", [KT, 128, MAIN], F32, kind="ExternalOutput").ap()

    S = Sched()
    h = nc.alloc_sbuf_tensor("h", [128, KT, NT], F32)
    A = nc.alloc_sbuf_tensor("A", [128, KT, NT], BF16)
    Y = nc.alloc_sbuf_tensor("Y", [128, KT, NT], BF16) if mode != "pre" else None
    NSTG = 6 if mode == "pre" else 4
    STG = nc.alloc_sbuf_tensor("STG", [128, NSTG, PADL + NT], BF16)
    ring = [nc.alloc_sbuf_tensor(f"wr{i}", [128, 1024], BF16) for i in range(NRING)]
    cst = nc.alloc_sbuf_tensor("cst_s", [128, NCF], F32)
    cbf = nc.alloc_sbuf_tensor("cbf_s", [128, NCB], BF16)
    pmat = nc.alloc_sbuf_tensor("pmat", [128, 8, 128], BF16)
    rgd = nc.alloc_sbuf_tensor("rgd", [128, 16, 128], BF16)
    ones = nc.alloc_sbuf_tensor("ones", [128, 128], BF16)
    dres = nc.alloc_sbuf_tensor("dres", [128, 62, 128], BF16) if mode == "pre" else None
    small = nc.alloc_sbuf_tensor("small", [128, 64], F32)
    tf = Pool(nc, "tf", [128, BW], F32, 17 if mode == "pre" else 12)
    ths = Pool(nc, "ths", [128, BW], F32, 5)
    tb = Pool(nc, "tb", [128, BW], BF16, 8 if mode == "pre" else 6)
    ps_t = [nc.alloc_psum_tensor(f"ps{i}", [128, 512], F32) for i in range(8)]
    ps_i = [0]

    def run_gen(g):
        for _ in g:
            pass

    def PS():
        i = ps_i[0] % 8
        ps_i[0] += 1
        return ps_t[i], ("ps", i)

    def C(name, j=0, w=1):
        o = _CST[name] + j
        return cst[:, o:o + w]

    def bcols(b):
        return slice(b * BW, (b + 1) * BW)

    S.op("sp", lambda e: e.dma_start(out=cst[:], in_=cst_d), writes=["cst"], dma_sem="ldc")
    S.op("pool", lambda e: e.dma_start(out=cbf[:], in_=cbf_d), writes=["cbf"], dma_sem="ldb")
    hT_p = hT.rearrange("k p t -> p k t")

    def load_h(blocks):
        for b in blocks:
            S.op("sp", lambda e, b=b: e.dma_start(out=h[:, :, bcols(b)], in_=hT_p[:, :, bcols(b)]),
                 writes=[("h", k, b) for k in range(KT)], dma_sem=f"ldh{b}")

    if mode == "pre":
        load_h(range(NB))

    wstate = {"next_dma": 0, "next_use": 0}

    def w_issue():
        i = wstate["next_dma"]
        if i >= nblocks:
            return
        wstate["next_dma"] += 1
        s = i % NRING
        S.op("pool", lambda e, i=i, s=s: e.dma_start(out=ring[s][:], in_=wstream[i]),
             writes=[("w", s)], dma_sem=f"w{s}")

    for _ in range(NRING):
        w_issue()

    def w_acquire():
        i = wstate["next_use"]
        wstate["next_use"] += 1
        s = i % NRING
        return ring[s], ("w", s)

    S.op("dve", lambda e: e.memset(ones[:], 1.0 / 1024.0), writes=["ones"])
    for s4 in range(NSTG):
        S.op("dve", lambda e, s4=s4: e.memset(STG[:, s4, 0:PADL], 0.0), writes=[("stgpad", s4)])
    S.op("act", lambda e: e.activation(out=small[:, 12:16], in_=C("lam", 0, 4), func=AF.Exp, scale=-1.0),
         reads=["cst"], writes=["sm_t"])
    S.op("act", lambda e: e.activation(out=small[:, 16:20], in_=small[:, 12:16], func=AF.Ln, bias=1.0),
         reads=["sm_t"], writes=["sm_t2"])
    S.op("dve", lambda e: e.tensor_scalar(out=small[:, 0:4], in0=small[:, 16:20], scalar1=-8.0, scalar2=None,
                                          op0=ALU.mult), reads=["sm_t2"], writes=["c1"])
    S.op("dve", lambda e: e.tensor_scalar(out=small[:, 4:8], in0=small[:, 16:20], scalar1=-16.0, scalar2=None,
                                          op0=ALU.mult), reads=["sm_t2"], writes=["c2"])
    S.op("dve", lambda e: e.memset(small[:, 8:12], 0.0), writes=["carry"])
    if mode != "pre":
        for r in range(8):
            gE = C("G", r * 8, 4)
            gP = C("G", r * 8 + 4, 4)
            S.op("dve", lambda e, gP=gP: e.tensor_tensor(out=small[:, 20:24], in0=gP, in1=small[:, 8:12], op=ALU.mult),
                 reads=["cst", "carry"], writes=["cc_t"])
            S.op("dve", lambda e, gE=gE: e.tensor_tensor(out=small[:, 24:28], in0=small[:, 20:24], in1=gE, op=ALU.add),
                 reads=["cc_t", "cst"], writes=["cc_u"])
            S.op("dve", lambda e: e.tensor_tensor(out=small[:, 28:32], in0=small[:, 24:28], in1=small[:, 8:12],
                                                  op=ALU.subtract), reads=["cc_u", "carry"], writes=["cc_v"])
            S.op("dve", lambda e, r=r: e.scalar_tensor_tensor(out=small[:, 32:36], in0=small[:, 28:32],
                                                              scalar=C("pm", r), in1=small[:, 8:12],
                                                              op0=ALU.mult, op1=ALU.add),
                 reads=["cc_v", "carry", "cst"], writes=["cc_w"])
            S.op("dve", lambda e: e.tensor_copy(out=small[:, 8:12], in_=small[:, 32:36]), reads=["cc_w"], writes=["carry"])
    for t in range(4):
        for k in range(4):
            S.op("pool", lambda e, t=t, k=k: e.tensor_scalar(out=rgd[:, t * 4 + k, :], in0=cbf[:, 0:128],
                                                             scalar1=C("rgc_w", t * 4 + k), scalar2=1.0,
                                                             op0=ALU.mult, op1=ALU.mult),
                 reads=["cst", "cbf"], writes=[("rgd", t)])
    if mode == "pre":
        for j in range(62):
            S.op("pool", lambda e, j=j: e.tensor_scalar(out=dres[:, j, :], in0=cbf[:, 0:128], scalar1=C("dw_w", j), scalar2=1.0,
                                                        op0=ALU.mult, op1=ALU.mult),
                 reads=["cst", "cbf"], writes=["dres"])
        for t in range(2):
            wg = cst[:, _CST["Wg"] + t * 128:_CST["Wg"] + (t + 1) * 128]
            S.op("dve", lambda e, t=t, wg=wg: e.tensor_copy(out=pmat[:, t, :], in_=wg), reads=["cst"], writes=[("pm_", t)])
            S.op("dve", lambda e, t=t, wg=wg: e.tensor_scalar(out=pmat[:, 2 + t, :], in0=wg, scalar1=C("invwm1", t),
                                                              scalar2=None, op0=ALU.mult), reads=["cst"], writes=[("pm_", t)])
            S.op("dve", lambda e, t=t, wg=wg: e.tensor_scalar(out=pmat[:, 4 + t, :], in0=wg, scalar1=C("invw", t),
                                                              scalar2=None, op0=ALU.mult), reads=["cst"], writes=[("pm_", t)])
            S.op("dve", lambda e, t=t, wg=wg: e.tensor_scalar(out=pmat[:, 6 + t, :], in0=wg, scalar1=C("invw", t),
                                                              scalar2=C("himask"), op0=ALU.mult, op1=ALU.mult),
                 reads=["cst"], writes=[("pm_", t)])

    def rms_to_A(gname):
        for b in range(NB):
            cs = bcols(b)
            ps, pk = PS()
            for k in range(KT):
                sq, sk = tb.get()
                S.op("act", lambda e, k=k, cs=cs, sq=sq: e.activation(out=sq[:], in_=h[:, k, cs], func=AF.Square),
                     reads=[("h", k, b)], writes=[sk])
                S.op("pe", lambda e, k=k, sq=sq, ps=ps: e.matmul(ps[:, 0:BW], lhsT=ones[:], rhs=sq[:],
                                                                 start=(k == 0), stop=(k == KT - 1)),
                     reads=["ones", sk], writes=[pk])
            l, lk = tf.get()
            S.op("act", lambda e, ps=ps, l=l: e.activation(out=l[:], in_=ps[:, 0:BW], func=AF.Ln, bias=EPS),
                 reads=[pk], writes=[lk])
            S.op("act", lambda e, ps=ps, l=l: e.activation(out=ps[:, 0:BW], in_=l[:], func=AF.Exp, scale=-0.5),
                 reads=[lk], writes=[pk])
            for k in range(KT):
                S.op("dve", lambda e, k=k, cs=cs, ps=ps: e.scalar_tensor_tensor(
                    out=A[:, k, cs], in0=h[:, k, cs], scalar=C(gname, k), in1=ps[:, 0:BW],
                    op0=ALU.mult, op1=ALU.mult),
                    reads=[("h", k, b), pk, "cst"], writes=[("A", k, b)])

    def proj(wt, wk, wsel, rhs_fn, nk, b):
        ps, pk = PS()
        for k in range(nk):
            rap, rkey = rhs_fn(k)
            S.op("pe", lambda e, k=k, rap=rap, ps=ps: e.matmul(ps[:, 0:BW], lhsT=wsel(wt, k), rhs=rap,
                                                               start=(k == 0), stop=(k == nk - 1)),
                 reads=[wk, rkey], writes=[pk])
        return ps, pk

    def w8(wt, k):
        return wt[:, k * 128:(k + 1) * 128]

    def a_rhs(b):
        return lambda k: (A[:, k, bcols(b)], ("A", k, b))

    def stg_cols(slot, b, shift):
        c0 = PADL + b * BW - shift
        return STG[:, slot, c0:c0 + BW]

    def stg_keys(slot, b):
        ks = [("stg", slot, b)]
        ks.append(("stg", slot, b - 1) if b > 0 else ("stgpad", slot))
        return ks

    def inproj_to_stg(slot):
        wt, wk = w_acquire()
        for b in range(NB):
            ps, pk = proj(wt, wk, w8, a_rhs(b), KT, b)
            S.op("act", lambda e, ps=ps, b=b: e.activation(out=STG[:, slot, PADL + b * BW:PADL + (b + 1) * BW],
                                                           in_=ps[:, 0:BW], func=AF.Copy),
                 reads=[pk], writes=[("stg", slot, b)])
        w_issue()

    extra_outs = []
    if mode == "pre":
        rms_to_A("g_mix")
    else:
        for t in range(4):
            S.op("sp", lambda e, t=t: e.dma_start(out=Y[:, t, :], in_=ya_in[t]),
                 writes=[("Y", t, b) for b in range(NB)], dma_sem=f"ldy{t}")

    if mode == "pre":
        inproj_to_stg(0)
        inproj_to_stg(1)
        def pool_tile(t):
            wlo, whi = (2, 4) if t == 0 else (8, 16)
            for b in range(NB):
                ps, pk = PS()
                for k in range(whi):
                    mat = pmat[:, 2 + t, :] if k == 0 else (pmat[:, 4 + t, :] if k < wlo else pmat[:, 6 + t, :])
                    S.op("pe", lambda e, k=k, mat=mat, ps=ps, b=b: e.matmul(ps[:, 0:BW], lhsT=mat, rhs=stg_cols(t, b, k),
                                                                           start=(k == 0), stop=(k == whi - 1)),
                         reads=[("pm_", t)] + stg_keys(t, b), writes=[pk])
                yt, kyt = tb.get()
                S.op("act", lambda e, ps=ps, yt=yt: e.activation(out=yt[:], in_=ps[:, 0:BW],
                                                                 func=AF.Identity, scale=C("pool_scale", t)),
                     reads=[pk, "cst"], writes=[kyt])
                if b == 0:
                    y0 = (yt, kyt)
                else:
                    S.op("sp", lambda e, yt=yt, b=b: e.dma_start(out=ya_out[t][:, bcols(b)], in_=yt[:]), reads=[kyt],
                         writes=[("oy", t, b)], dma_sem=f"sy{kyt[1]}")
                    extra_outs.append(("oy", t, b))
            psS, pkS = PS()
            for k in range(whi):
                mat = pmat[:, 4 + t, :] if k < wlo else pmat[:, 6 + t, :]
                S.op("pe", lambda e, k=k, mat=mat, psS=psS: e.matmul(psS[:, 0:PRE], lhsT=mat,
                                                                    rhs=STG[:, t, PADL - k:PADL - k + PRE],
                                                                    start=(k == 0), stop=(k == whi - 1)),
                     reads=[("pm_", t)] + stg_keys(t, 0), writes=[pkS])
            psX, pkX = PS()
            S.op("pe", lambda e, psX=psX: e.matmul(psX[:, 0:PRE], lhsT=pmat[:, t, :], rhs=STG[:, t, PADL:PADL + PRE],
                                                   start=True, stop=True),
                 reads=[("pm_", t)] + stg_keys(t, 0), writes=[pkX])
            t1, k1 = tf.get()
            S.op("dve", lambda e, psS=psS, t1=t1: e.tensor_tensor(out=t1[:, 0:PRE], in0=psS[:, 0:PRE],
                                                                  in1=C("ratio", t * 32, 32), op=ALU.mult),
                 reads=[pkS, "cst"], writes=[k1])
            t2, k2 = tf.get()
            S.op("dve", lambda e, psX=psX, t1=t1, t2=t2: e.tensor_tensor(out=t2[:, 0:PRE], in0=t1[:, 0:PRE],
                                                                         in1=psX[:, 0:PRE], op=ALU.subtract),
                 reads=[pkX, k1], writes=[k2])
            yt, kyt = y0
            S.op("act", lambda e, t2=t2, yt=yt: e.activation(out=yt[:, 0:PRE], in_=t2[:, 0:PRE], func=AF.Identity,
                                                             scale=C("pool_scale", t)),
                 reads=[k2, "cst"], writes=[kyt])
            S.op("sp", lambda e, yt=yt: e.dma_start(out=ya_out[t][:, bcols(0)], in_=yt[:]), reads=[kyt],
                 writes=[("oy", t, 0)], dma_sem=f"sy{kyt[1]}")
            extra_outs.append(("oy", t, 0))

        pool_tile(0)
        pool_tile(1)

    if mode == "pre":
        S.op("dve", lambda e: e.memset(small[:, 40:44], 0.0), writes=[("rsum", t) for t in range(4)])
    rg_slots = [2, 3, 2, 3] if mode != "pre" else [0, 1, 2, 3]
    if mode == "pre":
        for t in range(4):
            inproj_to_stg(rg_slots[t])
    else:
        pass

    rg_prev = {}

    def rg_group(tiles, b, cgws):
        U = {t: {} for t in tiles}
        if mode != "pre":
            for t in tiles:
                a, ka = tf.get()
                bb, kb = tf.get()
                S.op("sp", lambda e, a=a, t=t: e.dma_start(out=a[:], in_=arg_in[t][:, bcols(b)]), writes=[ka],
                     dma_sem=f"so{ka[1]}")
                S.op("sp", lambda e, bb=bb, t=t: e.dma_start(out=bb[:], in_=brg_in[t][:, bcols(b)]), writes=[kb],
                     dma_sem=f"so{kb[1]}")
                U[t].update(a=a, ka=ka, xc=bb, kxc=kb)
        else:
            for t in tiles:
                slot = rg_slots[t]
                ps_c, pk_c = PS()
                for k in range(4):
                    S.op("pe", lambda e, k=k, ps_c=ps_c, t=t, slot=slot: e.matmul(
                        ps_c[:, 0:BW], lhsT=rgd[:, t * 4 + k, :], rhs=stg_cols(slot, b, 3 - k),
                        start=(k == 0), stop=(k == 3)),
                        reads=[("rgd", t)] + stg_keys(slot, b), writes=[pk_c])
                U[t].update(ps_c=ps_c, pk_c=pk_c)
            yield
            for t in tiles:
                u = U[t]
                xc, kxc = tf.get()
                xcb, kxcb = tb.get()
                S.op("dve", lambda e, ps_c=u["ps_c"], xc=xc, t=t: e.tensor_scalar(out=xc[:], in0=ps_c[:, 0:BW], scalar1=C("rgc_b", t),
                                                                                 scalar2=None, op0=ALU.add),
                     reads=[u["pk_c"], "cst"], writes=[kxc])
                S.op("dve", lambda e, xc=xc, xcb=xcb: e.tensor_copy(out=xcb[:], in_=xc[:]), reads=[kxc], writes=[kxcb])
                u.update(xc=xc, kxc=kxc, xcb=xcb, kxcb=kxcb)
            for t in tiles:
                u = U[t]
                ps_a, pk_a = PS()
                S.op("pe", lambda e, ps_a=ps_a, xcb=u["xcb"], t=t: e.matmul(
                    ps_a[:, 0:BW], lhsT=cbf[:, _CBF["Wa"] + t * 128:_CBF["Wa"] + (t + 1) * 128], rhs=xcb[:], start=True, stop=True),
                    reads=["cbf", u["kxcb"]], writes=[pk_a])
                ps_x, pk_x = PS()
                S.op("pe", lambda e, ps_x=ps_x, xcb=u["xcb"], t=t: e.matmul(
                    ps_x[:, 0:BW], lhsT=cbf[:, _CBF["Wx"] + t * 128:_CBF["Wx"] + (t + 1) * 128], rhs=xcb[:], start=True, stop=True),
                    reads=["cbf", u["kxcb"]], writes=[pk_x])
                u.update(ps_a=ps_a, pk_a=pk_a, ps_x=ps_x, pk_x=pk_x)
            for t in tiles:
                u = U[t]
                r, kr = tf.get()
                if mode == "pre":
                    lo = PRE if b == 0 else 0
                    if b == 0:
                        S.op("act", lambda e, ps_a=u["ps_a"], r=r, t=t: e.activation(out=r[:, 0:PRE], in_=ps_a[:, 0:PRE],
                                                                                    func=AF.Sigmoid, bias=C("b_a", t)),
                             reads=[u["pk_a"], "cst"], writes=[kr])
                    S.op("act", lambda e, ps_a=u["ps_a"], r=r, lo=lo, t=t: e.activation(
                        out=r[:, lo:BW], in_=ps_a[:, lo:BW], func=AF.Sigmoid, bias=C("b_a", t),
                        accum_out=small[:, 44 + t:45 + t]),
                        reads=[u["pk_a"], "cst"], writes=[kr, ("racc", t)])
                    S.op("dve", lambda e, t=t: e.tensor_tensor(out=small[:, 40 + t:41 + t], in0=small[:, 40 + t:41 + t],
                                                               in1=small[:, 44 + t:45 + t], op=ALU.add),
                         reads=[("racc", t), ("rsum", t)], writes=[("rsum", t)])
                else:
                    S.op("act", lambda e, ps_a=u["ps_a"], r=r, t=t: e.activation(out=r[:], in_=ps_a[:, 0:BW], func=AF.Sigmoid,
                                                                                bias=C("b_a", t)),
                         reads=[u["pk_a"], "cst"], writes=[kr])
                u.update(r=r, kr=kr)
            for t in tiles:
                u = U[t]
                a, ka = tf.get()
                S.op("act", lambda e, r=u["r"], a=a, t=t: e.activation(out=a[:], in_=r[:], func=AF.Exp, scale=small[:, t:t + 1]),
                     reads=[u["kr"], "c1"], writes=[ka])
                u.update(a=a, ka=ka)
            for t in tiles:
                u = U[t]
                S.op("dve", lambda e, r=u["r"], a=u["a"]: e.tensor_tensor(out=r[:], in0=a[:], in1=a[:], op=ALU.mult),
                     reads=[u["ka"], u["kr"]], writes=[u["kr"]])
            for t in tiles:
                u = U[t]
                S.op("act", lambda e, ps_a=u["ps_a"], ps_x=u["ps_x"], t=t: e.activation(
                    out=ps_a[:, 0:BW], in_=ps_x[:, 0:BW], func=AF.Sigmoid, bias=C("b_x", t)),
                    reads=[u["pk_x"], "cst"], writes=[u["pk_a"]])
            for t in tiles:
                u = U[t]
                S.op("act", lambda e, r=u["r"], ps_x=u["ps_x"]: e.activation(out=ps_x[:, 0:BW], in_=r[:], func=AF.Sqrt,
                                                                             scale=-1.0, bias=1.0),
                     reads=[u["kr"]], writes=[u["pk_x"]])
            for t in tiles:
                u = U[t]
                S.op("dve", lambda e, ps_a=u["ps_a"], xc=u["xc"]: e.tensor_tensor(out=xc[:], in0=ps_a[:, 0:BW], in1=xc[:], op=ALU.mult),
                     reads=[u["pk_a"], u["kxc"]], writes=[u["kxc"]])
            for t in tiles:
                u = U[t]
                S.op("dve", lambda e, ps_x=u["ps_x"], xc=u["xc"]: e.tensor_tensor(out=xc[:], in0=ps_x[:, 0:BW], in1=xc[:], op=ALU.mult),
                     reads=[u["pk_x"], u["kxc"]], writes=[u["kxc"]])
        if mode == "pre":
            if b == 0:
                for t in tiles:
                    u = U[t]
                    S.op("dve", lambda e, bb=u["xc"]: e.tensor_tensor(out=bb[:, 0:PRE], in0=bb[:, 0:PRE], in1=C("scanmask", 0, 32),
                                                                      op=ALU.mult), reads=[u["kxc"], "cst"], writes=[u["kxc"]])
            for t in tiles:
                u = U[t]
                S.op("sp", lambda e, a=u["a"], t=t: e.dma_start(out=arg_out[t][:, bcols(b)], in_=a[:]), reads=[u["ka"]],
                     writes=[("oar", t, b)], dma_sem=f"so{u['ka'][1]}")
                S.op("sp", lambda e, bb=u["xc"], t=t: e.dma_start(out=brg_out[t][:, bcols(b)], in_=bb[:]), reads=[u["kxc"]],
                     writes=[("obr", t, b)], dma_sem=f"so{u['kxc'][1]}")
                extra_outs.append(("oar", t, b))
                extra_outs.append(("obr", t, b))
        if b == 0:
            if mode != "pre":
                for t in tiles:
                    u = U[t]
                    S.op("dve", lambda e, bb=u["xc"], a=u["a"], t=t: e.scalar_tensor_tensor(
                        out=bb[:, PRE:PRE + 1], in0=a[:, PRE:PRE + 1], scalar=small[:, 8 + t:9 + t],
                        in1=bb[:, PRE:PRE + 1], op0=ALU.mult, op1=ALU.add),
                        reads=[u["kxc"], u["ka"], "carry"], writes=[u["kxc"]])
        for t in tiles:
            u = U[t]
            hs, khs = ths.get()
            if t not in rg_prev:
                S.op("dve", lambda e, hs=hs, a=u["a"], bb=u["xc"]: e.tensor_tensor_scan(
                    out=hs[:], data0=a[:], data1=bb[:], initial=0.0, op0=ALU.mult, op1=ALU.add),
                    reads=[u["ka"], u["kxc"]], writes=[khs])
            else:
                ph, pkh = rg_prev[t]
                S.op("dve", lambda e, hs=hs, a=u["a"], bb=u["xc"], ph=ph: e.tensor_tensor_scan(
                    out=hs[:], data0=a[:], data1=bb[:], initial=ph[:, BW - 1:BW], op0=ALU.mult, op1=ALU.add),
                    reads=[u["ka"], u["kxc"], pkh], writes=[khs])
            rg_prev[t] = (hs, khs)
            u.update(hs=hs, khs=khs)
        if mode == "pre":
            if b == NB - 1:
                for t in tiles:
                    S.op("dve", lambda e, hs=U[t]["hs"], t=t: e.tensor_copy(out=small[:, 56 + t:57 + t], in_=hs[:, BW - 1:BW]),
                         reads=[U[t]["khs"]], writes=[("endst", t)])
            return
        for t in tiles:
            u = U[t]
            g, kg = tf.get()
            S.op("sp", lambda e, g=g, t=t: e.dma_start(out=g[:], in_=gate_in[t][:, bcols(b)]), writes=[kg], dma_sem=f"so{kg[1]}")
            S.op("pool", lambda e, g=g, hs=u["hs"], t=t: e.tensor_tensor(out=Y[:, 4 + t, bcols(b)], in0=g[:], in1=hs[:], op=ALU.mult),
                 reads=[kg, u["khs"]], writes=[("Y", 4 + t, b)])

    def gate_block(b, cgws):
        U = {t: {} for t in range(4)}
        for t in range(4):
            wt, wk = cgws[t]
            ps_g, pk_g = proj(wt, wk, w8, a_rhs(b), KT, b)
            U[t].update(ps_g=ps_g, pk_g=pk_g)
        yield
        for t in range(4):
            u = U[t]
            x2, kx2 = tf.get()
            S.op("act", lambda e, ps_g=u["ps_g"], x2=x2: e.activation(out=x2[:], in_=ps_g[:, 0:BW], func=AF.Square),
                 reads=[u["pk_g"]], writes=[kx2])
            u.update(x2=x2, kx2=kx2)
        for t in range(4):
            u = U[t]
            S.op("dve", lambda e, x2=u["x2"]: e.tensor_scalar(out=x2[:], in0=x2[:], scalar1=0.044715 * GELU_K, scalar2=GELU_K,
                                                              op0=ALU.mult, op1=ALU.add), reads=[u["kx2"]], writes=[u["kx2"]])
        for t in range(4):
            u = U[t]
            S.op("dve", lambda e, ps_g=u["ps_g"], x2=u["x2"]: e.tensor_tensor(out=x2[:], in0=ps_g[:, 0:BW], in1=x2[:], op=ALU.mult),
                 reads=[u["pk_g"], u["kx2"]], writes=[u["kx2"]])
        for t in range(4):
            u = U[t]
            S.op("act", lambda e, q=u["x2"]: e.activation(out=q[:], in_=q[:], func=AF.Sigmoid),
                 reads=[u["kx2"]], writes=[u["kx2"]])
        for t in range(4):
            u = U[t]
            S.op("dve", lambda e, ps_g=u["ps_g"], q=u["x2"]: e.tensor_tensor(out=q[:], in0=ps_g[:, 0:BW], in1=q[:], op=ALU.mult),
                 reads=[u["pk_g"], u["kx2"]], writes=[u["kx2"]])
            S.op("sp", lambda e, q=u["x2"], t=t: e.dma_start(out=gate_out[t][:, bcols(b)], in_=q[:]), reads=[u["kx2"]],
                 writes=[("og", t, b)], dma_sem=f"so{u['kx2'][1]}")
            extra_outs.append(("og", t, b))

    if mode == "pre":
        for m in range(2):
            wv, kv = w_acquire()
            wg_, kg_ = w_acquire()
            for b in range(NB):
                psv, pkv = proj(wv, kv, w8, a_rhs(b), KT, b)
                psg, pkg = proj(wg_, kg_, w8, a_rhs(b), KT, b)
                sg, ksg = tf.get()
                S.op("act", lambda e, psg=psg, sg=sg: e.activation(out=sg[:], in_=psg[:, 0:BW], func=AF.Sigmoid),
                     reads=[pkg], writes=[ksg])
                S.op("dve", lambda e, psv=psv, sg=sg, b=b, m=m: e.tensor_tensor(
                    out=STG[:, 4 + m, PADL + b * BW:PADL + (b + 1) * BW], in0=psv[:, 0:BW], in1=sg[:], op=ALU.mult),
                    reads=[pkv, ksg], writes=[("stg", 4 + m, b)])
            w_issue()
            w_issue()
        cg_w = [w_acquire() for t in range(4)]

        dg_i = [0]

        def conformer_block(b):
            cen_in = []
            for m in range(2):
                ps, pk = PS()
                for k in range(31):
                    S.op("pe", lambda e, m=m, k=k, ps=ps: e.matmul(ps[:, 0:BW], lhsT=dres[:, m * 31 + k, :],
                                                                  rhs=stg_cols(4 + m, b, 30 - k),
                                                                  start=(k == 0), stop=(k == 30)),
                         reads=["dres"] + stg_keys(4 + m, b), writes=[pk])
                c, kc = tf.get()
                S.op("act", lambda e, ps=ps, c=c, m=m: e.activation(out=c[:], in_=ps[:, 0:BW], func=AF.Identity,
                                                                    bias=C("dw_b", m)),
                     reads=[pk, "cst"], writes=[kc])
                cen_in.append((c, kc))
            yield
            cens = []
            for m in range(2):
                ps, pk = PS()
                for k in range(2):
                    c, kc = cen_in[k]
                    o = _CST["Cmat"] + k * 256 + m * 128
                    S.op("pe", lambda e, ps=ps, c=c, o=o, k=k: e.matmul(ps[:, 0:BW], lhsT=cst[:, o:o + 128], rhs=c[:],
                                                                       start=(k == 0), stop=(k == 1)),
                         reads=["cst", kc], writes=[pk])
                cens.append((ps, pk))
            psv, pkv = PS()
            for m in range(2):
                ps, pk = cens[m]
                sq, ksq = tf.get()
                S.op("act", lambda e, ps=ps, sq=sq: e.activation(out=sq[:], in_=ps[:, 0:BW], func=AF.Square),
                     reads=[pk], writes=[ksq])
                S.op("pe", lambda e, psv=psv, sq=sq, m=m: e.matmul(psv[:, 0:BW], lhsT=cst[:, _CST["Jm"]:_CST["Jm"] + 128],
                                                                  rhs=sq[:], start=(m == 0), stop=(m == 1)),
                     reads=["cst", ksq], writes=[pkv])
            yield
            l, kl = tf.get()
            S.op("act", lambda e, psv=psv, l=l: e.activation(out=l[:], in_=psv[:, 0:BW], func=AF.Ln, bias=EPS),
                 reads=[pkv], writes=[kl])
            rs, krs = tf.get()
            S.op("act", lambda e, l=l, rs=rs: e.activation(out=rs[:], in_=l[:], func=AF.Exp, scale=-0.5),
                 reads=[kl], writes=[krs])
            sls = []
            for m in range(2):
                ps, pk = cens[m]
                xn, kxn = tf.get()
                S.op("dve", lambda e, ps=ps, rs=rs, xn=xn: e.tensor_tensor(out=xn[:], in0=ps[:, 0:BW], in1=rs[:], op=ALU.mult),
                     reads=[pk, krs], writes=[kxn])
                sl, ksl = tb.get()
                S.op("act", lambda e, xn=xn, sl=sl, m=m: e.activation(out=sl[:], in_=xn[:], func=AF.Silu,
                                                                      scale=C("ln_g", m), bias=C("ln_b", m)),
                     reads=[kxn, "cst"], writes=[ksl])
                sls.append((sl, ksl))
            for m in range(2):
                ps, pk = PS()
                for k in range(2):
                    sl, ksl = sls[k]
                    o = _CBF["pw"] + k * 256 + m * 128
                    S.op("pe", lambda e, ps=ps, sl=sl, o=o, k=k: e.matmul(ps[:, 0:BW], lhsT=cbf[:, o:o + 128], rhs=sl[:],
                                                                         start=(k == 0), stop=(k == 1)),
                         reads=["cbf", ksl], writes=[pk])
                yt, kyt = tb.get()
                S.op("act", lambda e, ps=ps, yt=yt: e.activation(out=yt[:], in_=ps[:, 0:BW], func=AF.Copy),
                     reads=[pk], writes=[kyt])
                S.op("sp", lambda e, yt=yt, m=m: e.dma_start(out=ya_out[2 + m][:, bcols(b)], in_=yt[:]), reads=[kyt],
                     writes=[("oy", 2 + m, b)], dma_sem=f"sy{kyt[1]}")
                extra_outs.append(("oy", 2 + m, b))

        for b in range(NB):
            g_rg = rg_group([0, 1, 2, 3], b, None)
            next(g_rg)
            g_cf = conformer_block(b)
            next(g_cf)
            run_gen(g_rg)
            next(g_cf)
            g_gt = gate_block(b, cg_w)
            next(g_gt)
            run_gen(g_cf)
            run_gen(g_gt)
        for t in range(4):
            w_issue()
        S.op("dve", lambda e: e.tensor_tensor(out=small[:, 52:56], in0=small[:, 40:44], in1=small[:, 0:4], op=ALU.mult),
             reads=[("rsum", t) for t in range(4)] + ["c1"], writes=["plog"])
        S.op("act", lambda e: e.activation(out=small[:, 52:56], in_=small[:, 52:56], func=AF.Exp),
             reads=["plog"], writes=["pval"])
        S.op("sp", lambda e: e.dma_start(out=out_d[:, 0:4], in_=small[:, 56:60]),
             reads=[("endst", t) for t in range(4)], writes=["o0"], dma_sem="st0")
        S.op("sp", lambda e: e.dma_start(out=out_d[:, 4:8], in_=small[:, 52:56]), reads=["pval"], writes=["o1"], dma_sem="st1")
        S.op("sp", lambda e: e.nop(), reads=["o0", "o1"] + extra_outs)
    else:
        for b in range(NB):
            run_gen(rg_group([0, 1], b, None))
            run_gen(rg_group([2, 3], b, None))
            load_h([b])

        for mt in range(KT):
            wt, wk = w_acquire()
            for b in range(NB):
                ps, pk = proj(wt, wk, w8, lambda k, b=b: (Y[:, k, bcols(b)], ("Y", k, b)), KT, b)
                S.op("dve", lambda e, ps=ps, mt=mt, b=b: e.tensor_tensor(out=h[:, mt, bcols(b)], in0=ps[:, 0:BW],
                                                                        in1=h[:, mt, bcols(b)], op=ALU.add),
                     reads=[pk, ("h", mt, b)], writes=[("h", mt, b)])
            w_issue()

        rms_to_A("g_mlp")
        for G in range(8):
            mo = (G % 2) * 4
            for j in range(4):
                wt, wk = w_acquire()
                for b in range(NB):
                    ps, pk = proj(wt, wk, w8, a_rhs(b), KT, b)
                    r, kr = tf.get()
                    S.op("act", lambda e, ps=ps, r=r: e.activation(out=r[:], in_=ps[:, 0:BW], func=AF.Relu),
                         reads=[pk], writes=[kr])
                    S.op("dve", lambda e, ps=ps, r=r, j=j, b=b, mo=mo: e.tensor_tensor(
                        out=Y[:, mo + j, bcols(b)], in0=ps[:, 0:BW], in1=r[:], op=ALU.mult),
                        reads=[pk, kr], writes=[("Y", mo + j, b)])
                w_issue()
            def down(wt, wk, q, b):
                for mm in range(2):
                    mt = 2 * q + mm
                    ps, pk = proj(wt, wk, lambda wt_, k, mm=mm: wt_[:, (k * 2 + mm) * 128:(k * 2 + mm + 1) * 128],
                                  lambda k, b=b, mo=mo: (Y[:, mo + k, bcols(b)], ("Y", mo + k, b)), 4, b)
                    S.op("dve", lambda e, ps=ps, mt=mt, b=b: e.tensor_tensor(out=h[:, mt, bcols(b)], in0=ps[:, 0:BW],
                                                                            in1=h[:, mt, bcols(b)], op=ALU.add),
                         reads=[pk, ("h", mt, b)], writes=[("h", mt, b)])

            if G < 7:
                for q in range(4):
                    wt, wk = w_acquire()
                    for b in range(NB):
                        down(wt, wk, q, b)
                    w_issue()
            else:
                wq = [w_acquire() for q in range(4)]
                for b in range(NB):
                    for q in range(4):
                        down(wq[q][0], wq[q][1], q, b)
                for q in range(4):
                    w_issue()

        finals = []
        if mode == "layer":
            out_p = out_d.rearrange("k p t -> p k t")
            for b in range(NB):
                S.op("sp", lambda e, b=b: e.dma_start(out=out_p[:, :, bcols(b)], in_=h[:, :, bcols(b)]),
                     reads=[("h", k, b) for k in range(KT)], writes=[("o", b)], dma_sem=f"st{b}")
                finals.append(("o", b))
        else:
            opool = tf
            for b in range(NB):
                cs = bcols(b)
                ps, pk = PS()
                for k in range(KT):
                    sq, sk = tb.get()
                    S.op("act", lambda e, k=k, cs=cs, sq=sq: e.activation(out=sq[:], in_=h[:, k, cs], func=AF.Square),
                         reads=[("h", k, b)], writes=[sk])
                    S.op("pe", lambda e, k=k, sq=sq, ps=ps: e.matmul(ps[:, 0:BW], lhsT=ones[:], rhs=sq[:],
                                                                     start=(k == 0), stop=(k == KT - 1)),
                         reads=["ones", sk], writes=[pk])
                l, lk = tf.get()
                S.op("act", lambda e, ps=ps, l=l: e.activation(out=l[:], in_=ps[:, 0:BW], func=AF.Ln, bias=EPS),
                     reads=[pk], writes=[lk])
                S.op("act", lambda e, ps=ps, l=l: e.activation(out=ps[:, 0:BW], in_=l[:], func=AF.Exp, scale=-0.5),
                     reads=[lk], writes=[pk])
                lo = PRE if b == 0 else 0
                for k in range(KT):
                    o, ok = opool.get()
                    S.op("dve", lambda e, k=k, cs=cs, ps=ps, o=o: e.scalar_tensor_tensor(
                        out=o[:], in0=h[:, k, cs], scalar=C("g_fin", k), in1=ps[:, 0:BW], op0=ALU.mult, op1=ALU.mult),
                        reads=[("h", k, b), pk, "cst"], writes=[ok])
                    d0 = b * BW + lo - PRE
                    S.op("sp", lambda e, k=k, o=o, lo=lo, d0=d0: e.dma_start(out=out_d[k][:, d0:d0 + BW - lo], in_=o[:, lo:BW]),
                         reads=[ok], writes=[("o", k, b)], dma_sem=f"sto{ok[1]}")
                    finals.append(("o", k, b))
        S.op("sp", lambda e: e.nop(), reads=finals)

    assert wstate["next_use"] == nblocks, (wstate, nblocks)
    with contextlib.ExitStack() as st:
        S.emit(nc, st)
    return nc


def _blk(w):
    return np.ascontiguousarray(w.reshape(8, 128, 128).transpose(1, 0, 2).reshape(128, 1024))


def _cx_blocks(w_in_l):
    return [_blk(w_in_l[:, 1280 + 128 * t:1280 + 128 * (t + 1)]) for t in range(4)]


def _pre_stream(w_in_l):
    col = lambda c0: _blk(w_in_l[:, c0:c0 + 128])
    blocks = [col(0), col(128)]
    blocks += _cx_blocks(w_in_l)
    blocks += [col(256), col(512), col(384), col(640)]
    blocks += [col(768 + 128 * t) for t in range(4)]
    return np.stack(blocks).astype(np.float32)


def _layer_stream(w_in_l, w_out_l, w_up_l, w_down_l):
    blocks = [_blk(w_out_l[:, 128 * m:128 * (m + 1)]) for m in range(8)]
    for G in range(8):
        for j in range(4):
            c0 = G * 512 + j * 128
            blocks.append(_blk(w_up_l[:, c0:c0 + 128]))
        for q in range(4):
            sub = w_down_l[G * 512:(G + 1) * 512, q * 256:(q + 1) * 256]
            blocks.append(np.ascontiguousarray(sub.reshape(4, 128, 2, 128).transpose(1, 0, 2, 3).reshape(128, 1024)))
    return np.stack(blocks).astype(np.float32)


def _pk(v, ntile):
    return np.ascontiguousarray(np.asarray(v, np.float32).reshape(ntile, 128).T)


def _consts(l, j, G, P):
    c = np.zeros((128, NCF), np.float32)

    def put(name, arr):
        arr = np.asarray(arr, np.float32).reshape(128, -1)
        c[:, _CST[name]:_CST[name] + arr.shape[1]] = arr

    put("g_mix", _pk(P["mix_norm_g"][l], 8))
    put("g_mlp", _pk(P["mlp_norm_g"][l], 8))
    put("g_fin", _pk(P["final_norm_g"], 8))
    put("pool_scale", _pk(P["pool_scale"][l], 2))
    wins = np.array([2, 4, 8, 16], np.float32)
    wpp = np.repeat(wins, 64)
    put("invw", _pk(1.0 / wpp, 2))
    put("invwm1", _pk(1.0 / wpp - 1.0, 2))
    put("himask", (np.arange(128) >= 64).astype(np.float32))
    put("dw_w", P["convb_dw_w"][l].T.reshape(2, 128, 31).transpose(1, 0, 2))
    put("dw_b", _pk(P["convb_dw_b"][l], 2))
    put("ln_g", _pk(P["convb_ln_g"][l], 2))
    put("ln_b", _pk(P["convb_ln_b"][l], 2))
    put("rgc_w", P["rg_conv_w"][l].T.reshape(4, 128, 4).transpose(1, 0, 2))
    put("rgc_b", _pk(P["rg_conv_b"][l], 4))
    put("b_a", _pk(P["rg_b_a"][l], 4))
    put("b_x", _pk(P["rg_b_x"][l], 4))
    put("lam", _pk(P["rg_lambda"][l], 4))
    sm = np.zeros((128, 32), np.float32)
    ratio = np.ones((128, 2, 32), np.float32)
    if j == 0:
        sm[:, 16:] = 1.0
        pos = np.arange(1, 17, dtype=np.float32)
        for t in range(2):
            wv = wpp[t * 128:(t + 1) * 128][:, None]
            ratio[:, t, 16:] = wv / np.minimum(pos[None, :], wv)
    put("scanmask", sm)
    put("ratio", ratio)
    pm = np.zeros((128, 8), np.float32)
    put("pm", pm)
    if G is not None:
        put("G", G)
    wg = np.zeros((2, 128, 128), np.float32)
    for g in range(4):
        t, o = divmod(g, 2)
        wg[t, o * 64:(o + 1) * 64, o * 64:(o + 1) * 64] = P["pool_w"][l][g]
    put("Wg", wg.transpose(1, 0, 2))
    cm = (np.eye(256, dtype=np.float32) - np.float32(1.0 / 256.0)).reshape(2, 128, 256)
    put("Cmat", cm.transpose(1, 0, 2))
    put("Jm", np.full((128, 128), 1.0 / 256.0, np.float32))
    return c


def _cbf(l, P):
    c = np.zeros((128, NCB), np.float32)
    c[:, 0:128] = np.eye(128, dtype=np.float32)
    c[:, 128:640] = P["convb_pw_w"][l].reshape(2, 128, 256).transpose(1, 0, 2).reshape(128, 512)
    for nm, key in (("Wa", "rg_w_a"), ("Wx", "rg_w_x")):
        bd = np.zeros((4, 128, 128), np.float32)
        for hd in range(8):
            t, o = divmod(hd, 2)
            bd[t, o * 64:(o + 1) * 64, o * 64:(o + 1) * 64] = P[key][l][hd]
        c[:, _CBF[nm]:_CBF[nm] + 512] = bd.transpose(1, 0, 2).reshape(128, 512)
    return c


_PROG = {}


def _prog(mode, nblocks):
    if (mode, nblocks) not in _PROG:
        _PROG[(mode, nblocks)] = build_program(mode, nblocks)
    return _PROG[(mode, nblocks)]


def kernel(**inputs):
    P = {k: np.asarray(v, np.float32) for k, v in inputs.items()}
    x = P["x"]
    B = x.shape[0]
    hT = []
    for r in range(NCORES):
        b, j = divmod(r, 4)
        seq = np.concatenate([P["meta_tokens"], x[b]], axis=0)
        s0 = 16 + MAIN * j
        if j == 0:
            tok = np.concatenate([np.zeros((16, D), np.float32), seq[0:16 + MAIN]], axis=0)
        else:
            tok = seq[s0 - PRE:s0 + MAIN]
        hT.append(np.ascontiguousarray(tok.T).reshape(KT, 128, NT))
    out = None
    for l in range(2):
        cbf = _cbf(l, P)
        wpre = _pre_stream(P["w_in"][l])
        ncA = _prog("pre", wpre.shape[0])
        mapsA = [{"hT": hT[r], "wstream": wpre, "cst": _consts(l, r % 4, None, P), "cbf": cbf} for r in range(NCORES)]
        resA = run_bass_kernel_spmd(ncA, mapsA, core_ids=list(range(NCORES)))
        G = np.stack([np.asarray(resA.results[r]["endp"], np.float32) for r in range(NCORES)], axis=1)
        mode = "layer" if l == 0 else "last"
        wst = _layer_stream(P["w_in"][l], P["w_out"][l], P["w_up"][l], P["w_down"][l])
        ncB = _prog(mode, wst.shape[0])
        mapsB = []
        for r in range(NCORES):
            b, j = divmod(r, 4)
            c = _consts(l, j, G.reshape(128, 64), P)
            pm = np.zeros((128, 8), np.float32)
            pm[:, 4 * b:4 * b + j] = 1.0
            c[:, _CST["pm"]:_CST["pm"] + 8] = pm
            mapsB.append({"hT": hT[r], "wstream": wst, "cst": c, "cbf": cbf,
                          "ya_in": np.asarray(resA.results[r]["ya_out"]), "arg_in": np.asarray(resA.results[r]["arg_out"]),
                          "brg_in": np.asarray(resA.results[r]["brg_out"]),
                          "gate_in": np.asarray(resA.results[r]["gate_out"])})
        resB = run_bass_kernel_spmd(ncB, mapsB, core_ids=list(range(NCORES)))
        if l == 0:
            hn = [np.asarray(resB.results[r]["hout"], np.float32) for r in range(NCORES)]
            hT = []
            for r in range(NCORES):
                b, j = divmod(r, 4)
                t = hn[r].copy()
                if j == 0:
                    t[:, :, 0:16] = 0.0
                else:
                    t[:, :, 0:PRE] = hn[r - 1][:, :, NT - PRE:NT]
                hT.append(t)
        else:
            out = np.zeros((B, 4 * MAIN, D), np.float32)
            for r in range(NCORES):
                b, j = divmod(r, 4)
                y = np.asarray(resB.results[r]["yout"], np.float32).reshape(D, MAIN)
                out[b, MAIN * j:MAIN * (j + 1), :] = y.T
    return out
```

```python
import contextlib
import numpy as np
import concourse.bass as bass
import concourse.mybir as mybir
from concourse.bass_utils import run_bass_kernel_spmd

F32 = mybir.dt.float32
BF16 = mybir.dt.bfloat16
AF = mybir.ActivationFunctionType
ALU = mybir.AluOpType

NCORES = 8
D = 1024
KT = 8
PRE = 32
MAIN = 2048
NT = PRE + MAIN
NB = 5
BW = NT // NB
PADL = 32
EPS = 1e-6
NRING = 6
GELU_K = 1.5957691216057308

_CST = {}
_off = 0
for _n, _w in [("g_mix", 8), ("g_mlp", 8), ("g_fin", 8), ("pool_scale", 2), ("invw", 2), ("invwm1", 2),
               ("himask", 1), ("dw_w", 62), ("dw_b", 2), ("ln_g", 2), ("ln_b", 2), ("rgc_w", 16),
               ("rgc_b", 4), ("b_a", 4), ("b_x", 4), ("lam", 4), ("scanmask", 32), ("ratio", 64),
               ("pm", 8), ("G", 64), ("Wg", 256), ("Cmat", 512), ("Jm", 128)]:
    _CST[_n] = _off
    _off += _w
NCF = _off
_CBF = {"ident": 0, "pw": 128, "Wa": 640, "Wx": 1152}
NCB = 1664

ENGS = ("pe", "act", "dve", "pool", "sp")


class Op:
    __slots__ = ("idx", "eng", "fn", "deps", "dma_sem", "count", "signal", "waits", "inc")

    def __init__(self, idx, eng, fn, dma_sem, inc):
        self.idx = idx
        self.eng = eng
        self.fn = fn
        self.deps = {}
        self.dma_sem = dma_sem
        self.count = None
        self.signal = False
        self.waits = []
        self.inc = inc


class Sched:
    def __init__(self):
        self.ops = []
        self.last_writer = {}
        self.readers = {}

    def op(self, eng, fn, reads=(), writes=(), dma_sem=None, inc=16):
        o = Op(len(self.ops), eng, fn, dma_sem, inc)
        for k in reads:
            w = self.last_writer.get(k)
            if w is not None:
                o.deps[w] = "raw"
            if isinstance(k, tuple) and k[0] == "ps":
                for r in self.readers.get(k, ()):
                    if self.ops[r].eng != eng and r not in o.deps:
                        o.deps[r] = "rar"
        for k in writes:
            w = self.last_writer.get(k)
            if w is not None and w not in o.deps:
                o.deps[w] = "waw"
            for r in self.readers.get(k, ()):
                if r not in o.deps:
                    o.deps[r] = "war"
        for k in reads:
            self.readers.setdefault(k, []).append(o.idx)
        for k in writes:
            self.last_writer[k] = o.idx
            self.readers[k] = []
        self.ops.append(o)
        return o

    def emit(self, nc, st):
        ops = self.ops
        for o in ops:
            for d, kind in o.deps.items():
                p = ops[d]
                if p.dma_sem is not None or o.dma_sem is not None or p.eng != o.eng:
                    need = True
                else:
                    need = (o.eng != "pe")
                if need:
                    o.waits.append(d)
                    p.signal = True
        cnt = {e: 0 for e in ENGS}
        dcnt = {}
        for o in ops:
            if o.dma_sem is not None:
                dcnt[o.dma_sem] = dcnt.get(o.dma_sem, 0) + o.inc
                o.count = dcnt[o.dma_sem]
            elif o.signal:
                cnt[o.eng] += 1
                o.count = cnt[o.eng]
        sems = {e: st.enter_context(nc.semaphore("s_" + e)) for e in ENGS}
        dsems = {k: st.enter_context(nc.semaphore("d_" + str(k))) for k in sorted(dcnt)}
        block = st.enter_context(nc.Block())
        queues = {e: [o for o in ops if o.eng == e] for e in ENGS}

        def run_queue(eng_name, engine):
            waited = {}
            for o in queues[eng_name]:
                need = {}
                for d in o.waits:
                    p = ops[d]
                    key = ("d", p.dma_sem) if p.dma_sem is not None else ("e", p.eng)
                    if p.count > need.get(key, 0):
                        need[key] = p.count
                for key, val in need.items():
                    if waited.get(key, 0) >= val:
                        continue
                    waited[key] = val
                    engine.wait_ge(dsems[key[1]] if key[0] == "d" else sems[key[1]], val)
                ins = o.fn(engine)
                if o.dma_sem is not None:
                    ins.then_inc(dsems[o.dma_sem], o.inc)
                elif o.signal:
                    ins.then_inc(sems[o.eng], 1)

        @block.tensor
        def _(e):
            run_queue("pe", e)

        @block.scalar
        def _(e):
            run_queue("act", e)

        @block.vector
        def _(e):
            run_queue("dve", e)

        @block.gpsimd
        def _(e):
            run_queue("pool", e)

        @block.sync
        def _(e):
            run_queue("sp", e)


class Pool:
    def __init__(self, nc, name, shape, dtype, n):
        self.t = [nc.alloc_sbuf_tensor(f"{name}{i}", shape, dtype) for i in range(n)]
        self.name = name
        self.i = 0

    def get(self):
        i = self.i % len(self.t)
        self.i += 1
        return self.t[i], (self.name, i)


def build_program(mode, nblocks):
    nc = bass.Bass("TRN2", target_bir_lowering=False)
    hT = nc.dram_tensor("hT", [KT, 128, NT], F32, kind="ExternalInput").ap()
    wstream = nc.dram_tensor("wstream", [nblocks, 128, 1024], F32, kind="ExternalInput").ap()
    cst_d = nc.dram_tensor("cst", [128, NCF], F32, kind="ExternalInput").ap()
    cbf_d = nc.dram_tensor("cbf", [128, NCB], F32, kind="ExternalInput").ap()
    if mode == "pre":
        out_d = nc.dram_tensor("endp", [128, 8], F32, kind="ExternalOutput").ap()
        gate_out = nc.dram_tensor("gate_out", [4, 128, NT], F32, kind="ExternalOutput").ap()
        ya_out = nc.dram_tensor("ya_out", [4, 128, NT], F32, kind="ExternalOutput").ap()
        arg_out = nc.dram_tensor("arg_out", [4, 128, NT], F32, kind="ExternalOutput").ap()
        brg_out = nc.dram_tensor("brg_out", [4, 128, NT], F32, kind="ExternalOutput").ap()
    else:
        gate_in = nc.dram_tensor("gate_in", [4, 128, NT], F32, kind="ExternalInput").ap()
        ya_in = nc.dram_tensor("ya_in", [4, 128, NT], F32, kind="ExternalInput").ap()
        arg_in = nc.dram_tensor("arg_in", [4, 128, NT], F32, kind="ExternalInput").ap()
        brg_in = nc.dram_tensor("brg_in", [4, 128, NT], F32, kind="ExternalInput").ap()
    if mode == "pre":
        pass
    elif mode == "layer":
        out_d = nc.dram_tensor("hout", [KT, 128, NT], F32, kind="ExternalOutput").ap()
    else:
        out_d = nc.dram_tensor("yout", [KT, 128, MAIN], F32, kind="ExternalOutput").ap()

    S = Sched()
    h = nc.alloc_sbuf_tensor("h", [128, KT, NT], F32)
    A = nc.alloc_sbuf_tensor("A", [128, KT, NT], BF16)
    Y = nc.alloc_sbuf_tensor("Y", [128, KT, NT], BF16) if mode != "pre" else None
    NSTG = 6 if mode == "pre" else 4
    STG = nc.alloc_sbuf_tensor("STG", [128, NSTG, PADL + NT], BF16)
    ring = [nc.alloc_sbuf_tensor(f"wr{i}", [128, 1024], BF16) for i in range(NRING)]
    cst = nc.alloc_sbuf_tensor("cst_s", [128, NCF], F32)
    cbf = nc.alloc_sbuf_tensor("cbf_s", [128, NCB], BF16)
    pmat = nc.alloc_sbuf_tensor("pmat", [128, 8, 128], BF16)
    rgd = nc.alloc_sbuf_tensor("rgd", [128, 16, 128], BF16)
    ones = nc.alloc_sbuf_tensor("ones", [128, 128], BF16)
    dres = nc.alloc_sbuf_tensor("dres", [128, 62, 128], BF16) if mode == "pre" else None
    small = nc.alloc_sbuf_tensor("small", [128, 64], F32)
    tf = Pool(nc, "tf", [128, BW], F32, 17 if mode == "pre" else 12)
    ths = Pool(nc, "ths", [128, BW], F32, 5)
    tb = Pool(nc, "tb", [128, BW], BF16, 8 if mode == "pre" else 6)
    ps_t = [nc.alloc_psum_tensor(f"ps{i}", [128, 512], F32) for i in range(8)]
    ps_i = [0]

    def run_gen(g):
        for _ in g:
            pass

    def PS():
        i = ps_i[0] % 8
        ps_i[0] += 1
        return ps_t[i], ("ps", i)

    def C(name, j=0, w=1):
        o = _CST[name] + j
        return cst[:, o:o + w]

    def bcols(b):
        return slice(b * BW, (b + 1) * BW)

    S.op("sp", lambda e: e.dma_start(out=cst[:], in_=cst_d), writes=["cst"], dma_sem="ldc")
    S.op("pool", lambda e: e.dma_start(out=cbf[:], in_=cbf_d), writes=["cbf"], dma_sem="ldb")
    hT_p = hT.rearrange("k p t -> p k t")

    def load_h(blocks):
        for b in blocks:
            S.op("sp", lambda e, b=b: e.dma_start(out=h[:, :, bcols(b)], in_=hT_p[:, :, bcols(b)]),
                 writes=[("h", k, b) for k in range(KT)], dma_sem=f"ldh{b}")

    if mode == "pre":
        load_h(range(NB))

    wstate = {"next_dma": 0, "next_use": 0}

    def w_issue():
        i = wstate["next_dma"]
        if i >= nblocks:
            return
        wstate["next_dma"] += 1
        s = i % NRING
        S.op("pool", lambda e, i=i, s=s: e.dma_start(out=ring[s][:], in_=wstream[i]),
             writes=[("w", s)], dma_sem=f"w{s}")

    for _ in range(NRING):
        w_issue()

    def w_acquire():
        i = wstate["next_use"]
        wstate["next_use"] += 1
        s = i % NRING
        return ring[s], ("w", s)

    S.op("dve", lambda e: e.memset(ones[:], 1.0 / 1024.0), writes=["ones"])
    for s4 in range(NSTG):
        S.op("dve", lambda e, s4=s4: e.memset(STG[:, s4, 0:PADL], 0.0), writes=[("stgpad", s4)])
    S.op("act", lambda e: e.activation(out=small[:, 12:16], in_=C("lam", 0, 4), func=AF.Exp, scale=-1.0),
         reads=["cst"], writes=["sm_t"])
    S.op("act", lambda e: e.activation(out=small[:, 16:20], in_=small[:, 12:16], func=AF.Ln, bias=1.0),
         reads=["sm_t"], writes=["sm_t2"])
    S.op("dve", lambda e: e.tensor_scalar(out=small[:, 0:4], in0=small[:, 16:20], scalar1=-8.0, scalar2=None,
                                          op0=ALU.mult), reads=["sm_t2"], writes=["c1"])
    S.op("dve", lambda e: e.tensor_scalar(out=small[:, 4:8], in0=small[:, 16:20], scalar1=-16.0, scalar2=None,
                                          op0=ALU.mult), reads=["sm_t2"], writes=["c2"])
    S.op("dve", lambda e: e.memset(small[:, 8:12], 0.0), writes=["carry"])
    if mode != "pre":
        for r in range(8):
            gE = C("G", r * 8, 4)
            gP = C("G", r * 8 + 4, 4)
            S.op("dve", lambda e, gP=gP: e.tensor_tensor(out=small[:, 20:24], in0=gP, in1=small[:, 8:12], op=ALU.mult),
                 reads=["cst", "carry"], writes=["cc_t"])
            S.op("dve", lambda e, gE=gE: e.tensor_tensor(out=small[:, 24:28], in0=small[:, 20:24], in1=gE, op=ALU.add),
                 reads=["cc_t", "cst"], writes=["cc_u"])
            S.op("dve", lambda e: e.tensor_tensor(out=small[:, 28:32], in0=small[:, 24:28], in1=small[:, 8:12],
                                                  op=ALU.subtract), reads=["cc_u", "carry"], writes=["cc_v"])
            S.op("dve", lambda e, r=r: e.scalar_tensor_tensor(out=small[:, 32:36], in0=small[:, 28:32],
                                                              scalar=C("pm", r), in1=small[:, 8:12],
                                                              op0=ALU.mult, op1=ALU.add),
                 reads=["cc_v", "carry", "cst"], writes=["cc_w"])
            S.op("dve", lambda e: e.tensor_copy(out=small[:, 8:12], in_=small[:, 32:36]), reads=["cc_w"], writes=["carry"])
    for t in range(4):
        for k in range(4):
            S.op("pool", lambda e, t=t, k=k: e.tensor_scalar(out=rgd[:, t * 4 + k, :], in0=cbf[:, 0:128],
                                                             scalar1=C("rgc_w", t * 4 + k), scalar2=1.0,
                                                             op0=ALU.mult, op1=ALU.mult),
                 reads=["cst", "cbf"], writes=[("rgd", t)])
    if mode == "pre":
        for j in range(62):
            S.op("pool", lambda e, j=j: e.tensor_scalar(out=dres[:, j, :], in0=cbf[:, 0:128], scalar1=C("dw_w", j), scalar2=1.0,
                                                        op0=ALU.mult, op1=ALU.mult),
                 reads=["cst", "cbf"], writes=["dres"])
        for t in range(2):
            wg = cst[:, _CST["Wg"] + t * 128:_CST["Wg"] + (t + 1) * 128]
            S.op("dve", lambda e, t=t, wg=wg: e.tensor_copy(out=pmat[:, t, :], in_=wg), reads=["cst"], writes=[("pm_", t)])
            S.op("dve", lambda e, t=t, wg=wg: e.tensor_scalar(out=pmat[:, 2 + t, :], in0=wg, scalar1=C("invwm1", t),
                                                              scalar2=None, op0=ALU.mult), reads=["cst"], writes=[("pm_", t)])
            S.op("dve", lambda e, t=t, wg=wg: e.tensor_scalar(out=pmat[:, 4 + t, :], in0=wg, scalar1=C("invw", t),
                                                              scalar2=None, op0=ALU.mult), reads=["cst"], writes=[("pm_", t)])
            S.op("dve", lambda e, t=t, wg=wg: e.tensor_scalar(out=pmat[:, 6 + t, :], in0=wg, scalar1=C("invw", t),
                                                              scalar2=C("himask"), op0=ALU.mult, op1=ALU.mult),
                 reads=["cst"], writes=[("pm_", t)])

    def rms_to_A(gname):
        for b in range(NB):
            cs = bcols(b)
            ps, pk = PS()
            for k in range(KT):
                sq, sk = tb.get()
                S.op("act", lambda e, k=k, cs=cs, sq=sq: e.activation(out=sq[:], in_=h[:, k, cs], func=AF.Square),
                     reads=[("h", k, b)], writes=[sk])
                S.op("pe", lambda e, k=k, sq=sq, ps=ps: e.matmul(ps[:, 0:BW], lhsT=ones[:], rhs=sq[:],
                                                                 start=(k == 0), stop=(k == KT - 1)),
                     reads=["ones", sk], writes=[pk])
            l, lk = tf.get()
            S.op("act", lambda e, ps=ps, l=l: e.activation(out=l[:], in_=ps[:, 0:BW], func=AF.Ln, bias=EPS),
                 reads=[pk], writes=[lk])
            S.op("act", lambda e, ps=ps, l=l: e.activation(out=ps[:, 0:BW], in_=l[:], func=AF.Exp, scale=-0.5),
                 reads=[lk], writes=[pk])
            for k in range(KT):
                S.op("dve", lambda e, k=k, cs=cs, ps=ps: e.scalar_tensor_tensor(
                    out=A[:, k, cs], in0=h[:, k, cs], scalar=C(gname, k), in1=ps[:, 0:BW],
                    op0=ALU.mult, op1=ALU.mult),
                    reads=[("h", k, b), pk, "cst"], writes=[("A", k, b)])

    def proj(wt, wk, wsel, rhs_fn, nk, b):
        ps, pk = PS()
        for k in range(nk):
            rap, rkey = rhs_fn(k)
            S.op("pe", lambda e, k=k, rap=rap, ps=ps: e.matmul(ps[:, 0:BW], lhsT=wsel(wt, k), rhs=rap,
                                                               start=(k == 0), stop=(k == nk - 1)),
                 reads=[wk, rkey], writes=[pk])
        return ps, pk

    def w8(wt, k):
        return wt[:, k * 128:(k + 1) * 128]

    def a_rhs(b):
        return lambda k: (A[:, k, bcols(b)], ("A", k, b))

    def stg_cols(slot, b, shift):
        c0 = PADL + b * BW - shift
        return STG[:, slot, c0:c0 + BW]

    def stg_keys(slot, b):
        ks = [("stg", slot, b)]
        ks.append(("stg", slot, b - 1) if b > 0 else ("stgpad", slot))
        return ks

    def inproj_to_stg(slot):
        wt, wk = w_acquire()
        for b in range(NB):
            ps, pk = proj(wt, wk, w8, a_rhs(b), KT, b)
            S.op("act", lambda e, ps=ps, b=b: e.activation(out=STG[:, slot, PADL + b * BW:PADL + (b + 1) * BW],
                                                           in_=ps[:, 0:BW], func=AF.Copy),
                 reads=[pk], writes=[("stg", slot, b)])
        w_issue()

    extra_outs = []
    if mode == "pre":
        rms_to_A("g_mix")
    else:
        pass

    if mode == "pre":
        inproj_to_stg(0)
        inproj_to_stg(1)
        def pool_tile(t):
            wlo, whi = (2, 4) if t == 0 else (8, 16)
            for b in range(NB):
                ps, pk = PS()
                for k in range(whi):
                    mat = pmat[:, 2 + t, :] if k == 0 else (pmat[:, 4 + t, :] if k < wlo else pmat[:, 6 + t, :])
                    S.op("pe", lambda e, k=k, mat=mat, ps=ps, b=b: e.matmul(ps[:, 0:BW], lhsT=mat, rhs=stg_cols(t, b, k),
                                                                           start=(k == 0), stop=(k == whi - 1)),
                         reads=[("pm_", t)] + stg_keys(t, b), writes=[pk])
                yt, kyt = tf.get()
                S.op("act", lambda e, ps=ps, yt=yt: e.activation(out=yt[:], in_=ps[:, 0:BW],
                                                                 func=AF.Identity, scale=C("pool_scale", t)),
                     reads=[pk, "cst"], writes=[kyt])
                if b == 0:
                    y0 = (yt, kyt)
                else:
                    S.op("sp", lambda e, yt=yt, b=b: e.dma_start(out=ya_out[t][:, bcols(b)], in_=yt[:]), reads=[kyt],
                         writes=[("oy", t, b)], dma_sem=f"so{kyt[1]}")
                    extra_outs.append(("oy", t, b))
            psS, pkS = PS()
            for k in range(whi):
                mat = pmat[:, 4 + t, :] if k < wlo else pmat[:, 6 + t, :]
                S.op("pe", lambda e, k=k, mat=mat, psS=psS: e.matmul(psS[:, 0:PRE], lhsT=mat,
                                                                    rhs=STG[:, t, PADL - k:PADL - k + PRE],
                                                                    start=(k == 0), stop=(k == whi - 1)),
                     reads=[("pm_", t)] + stg_keys(t, 0), writes=[pkS])
            psX, pkX = PS()
            S.op("pe", lambda e, psX=psX: e.matmul(psX[:, 0:PRE], lhsT=pmat[:, t, :], rhs=STG[:, t, PADL:PADL + PRE],
                                                   start=True, stop=True),
                 reads=[("pm_", t)] + stg_keys(t, 0), writes=[pkX])
            t1, k1 = tf.get()
            S.op("dve", lambda e, psS=psS, t1=t1: e.tensor_tensor(out=t1[:, 0:PRE], in0=psS[:, 0:PRE],
                                                                  in1=C("ratio", t * 32, 32), op=ALU.mult),
                 reads=[pkS, "cst"], writes=[k1])
            t2, k2 = tf.get()
            S.op("dve", lambda e, psX=psX, t1=t1, t2=t2: e.tensor_tensor(out=t2[:, 0:PRE], in0=t1[:, 0:PRE],
                                                                         in1=psX[:, 0:PRE], op=ALU.subtract),
                 reads=[pkX, k1], writes=[k2])
            yt, kyt = y0
            S.op("act", lambda e, t2=t2, yt=yt: e.activation(out=yt[:, 0:PRE], in_=t2[:, 0:PRE], func=AF.Identity,
                                                             scale=C("pool_scale", t)),
                 reads=[k2, "cst"], writes=[kyt])
            S.op("sp", lambda e, yt=yt: e.dma_start(out=ya_out[t][:, bcols(0)], in_=yt[:]), reads=[kyt],
                 writes=[("oy", t, 0)], dma_sem=f"so{kyt[1]}")
            extra_outs.append(("oy", t, 0))

        pool_tile(0)
        pool_tile(1)

    if mode == "pre":
        S.op("dve", lambda e: e.memset(small[:, 40:44], 0.0), writes=[("rsum", t) for t in range(4)])
    rg_slots = [2, 3, 2, 3] if mode != "pre" else [0, 1, 2, 3]
    if mode == "pre":
        for t in range(4):
            inproj_to_stg(rg_slots[t])
    else:
        pass

    rg_prev = {}

    def rg_group(tiles, b, cgws):
        U = {t: {} for t in tiles}
        if mode != "pre":
            for t in tiles:
                a, ka = tf.get()
                bb, kb = tf.get()
                S.op("sp", lambda e, a=a, t=t: e.dma_start(out=a[:], in_=arg_in[t][:, bcols(b)]), writes=[ka],
                     dma_sem=f"so{ka[1]}")
                S.op("sp", lambda e, bb=bb, t=t: e.dma_start(out=bb[:], in_=brg_in[t][:, bcols(b)]), writes=[kb],
                     dma_sem=f"so{kb[1]}")
                U[t].update(a=a, ka=ka, xc=bb, kxc=kb)
        else:
            for t in tiles:
                slot = rg_slots[t]
                ps_c, pk_c = PS()
                for k in range(4):
                    S.op("pe", lambda e, k=k, ps_c=ps_c, t=t, slot=slot: e.matmul(
                        ps_c[:, 0:BW], lhsT=rgd[:, t * 4 + k, :], rhs=stg_cols(slot, b, 3 - k),
                        start=(k == 0), stop=(k == 3)),
                        reads=[("rgd", t)] + stg_keys(slot, b), writes=[pk_c])
                U[t].update(ps_c=ps_c, pk_c=pk_c)
            yield
            for t in tiles:
                u = U[t]
                xc, kxc = tf.get()
                xcb, kxcb = tb.get()
                S.op("dve", lambda e, ps_c=u["ps_c"], xc=xc, t=t: e.tensor_scalar(out=xc[:], in0=ps_c[:, 0:BW], scalar1=C("rgc_b", t),
                                                                                 scalar2=None, op0=ALU.add),
                     reads=[u["pk_c"], "cst"], writes=[kxc])
                S.op("dve", lambda e, xc=xc, xcb=xcb: e.tensor_copy(out=xcb[:], in_=xc[:]), reads=[kxc], writes=[kxcb])
                u.update(xc=xc, kxc=kxc, xcb=xcb, kxcb=kxcb)
            for t in tiles:
                u = U[t]
                ps_a, pk_a = PS()
                S.op("pe", lambda e, ps_a=ps_a, xcb=u["xcb"], t=t: e.matmul(
                    ps_a[:, 0:BW], lhsT=cbf[:, _CBF["Wa"] + t * 128:_CBF["Wa"] + (t + 1) * 128], rhs=xcb[:], start=True, stop=True),
                    reads=["cbf", u["kxcb"]], writes=[pk_a])
                ps_x, pk_x = PS()
                S.op("pe", lambda e, ps_x=ps_x, xcb=u["xcb"], t=t: e.matmul(
                    ps_x[:, 0:BW], lhsT=cbf[:, _CBF["Wx"] + t * 128:_CBF["Wx"] + (t + 1) * 128], rhs=xcb[:], start=True, stop=True),
                    reads=["cbf", u["kxcb"]], writes=[pk_x])
                u.update(ps_a=ps_a, pk_a=pk_a, ps_x=ps_x, pk_x=pk_x)
            for t in tiles:
                u = U[t]
                r, kr = tf.get()
                if mode == "pre":
                    lo = PRE if b == 0 else 0
                    if b == 0:
                        S.op("act", lambda e, ps_a=u["ps_a"], r=r, t=t: e.activation(out=r[:, 0:PRE], in_=ps_a[:, 0:PRE],
                                                                                    func=AF.Sigmoid, bias=C("b_a", t)),
                             reads=[u["pk_a"], "cst"], writes=[kr])
                    S.op("act", lambda e, ps_a=u["ps_a"], r=r, lo=lo, t=t: e.activation(
                        out=r[:, lo:BW], in_=ps_a[:, lo:BW], func=AF.Sigmoid, bias=C("b_a", t),
                        accum_out=small[:, 44 + t:45 + t]),
                        reads=[u["pk_a"], "cst"], writes=[kr, ("racc", t)])
                    S.op("dve", lambda e, t=t: e.tensor_tensor(out=small[:, 40 + t:41 + t], in0=small[:, 40 + t:41 + t],
                                                               in1=small[:, 44 + t:45 + t], op=ALU.add),
                         reads=[("racc", t), ("rsum", t)], writes=[("rsum", t)])
                else:
                    S.op("act", lambda e, ps_a=u["ps_a"], r=r, t=t: e.activation(out=r[:], in_=ps_a[:, 0:BW], func=AF.Sigmoid,
                                                                                bias=C("b_a", t)),
                         reads=[u["pk_a"], "cst"], writes=[kr])
                u.update(r=r, kr=kr)
            for t in tiles:
                u = U[t]
                a, ka = tf.get()
                S.op("act", lambda e, r=u["r"], a=a, t=t: e.activation(out=a[:], in_=r[:], func=AF.Exp, scale=small[:, t:t + 1]),
                     reads=[u["kr"], "c1"], writes=[ka])
                u.update(a=a, ka=ka)
            for t in tiles:
                u = U[t]
                S.op("dve", lambda e, r=u["r"], a=u["a"]: e.tensor_tensor(out=r[:], in0=a[:], in1=a[:], op=ALU.mult),
                     reads=[u["ka"], u["kr"]], writes=[u["kr"]])
            for t in tiles:
                u = U[t]
                S.op("act", lambda e, ps_a=u["ps_a"], ps_x=u["ps_x"], t=t: e.activation(
                    out=ps_a[:, 0:BW], in_=ps_x[:, 0:BW], func=AF.Sigmoid, bias=C("b_x", t)),
                    reads=[u["pk_x"], "cst"], writes=[u["pk_a"]])
            for t in tiles:
                u = U[t]
                S.op("act", lambda e, r=u["r"], ps_x=u["ps_x"]: e.activation(out=ps_x[:, 0:BW], in_=r[:], func=AF.Sqrt,
                                                                             scale=-1.0, bias=1.0),
                     reads=[u["kr"]], writes=[u["pk_x"]])
            for t in tiles:
                u = U[t]
                S.op("dve", lambda e, ps_a=u["ps_a"], xc=u["xc"]: e.tensor_tensor(out=xc[:], in0=ps_a[:, 0:BW], in1=xc[:], op=ALU.mult),
                     reads=[u["pk_a"], u["kxc"]], writes=[u["kxc"]])
            for t in tiles:
                u = U[t]
                S.op("dve", lambda e, ps_x=u["ps_x"], xc=u["xc"]: e.tensor_tensor(out=xc[:], in0=ps_x[:, 0:BW], in1=xc[:], op=ALU.mult),
                     reads=[u["pk_x"], u["kxc"]], writes=[u["kxc"]])
        if mode == "pre":
            if b == 0:
                for t in tiles:
                    u = U[t]
                    S.op("dve", lambda e, bb=u["xc"]: e.tensor_tensor(out=bb[:, 0:PRE], in0=bb[:, 0:PRE], in1=C("scanmask", 0, 32),
                                                                      op=ALU.mult), reads=[u["kxc"], "cst"], writes=[u["kxc"]])
            for t in tiles:
                u = U[t]
                S.op("sp", lambda e, a=u["a"], t=t: e.dma_start(out=arg_out[t][:, bcols(b)], in_=a[:]), reads=[u["ka"]],
                     writes=[("oar", t, b)], dma_sem=f"so{u['ka'][1]}")
                S.op("sp", lambda e, bb=u["xc"], t=t: e.dma_start(out=brg_out[t][:, bcols(b)], in_=bb[:]), reads=[u["kxc"]],
                     writes=[("obr", t, b)], dma_sem=f"so{u['kxc'][1]}")
                extra_outs.append(("oar", t, b))
                extra_outs.append(("obr", t, b))
        if b == 0:
            if mode != "pre":
                for t in tiles:
                    u = U[t]
                    S.op("dve", lambda e, bb=u["xc"], a=u["a"], t=t: e.scalar_tensor_tensor(
                        out=bb[:, PRE:PRE + 1], in0=a[:, PRE:PRE + 1], scalar=small[:, 8 + t:9 + t],
                        in1=bb[:, PRE:PRE + 1], op0=ALU.mult, op1=ALU.add),
                        reads=[u["kxc"], u["ka"], "carry"], writes=[u["kxc"]])
        for t in tiles:
            u = U[t]
            hs, khs = ths.get()
            if t not in rg_prev:
                S.op("dve", lambda e, hs=hs, a=u["a"], bb=u["xc"]: e.tensor_tensor_scan(
                    out=hs[:], data0=a[:], data1=bb[:], initial=0.0, op0=ALU.mult, op1=ALU.add),
                    reads=[u["ka"], u["kxc"]], writes=[khs])
            else:
                ph, pkh = rg_prev[t]
                S.op("dve", lambda e, hs=hs, a=u["a"], bb=u["xc"], ph=ph: e.tensor_tensor_scan(
                    out=hs[:], data0=a[:], data1=bb[:], initial=ph[:, BW - 1:BW], op0=ALU.mult, op1=ALU.add),
                    reads=[u["ka"], u["kxc"], pkh], writes=[khs])
            rg_prev[t] = (hs, khs)
            u.update(hs=hs, khs=khs)
        if mode == "pre":
            if b == NB - 1:
                for t in tiles:
                    S.op("dve", lambda e, hs=U[t]["hs"], t=t: e.tensor_copy(out=small[:, 56 + t:57 + t], in_=hs[:, BW - 1:BW]),
                         reads=[U[t]["khs"]], writes=[("endst", t)])
            return
        for t in tiles:
            u = U[t]
            g, kg = tf.get()
            S.op("sp", lambda e, g=g, t=t: e.dma_start(out=g[:], in_=gate_in[t][:, bcols(b)]), writes=[kg], dma_sem=f"so{kg[1]}")
            S.op("dve", lambda e, g=g, hs=u["hs"], t=t: e.tensor_tensor(out=Y[:, 4 + t, bcols(b)], in0=g[:], in1=hs[:], op=ALU.mult),
                 reads=[kg, u["khs"]], writes=[("Y", 4 + t, b)])

    def gate_block(b, cgws):
        U = {t: {} for t in range(4)}
        for t in range(4):
            wt, wk = cgws[t]
            ps_g, pk_g = proj(wt, wk, w8, a_rhs(b), KT, b)
            U[t].update(ps_g=ps_g, pk_g=pk_g)
        yield
        for t in range(4):
            u = U[t]
            x2, kx2 = tf.get()
            S.op("act", lambda e, ps_g=u["ps_g"], x2=x2: e.activation(out=x2[:], in_=ps_g[:, 0:BW], func=AF.Square),
                 reads=[u["pk_g"]], writes=[kx2])
            u.update(x2=x2, kx2=kx2)
        for t in range(4):
            u = U[t]
            S.op("dve", lambda e, x2=u["x2"]: e.tensor_scalar(out=x2[:], in0=x2[:], scalar1=0.044715 * GELU_K, scalar2=GELU_K,
                                                              op0=ALU.mult, op1=ALU.add), reads=[u["kx2"]], writes=[u["kx2"]])
        for t in range(4):
            u = U[t]
            S.op("dve", lambda e, ps_g=u["ps_g"], x2=u["x2"]: e.tensor_tensor(out=x2[:], in0=ps_g[:, 0:BW], in1=x2[:], op=ALU.mult),
                 reads=[u["pk_g"], u["kx2"]], writes=[u["kx2"]])
        for t in range(4):
            u = U[t]
            S.op("act", lambda e, q=u["x2"]: e.activation(out=q[:], in_=q[:], func=AF.Sigmoid),
                 reads=[u["kx2"]], writes=[u["kx2"]])
        for t in range(4):
            u = U[t]
            S.op("dve", lambda e, ps_g=u["ps_g"], q=u["x2"]: e.tensor_tensor(out=q[:], in0=ps_g[:, 0:BW], in1=q[:], op=ALU.mult),
                 reads=[u["pk_g"], u["kx2"]], writes=[u["kx2"]])
            S.op("sp", lambda e, q=u["x2"], t=t: e.dma_start(out=gate_out[t][:, bcols(b)], in_=q[:]), reads=[u["kx2"]],
                 writes=[("og", t, b)], dma_sem=f"so{u['kx2'][1]}")
            extra_outs.append(("og", t, b))

    if mode == "pre":
        for m in range(2):
            wv, kv = w_acquire()
            wg_, kg_ = w_acquire()
            for b in range(NB):
                psv, pkv = proj(wv, kv, w8, a_rhs(b), KT, b)
                psg, pkg = proj(wg_, kg_, w8, a_rhs(b), KT, b)
                sg, ksg = tf.get()
                S.op("act", lambda e, psg=psg, sg=sg: e.activation(out=sg[:], in_=psg[:, 0:BW], func=AF.Sigmoid),
                     reads=[pkg], writes=[ksg])
                S.op("dve", lambda e, psv=psv, sg=sg, b=b, m=m: e.tensor_tensor(
                    out=STG[:, 4 + m, PADL + b * BW:PADL + (b + 1) * BW], in0=psv[:, 0:BW], in1=sg[:], op=ALU.mult),
                    reads=[pkv, ksg], writes=[("stg", 4 + m, b)])
            w_issue()
            w_issue()
        cg_w = [w_acquire() for t in range(4)]

        dg_i = [0]

        def conformer_block(b):
            cen_in = []
            for m in range(2):
                ps, pk = PS()
                for k in range(31):
                    S.op("pe", lambda e, m=m, k=k, ps=ps: e.matmul(ps[:, 0:BW], lhsT=dres[:, m * 31 + k, :],
                                                                  rhs=stg_cols(4 + m, b, 30 - k),
                                                                  start=(k == 0), stop=(k == 30)),
                         reads=["dres"] + stg_keys(4 + m, b), writes=[pk])
                c, kc = tf.get()
                S.op("act", lambda e, ps=ps, c=c, m=m: e.activation(out=c[:], in_=ps[:, 0:BW], func=AF.Identity,
                                                                    bias=C("dw_b", m)),
                     reads=[pk, "cst"], writes=[kc])
                cen_in.append((c, kc))
            yield
            cens = []
            for m in range(2):
                ps, pk = PS()
                for k in range(2):
                    c, kc = cen_in[k]
                    o = _CST["Cmat"] + k * 256 + m * 128
                    S.op("pe", lambda e, ps=ps, c=c, o=o, k=k: e.matmul(ps[:, 0:BW], lhsT=cst[:, o:o + 128], rhs=c[:],
                                                                       start=(k == 0), stop=(k == 1)),
                         reads=["cst", kc], writes=[pk])
                cens.append((ps, pk))
            psv, pkv = PS()
            for m in range(2):
                ps, pk = cens[m]
                sq, ksq = tf.get()
                S.op("act", lambda e, ps=ps, sq=sq: e.activation(out=sq[:], in_=ps[:, 0:BW], func=AF.Square),
                     reads=[pk], writes=[ksq])
                S.op("pe", lambda e, psv=psv, sq=sq, m=m: e.matmul(psv[:, 0:BW], lhsT=cst[:, _CST["Jm"]:_CST["Jm"] + 128],
                                                                  rhs=sq[:], start=(m == 0), stop=(m == 1)),
                     reads=["cst", ksq], writes=[pkv])
            yield
            l, kl = tf.get()
            S.op("act", lambda e, psv=psv, l=l: e.activation(out=l[:], in_=psv[:, 0:BW], func=AF.Ln, bias=EPS),
                 reads=[pkv], writes=[kl])
            rs, krs = tf.get()
            S.op("act", lambda e, l=l, rs=rs: e.activation(out=rs[:], in_=l[:], func=AF.Exp, scale=-0.5),
                 reads=[kl], writes=[krs])
            sls = []
            for m in range(2):
                ps, pk = cens[m]
                xn, kxn = tf.get()
                S.op("dve", lambda e, ps=ps, rs=rs, xn=xn: e.tensor_tensor(out=xn[:], in0=ps[:, 0:BW], in1=rs[:], op=ALU.mult),
                     reads=[pk, krs], writes=[kxn])
                sl, ksl = tb.get()
                S.op("act", lambda e, xn=xn, sl=sl, m=m: e.activation(out=sl[:], in_=xn[:], func=AF.Silu,
                                                                      scale=C("ln_g", m), bias=C("ln_b", m)),
                     reads=[kxn, "cst"], writes=[ksl])
                sls.append((sl, ksl))
            for m in range(2):
                ps, pk = PS()
                for k in range(2):
                    sl, ksl = sls[k]
                    o = _CBF["pw"] + k * 256 + m * 128
                    S.op("pe", lambda e, ps=ps, sl=sl, o=o, k=k: e.matmul(ps[:, 0:BW], lhsT=cbf[:, o:o + 128], rhs=sl[:],
                                                                         start=(k == 0), stop=(k == 1)),
                         reads=["cbf", ksl], writes=[pk])
                yt, kyt = tf.get()
                S.op("act", lambda e, ps=ps, yt=yt: e.activation(out=yt[:], in_=ps[:, 0:BW], func=AF.Copy),
                     reads=[pk], writes=[kyt])
                S.op("sp", lambda e, yt=yt, m=m: e.dma_start(out=ya_out[2 + m][:, bcols(b)], in_=yt[:]), reads=[kyt],
                     writes=[("oy", 2 + m, b)], dma_sem=f"so{kyt[1]}")
                extra_outs.append(("oy", 2 + m, b))

        for b in range(NB):
            g_rg = rg_group([0, 1, 2, 3], b, None)
            next(g_rg)
            g_cf = conformer_block(b)
            next(g_cf)
            run_gen(g_rg)
            next(g_cf)
            g_gt = gate_block(b, cg_w)
            next(g_gt)
            run_gen(g_cf)
            run_gen(g_gt)
        for t in range(4):
            w_issue()
        S.op("dve", lambda e: e.tensor_tensor(out=small[:, 52:56], in0=small[:, 40:44], in1=small[:, 0:4], op=ALU.mult),
             reads=[("rsum", t) for t in range(4)] + ["c1"], writes=["plog"])
        S.op("act", lambda e: e.activation(out=small[:, 52:56], in_=small[:, 52:56], func=AF.Exp),
             reads=["plog"], writes=["pval"])
        S.op("sp", lambda e: e.dma_start(out=out_d[:, 0:4], in_=small[:, 56:60]),
             reads=[("endst", t) for t in range(4)], writes=["o0"], dma_sem="st0")
        S.op("sp", lambda e: e.dma_start(out=out_d[:, 4:8], in_=small[:, 52:56]), reads=["pval"], writes=["o1"], dma_sem="st1")
        S.op("sp", lambda e: e.nop(), reads=["o0", "o1"] + extra_outs)
    else:
        for b in range(NB):
            for t in range(4):
                yt, kyt = tf.get()
                S.op("sp", lambda e, yt=yt, t=t, b=b: e.dma_start(out=yt[:], in_=ya_in[t][:, bcols(b)]), writes=[kyt],
                     dma_sem=f"so{kyt[1]}")
                S.op("act", lambda e, yt=yt, t=t, b=b: e.activation(out=Y[:, t, bcols(b)], in_=yt[:], func=AF.Copy),
                     reads=[kyt], writes=[("Y", t, b)])
            run_gen(rg_group([0, 1], b, None))
            run_gen(rg_group([2, 3], b, None))
            load_h([b])

        for mt in range(KT):
            wt, wk = w_acquire()
            for b in range(NB):
                ps, pk = proj(wt, wk, w8, lambda k, b=b: (Y[:, k, bcols(b)], ("Y", k, b)), KT, b)
                S.op("dve", lambda e, ps=ps, mt=mt, b=b: e.tensor_tensor(out=h[:, mt, bcols(b)], in0=ps[:, 0:BW],
                                                                        in1=h[:, mt, bcols(b)], op=ALU.add),
                     reads=[pk, ("h", mt, b)], writes=[("h", mt, b)])
            w_issue()

        rms_to_A("g_mlp")
        for G in range(8):
            mo = (G % 2) * 4
            for j in range(4):
                wt, wk = w_acquire()
                for b in range(NB):
                    ps, pk = proj(wt, wk, w8, a_rhs(b), KT, b)
                    r, kr = tf.get()
                    S.op("act", lambda e, ps=ps, r=r: e.activation(out=r[:], in_=ps[:, 0:BW], func=AF.Relu),
                         reads=[pk], writes=[kr])
                    S.op("dve", lambda e, ps=ps, r=r, j=j, b=b, mo=mo: e.tensor_tensor(
                        out=Y[:, mo + j, bcols(b)], in0=ps[:, 0:BW], in1=r[:], op=ALU.mult),
                        reads=[pk, kr], writes=[("Y", mo + j, b)])
                w_issue()
            def down(wt, wk, q, b):
                for mm in range(2):
                    mt = 2 * q + mm
                    ps, pk = proj(wt, wk, lambda wt_, k, mm=mm: wt_[:, (k * 2 + mm) * 128:(k * 2 + mm + 1) * 128],
                                  lambda k, b=b, mo=mo: (Y[:, mo + k, bcols(b)], ("Y", mo + k, b)), 4, b)
                    S.op("dve", lambda e, ps=ps, mt=mt, b=b: e.tensor_tensor(out=h[:, mt, bcols(b)], in0=ps[:, 0:BW],
                                                                            in1=h[:, mt, bcols(b)], op=ALU.add),
                         reads=[pk, ("h", mt, b)], writes=[("h", mt, b)])

            if G < 7:
                for q in range(4):
                    wt, wk = w_acquire()
                    for b in range(NB):
                        down(wt, wk, q, b)
                    w_issue()
            else:
                wq = [w_acquire() for q in range(4)]
                for b in range(NB):
                    for q in range(4):
                        down(wq[q][0], wq[q][1], q, b)
                for q in range(4):
                    w_issue()

        finals = []
        if mode == "layer":
            out_p = out_d.rearrange("k p t -> p k t")
            for b in range(NB):
                S.op("sp", lambda e, b=b: e.dma_start(out=out_p[:, :, bcols(b)], in_=h[:, :, bcols(b)]),
                     reads=[("h", k, b) for k in range(KT)], writes=[("o", b)], dma_sem=f"st{b}")
                finals.append(("o", b))
        else:
            opool = tf
            for b in range(NB):
                cs = bcols(b)
                ps, pk = PS()
                for k in range(KT):
                    sq, sk = tb.get()
                    S.op("act", lambda e, k=k, cs=cs, sq=sq: e.activation(out=sq[:], in_=h[:, k, cs], func=AF.Square),
                         reads=[("h", k, b)], writes=[sk])
                    S.op("pe", lambda e, k=k, sq=sq, ps=ps: e.matmul(ps[:, 0:BW], lhsT=ones[:], rhs=sq[:],
                                                                     start=(k == 0), stop=(k == KT - 1)),
                         reads=["ones", sk], writes=[pk])
                l, lk = tf.get()
                S.op("act", lambda e, ps=ps, l=l: e.activation(out=l[:], in_=ps[:, 0:BW], func=AF.Ln, bias=EPS),
                     reads=[pk], writes=[lk])
                S.op("act", lambda e, ps=ps, l=l: e.activation(out=ps[:, 0:BW], in_=l[:], func=AF.Exp, scale=-0.5),
                     reads=[lk], writes=[pk])
                lo = PRE if b == 0 else 0
                for k in range(KT):
                    o, ok = opool.get()
                    S.op("dve", lambda e, k=k, cs=cs, ps=ps, o=o: e.scalar_tensor_tensor(
                        out=o[:], in0=h[:, k, cs], scalar=C("g_fin", k), in1=ps[:, 0:BW], op0=ALU.mult, op1=ALU.mult),
                        reads=[("h", k, b), pk, "cst"], writes=[ok])
                    d0 = b * BW + lo - PRE
                    S.op("sp", lambda e, k=k, o=o, lo=lo, d0=d0: e.dma_start(out=out_d[k][:, d0:d0 + BW - lo], in_=o[:, lo:BW]),
                         reads=[ok], writes=[("o", k, b)], dma_sem=f"sto{ok[1]}")
                    finals.append(("o", k, b))
        S.op("sp", lambda e: e.nop(), reads=finals)

    assert wstate["next_use"] == nblocks, (wstate, nblocks)
    with contextlib.ExitStack() as st:
        S.emit(nc, st)
    return nc


def _blk(w):
    return np.ascontiguousarray(w.reshape(8, 128, 128).transpose(1, 0, 2).reshape(128, 1024))


def _cx_blocks(w_in_l):
    return [_blk(w_in_l[:, 1280 + 128 * t:1280 + 128 * (t + 1)]) for t in range(4)]


def _pre_stream(w_in_l):
    col = lambda c0: _blk(w_in_l[:, c0:c0 + 128])
    blocks = [col(0), col(128)]
    blocks += _cx_blocks(w_in_l)
    blocks += [col(256), col(512), col(384), col(640)]
    blocks += [col(768 + 128 * t) for t in range(4)]
    return np.stack(blocks).astype(np.float32)


def _layer_stream(w_in_l, w_out_l, w_up_l, w_down_l):
    blocks = [_blk(w_out_l[:, 128 * m:128 * (m + 1)]) for m in range(8)]
    for G in range(8):
        for j in range(4):
            c0 = G * 512 + j * 128
            blocks.append(_blk(w_up_l[:, c0:c0 + 128]))
        for q in range(4):
            sub = w_down_l[G * 512:(G + 1) * 512, q * 256:(q + 1) * 256]
            blocks.append(np.ascontiguousarray(sub.reshape(4, 128, 2, 128).transpose(1, 0, 2, 3).reshape(128, 1024)))
    return np.stack(blocks).astype(np.float32)


def _pk(v, ntile):
    return np.ascontiguousarray(np.asarray(v, np.float32).reshape(ntile, 128).T)


def _consts(l, j, G, P):
    c = np.zeros((128, NCF), np.float32)

    def put(name, arr):
        arr = np.asarray(arr, np.float32).reshape(128, -1)
        c[:, _CST[name]:_CST[name] + arr.shape[1]] = arr

    put("g_mix", _pk(P["mix_norm_g"][l], 8))
    put("g_mlp", _pk(P["mlp_norm_g"][l], 8))
    put("g_fin", _pk(P["final_norm_g"], 8))
    put("pool_scale", _pk(P["pool_scale"][l], 2))
    wins = np.array([2, 4, 8, 16], np.float32)
    wpp = np.repeat(wins, 64)
    put("invw", _pk(1.0 / wpp, 2))
    put("invwm1", _pk(1.0 / wpp - 1.0, 2))
    put("himask", (np.arange(128) >= 64).astype(np.float32))
    put("dw_w", P["convb_dw_w"][l].T.reshape(2, 128, 31).transpose(1, 0, 2))
    put("dw_b", _pk(P["convb_dw_b"][l], 2))
    put("ln_g", _pk(P["convb_ln_g"][l], 2))
    put("ln_b", _pk(P["convb_ln_b"][l], 2))
    put("rgc_w", P["rg_conv_w"][l].T.reshape(4, 128, 4).transpose(1, 0, 2))
    put("rgc_b", _pk(P["rg_conv_b"][l], 4))
    put("b_a", _pk(P["rg_b_a"][l], 4))
    put("b_x", _pk(P["rg_b_x"][l], 4))
    put("lam", _pk(P["rg_lambda"][l], 4))
    sm = np.zeros((128, 32), np.float32)
    ratio = np.ones((128, 2, 32), np.float32)
    if j == 0:
        sm[:, 16:] = 1.0
        pos = np.arange(1, 17, dtype=np.float32)
        for t in range(2):
            wv = wpp[t * 128:(t + 1) * 128][:, None]
            ratio[:, t, 16:] = wv / np.minimum(pos[None, :], wv)
    put("scanmask", sm)
    put("ratio", ratio)
    pm = np.zeros((128, 8), np.float32)
    put("pm", pm)
    if G is not None:
        put("G", G)
    wg = np.zeros((2, 128, 128), np.float32)
    for g in range(4):
        t, o = divmod(g, 2)
        wg[t, o * 64:(o + 1) * 64, o * 64:(o + 1) * 64] = P["pool_w"][l][g]
    put("Wg", wg.transpose(1, 0, 2))
    cm = (np.eye(256, dtype=np.float32) - np.float32(1.0 / 256.0)).reshape(2, 128, 256)
    put("Cmat", cm.transpose(1, 0, 2))
    put("Jm", np.full((128, 128), 1.0 / 256.0, np.float32))
    return c


def _cbf(l, P):
    c = np.zeros((128, NCB), np.float32)
    c[:, 0:128] = np.eye(128, dtype=np.float32)
    c[:, 128:640] = P["convb_pw_w"][l].reshape(2, 128, 256).transpose(1, 0, 2).reshape(128, 512)
    for nm, key in (("Wa", "rg_w_a"), ("Wx", "rg_w_x")):
        bd = np.zeros((4, 128, 128), np.float32)
        for hd in range(8):
            t, o = divmod(hd, 2)
            bd[t, o * 64:(o + 1) * 64, o * 64:(o + 1) * 64] = P[key][l][hd]
        c[:, _CBF[nm]:_CBF[nm] + 512] = bd.transpose(1, 0, 2).reshape(128, 512)
    return c


_PROG = {}


def _prog(mode, nblocks):
    if (mode, nblocks) not in _PROG:
        _PROG[(mode, nblocks)] = build_program(mode, nblocks)
    return _PROG[(mode, nblocks)]


def kernel(**inputs):
    P = {k: np.asarray(v, np.float32) for k, v in inputs.items()}
    x = P["x"]
    B = x.shape[0]
    hT = []
    for r in range(NCORES):
        b, j = divmod(r, 4)
        seq = np.concatenate([P["meta_tokens"], x[b]], axis=0)
        s0 = 16 + MAIN * j
        if j == 0:
            tok = np.concatenate([np.zeros((16, D), np.float32), seq[0:16 + MAIN]], axis=0)
        else:
            tok = seq[s0 - PRE:s0 + MAIN]
        hT.append(np.ascontiguousarray(tok.T).reshape(KT, 128, NT))
    out = None
    for l in range(2):
        cbf = _cbf(l, P)
        wpre = _pre_stream(P["w_in"][l])
        ncA = _prog("pre", wpre.shape[0])
        mapsA = [{"hT": hT[r], "wstream": wpre, "cst": _consts(l, r % 4, None, P), "cbf": cbf} for r in range(NCORES)]
        resA = run_bass_kernel_spmd(ncA, mapsA, core_ids=list(range(NCORES)))
        G = np.stack([np.asarray(resA.results[r]["endp"], np.float32) for r in range(NCORES)], axis=1)
        mode = "layer" if l == 0 else "last"
        wst = _layer_stream(P["w_in"][l], P["w_out"][l], P["w_up"][l], P["w_down"][l])
        ncB = _prog(mode, wst.shape[0])
        mapsB = []
        for r in range(NCORES):
            b, j = divmod(r, 4)
            c = _consts(l, j, G.reshape(128, 64), P)
            pm = np.zeros((128, 8), np.float32)
            pm[:, 4 * b:4 * b + j] = 1.0
            c[:, _CST["pm"]:_CST["pm"] + 8] = pm
            mapsB.append({"hT": hT[r], "wstream": wst, "cst": c, "cbf": cbf,
                          "ya_in": np.asarray(resA.results[r]["ya_out"]), "arg_in": np.asarray(resA.results[r]["arg_out"]),
                          "brg_in": np.asarray(resA.results[r]["brg_out"]),
                          "gate_in": np.asarray(resA.results[r]["gate_out"])})
        resB = run_bass_kernel_spmd(ncB, mapsB, core_ids=list(range(NCORES)))
        if l == 0:
            hn = [np.asarray(resB.results[r]["hout"], np.float32) for r in range(NCORES)]
            hT = []
            for r in range(NCORES):
                b, j = divmod(r, 4)
                t = hn[r].copy()
                if j == 0:
                    t[:, :, 0:16] = 0.0
                else:
                    t[:, :, 0:PRE] = hn[r - 1][:, :, NT - PRE:NT]
                hT.append(t)
        else:
            out = np.zeros((B, 4 * MAIN, D), np.float32)
            for r in range(NCORES):
                b, j = divmod(r, 4)
                y = np.asarray(resB.results[r]["yout"], np.float32).reshape(D, MAIN)
                out[b, MAIN * j:MAIN * (j + 1), :] = y.T
    return out
```
